# Optimizing a Trainium2 kernel written in Bass

```python
import jax, jax.numpy as jnp
from jax import lax
import numpy as np

D_MODEL = 1024
BATCH = 16
SEQ = 2048
DEPTH = 1

CHUNK = 64
Q_BLOCK = 128
N_HEADS = 8
D_LATENT = D_MODEL // 8
V_HEAD_DIM = 64
ATTN_WIDTH = N_HEADS * V_HEAD_DIM
IDX_HEADS = 8
IDX_DIM = 64
TOPK_MAX = 256
POOL_WINDOWS = (2, 4, 8, 16)
N_POOL_GROUPS = 4
POOL_WIDTH = D_MODEL // 2
POOL_GROUP = POOL_WIDTH // N_POOL_GROUPS
ROPE_THETA = 500000.0
ROPE_FRACTION = 4
D_FF = 11 * D_MODEL // 4
CONV_WIDTH = 3
LN_EPS = 1e-5
ALPHA = (2.0 * DEPTH) ** 0.25
BETA = (8.0 * DEPTH) ** -0.25

PROJ_SIZES = (N_HEADS * D_LATENT,
              D_LATENT,
              IDX_HEADS * IDX_DIM,
              IDX_HEADS,
              IDX_DIM,
              POOL_WIDTH,
              2 * D_MODEL)
PROJ_WIDTH = sum(PROJ_SIZES)
PROJ_SPLITS = tuple(int(v) for v in np.cumsum(PROJ_SIZES)[:-1])

kernel_name = "hybrid_sparse_pool_convffn_deepnorm"


def layer_norm(x, g, b):
    xf = x.astype(jnp.float32)
    mu = jnp.mean(xf, axis=-1, keepdims=True)
    var = jnp.mean(jnp.square(xf - mu), axis=-1, keepdims=True)
    return ((xf - mu) * lax.rsqrt(var + LN_EPS)).astype(x.dtype) * g + b


def rope_cos_sin(positions, dim):
    half = dim // ROPE_FRACTION // 2
    inv_freq = ROPE_THETA ** (-jnp.arange(half, dtype=jnp.float32) / half)
    ang = positions.astype(jnp.float32)[..., None] * inv_freq
    return jnp.cos(ang), jnp.sin(ang)


def apply_partial_rope(x, cos, sin):
    half = cos.shape[-1]
    r = 2 * half
    if x.ndim == 4:
        cos, sin = cos[:, :, None, :], sin[:, :, None, :]
    cos, sin = cos.astype(x.dtype), sin.astype(x.dtype)
    x1, x2 = x[..., :half], x[..., half:r]
    return jnp.concatenate([x1 * cos - x2 * sin, x2 * cos + x1 * sin, x[..., r:]], axis=-1)


def sparse_indexed_attention(q, k_lat, v_lat, q_idx, w_idx, k_idx):
    B, S, H, DL = q.shape
    topk = min(TOPK_MAX, S // 4)
    n_blocks = S // Q_BLOCK
    key_chunk = jnp.arange(S) // CHUNK
    attn_scale = DL ** -0.5
    idx_scale = IDX_DIM ** -0.5
    head_w_scale = IDX_HEADS ** -0.5
    gather = jax.vmap(lambda table, ids: table[ids])

    def one_block(blk):
        start = blk * Q_BLOCK
        qb = lax.dynamic_slice_in_dim(q, start, Q_BLOCK, axis=1)
        qib = lax.dynamic_slice_in_dim(q_idx, start, Q_BLOCK, axis=1)
        wib = lax.dynamic_slice_in_dim(w_idx, start, Q_BLOCK, axis=1)
        q_chunk = (start + jnp.arange(Q_BLOCK)) // CHUNK
        admissible = key_chunk[None, :] <= q_chunk[:, None]
        rel = jax.nn.relu(jnp.einsum('bqhd,bsd->bqhs', qib, k_idx).astype(jnp.float32) * idx_scale)
        idx_score = jnp.einsum('bqh,bqhs->bqs', wib.astype(jnp.float32) * head_w_scale, rel)
        idx_score = jnp.where(admissible[None], idx_score, -jnp.inf)
        _, sel = lax.top_k(idx_score, topk)
        valid = (sel // CHUNK) <= q_chunk[None, :, None]
        k_sel = gather(k_lat, sel)
        v_sel = gather(v_lat, sel)
        s = jnp.einsum('bqhd,bqkd->bqhk', qb, k_sel).astype(jnp.float32) * attn_scale
        s = jnp.where(valid[:, :, None, :], s, -jnp.inf)
        p = jax.nn.softmax(s, axis=-1).astype(v_sel.dtype)
        return jnp.einsum('bqhk,bqkd->bqhd', p, v_sel)

    out = lax.map(one_block, jnp.arange(n_blocks))
    return jnp.transpose(out, (1, 0, 2, 3, 4)).reshape(B, S, H, DL)


def multiscale_pool(u, w_pool, pool_scale):
    B, S, _ = u.shape
    cs = jnp.cumsum(u.astype(jnp.float32), axis=1)
    t = jnp.arange(S)
    means = []
    for g, w in enumerate(POOL_WINDOWS):
        csg = cs[..., g * POOL_GROUP:(g + 1) * POOL_GROUP]
        lagged = jnp.pad(csg, ((0, 0), (w, 0), (0, 0)))[:, :S]
        count = jnp.minimum(t + 1, w).astype(jnp.float32)[None, :, None]
        means.append((csg - lagged) / count)
    pooled = jnp.concatenate(means, axis=-1).astype(u.dtype) - u
    grouped = pooled.reshape(B, S, N_POOL_GROUPS, POOL_GROUP)
    mixed = jnp.einsum('bsgc,gcd->bsgd', grouped, w_pool).reshape(B, S, POOL_WIDTH)
    return mixed * pool_scale


def conv_ffn(x, conv_w, conv_b, w_up, w_down):
    h = x @ w_up
    h = lax.conv_general_dilated(h, conv_w[:, None, :], window_strides=(1,),
                                 padding=[(CONV_WIDTH - 1, 0)],
                                 dimension_numbers=('NWC', 'WIO', 'NWC'),
                                 feature_group_count=2 * D_FF) + conv_b
    gate, val = jnp.split(h, 2, axis=-1)
    return (jax.nn.silu(gate) * val) @ w_down


def setup_inputs(seed: int = 0) -> dict:
    key = jax.random.key(seed)
    ks = jax.random.split(key, 18)

    def nrm(k, shape, scale):
        return jax.random.normal(k, shape, jnp.float32) * scale

    x = nrm(ks[0], (BATCH, SEQ, D_MODEL), 1.0)
    offsets = jax.random.randint(ks[1], (BATCH, 1), 0, 4096, dtype=jnp.int32)
    positions = (offsets + jnp.arange(SEQ, dtype=jnp.int32)[None, :]).astype(jnp.int32)
    return {
        "x": x,
        "positions": positions,
        "w_in": nrm(ks[2], (DEPTH, D_MODEL, PROJ_WIDTH), D_MODEL ** -0.5),
        "w_uv": nrm(ks[3], (DEPTH, N_HEADS, D_LATENT, V_HEAD_DIM), D_LATENT ** -0.5),
        "w_attn_branch": nrm(ks[4], (DEPTH, ATTN_WIDTH, D_MODEL), ATTN_WIDTH ** -0.5),
        "w_pool": nrm(ks[5], (DEPTH, N_POOL_GROUPS, POOL_GROUP, POOL_GROUP), POOL_GROUP ** -0.5),
        "pool_scale": 1.0 + nrm(ks[6], (DEPTH, POOL_WIDTH), 0.1),
        "w_pool_branch": nrm(ks[7], (DEPTH, POOL_WIDTH, D_MODEL), POOL_WIDTH ** -0.5),
        "w_out": nrm(ks[8], (DEPTH, D_MODEL, D_MODEL), BETA * D_MODEL ** -0.5),
        "ln1_g": 1.0 + nrm(ks[9], (DEPTH, D_MODEL), 0.02),
        "ln1_b": nrm(ks[10], (DEPTH, D_MODEL), 0.02),
        "conv_w": nrm(ks[11], (DEPTH, CONV_WIDTH, 2 * D_FF), CONV_WIDTH ** -0.5),
        "conv_b": nrm(ks[12], (DEPTH, 2 * D_FF), 0.01),
        "w_ffn_up": nrm(ks[13], (DEPTH, D_MODEL, 2 * D_FF), D_MODEL ** -0.5),
        "w_ffn_down": nrm(ks[14], (DEPTH, D_FF, D_MODEL), BETA * D_FF ** -0.5),
        "ln2_g": 1.0 + nrm(ks[15], (DEPTH, D_MODEL), 0.02),
        "ln2_b": nrm(ks[16], (DEPTH, D_MODEL), 0.02),
    }


def reference(x, positions, w_in, w_uv, w_attn_branch, w_pool, pool_scale, w_pool_branch,
              w_out, ln1_g, ln1_b, conv_w, conv_b, w_ffn_up, w_ffn_down, ln2_g, ln2_b):
    B, S, _ = x.shape
    cos_q, sin_q = rope_cos_sin(positions, D_LATENT)
    cos_i, sin_i = rope_cos_sin(positions, IDX_DIM)
    for l in range(DEPTH):
        proj = x @ w_in[l]
        q, c_kv, q_idx, w_idx, k_idx, pool_in, gate_logits = jnp.split(proj, PROJ_SPLITS, axis=-1)
        q = apply_partial_rope(q.reshape(B, S, N_HEADS, D_LATENT), cos_q, sin_q)
        k_lat = apply_partial_rope(c_kv, cos_q, sin_q)
        q_idx = apply_partial_rope(q_idx.reshape(B, S, IDX_HEADS, IDX_DIM), cos_i, sin_i)
        k_idx = apply_partial_rope(k_idx, cos_i, sin_i)
        attn = sparse_indexed_attention(q, k_lat, c_kv, q_idx, w_idx, k_idx)
        attn = jnp.einsum('bshd,hde->bshe', attn, w_uv[l]).reshape(B, S, ATTN_WIDTH)
        branch_a = attn @ w_attn_branch[l]
        branch_b = multiscale_pool(pool_in, w_pool[l], pool_scale[l]) @ w_pool_branch[l]
        gate_a, gate_b = jnp.split(jax.nn.sigmoid(gate_logits), 2, axis=-1)
        mixed = (gate_a * branch_a + gate_b * branch_b) @ w_out[l]
        x = layer_norm(ALPHA * x + mixed, ln1_g[l], ln1_b[l])
        ffn = conv_ffn(x, conv_w[l], conv_b[l], w_ffn_up[l], w_ffn_down[l])
        x = layer_norm(ALPHA * x + ffn, ln2_g[l], ln2_b[l])
    return x
```

```python
import bisect
from contextlib import ExitStack

import numpy as np
import ml_dtypes

import concourse.bass as bass
import concourse.mybir as mybir
from concourse.bass_utils import run_bass_kernel_spmd

F32 = mybir.dt.float32
BF16 = mybir.dt.bfloat16
I32 = mybir.dt.int32
AF = mybir.ActivationFunctionType
ALU = mybir.AluOpType

PE, ACT, DVE, POOL, SP = "pe", "act", "dve", "pool", "sp"
ENGINES = (PE, ACT, DVE, POOL, SP)
SELF_SYNC = {PE: False, ACT: True, DVE: True, POOL: True, SP: False}

D = 1024
S = 2048
NSEQ = 2
NB = 256
NBLK = S // NB
H = 8
DFF = 2816
NCH = DFF // 128
ALPHA = 2.0 ** 0.25
LN_EPS = 1e-5
TOPK = 256
THETA = 500000.0
TILE = 2048
NTILES = 9 + 16 + 22 + 11
TOT = NTILES * TILE
RING = 8
NBIS = 18
BSTEP0 = 4.0
FSKEW = 2
TWO_PI = 2.0 * np.pi
C1 = 6.28125
C2 = float(TWO_PI - 6.28125)
PI_SAFE = 3.1415925


class Buf:
    __slots__ = ("name", "w", "r", "excl")

    def __init__(self, name, excl=False):
        self.name = name
        self.w = None
        self.r = []
        self.excl = excl


class Chan:
    ALL = []

    def __init__(self, sched, name):
        Chan.ALL.append(self)
        self.key = "dma:" + name
        self.sem = sched.new_sem(name)
        self.cum = 0
        sched.sems[self.key] = self.sem


class Sched:
    def __init__(self, nc):
        self.nc = nc
        self.ops = {e: [] for e in ENGINES}
        self.cnt = {e: 0 for e in ENGINES}
        self.clock = {e: {} for e in ENGINES}
        self.sems = {}
        self.phase = "pro"
        self.labels = {e: [] for e in ENGINES}
        self._semctx = []
        for e in ENGINES:
            self.sems[e] = self.new_sem("eng_" + e)

    def new_sem(self, name):
        ctx = self.nc.semaphore(name)
        s = ctx.__enter__()
        self._semctx.append(ctx)
        return s

    def close(self):
        for c in reversed(self._semctx):
            c.__exit__(None, None, None)

    def _need(self, eng, tok, waits):
        if tok is None:
            return
        key, val, snap = tok
        if key == eng and not SELF_SYNC[eng]:
            return
        clk = self.clock[eng]
        if clk.get(key, 0) >= val:
            return
        waits.append((key, val))
        clk[key] = val
        for k, v in snap.items():
            if clk.get(k, 0) < v:
                clk[k] = v

    def op(self, eng, fn, reads=(), writes=(), chan=None):
        if any(b.excl for b in reads):
            writes = list(writes) + [b for b in reads if b.excl]
            reads = [b for b in reads if not b.excl]
        waits = []
        for b in reads:
            self._need(eng, b.w, waits)
        for b in writes:
            self._need(eng, b.w, waits)
            for t in b.r:
                self._need(eng, t, waits)
        wd = {}
        for k, v in waits:
            if wd.get(k, 0) < v:
                wd[k] = v
        if chan is None:
            self.cnt[eng] += 1
            val = self.cnt[eng]
            key = eng
            inc = 1
        else:
            chan.cum += 16
            val = chan.cum
            key = chan.key
            inc = 16
        tok = (key, val, dict(self.clock[eng]))
        self.ops[eng].append((list(wd.items()), fn, key, inc))
        if chan is None:
            self.labels[eng].append(self.phase)
        for b in reads:
            b.r.append(tok)
        for b in writes:
            b.w = tok
            b.r = []
        return tok

    def wait_tokens(self, eng, toks):
        waits = []
        for t in toks:
            self._need(eng, t, waits)
        wd = {}
        for k, v in waits:
            if wd.get(k, 0) < v:
                wd[k] = v
        if wd:
            self.ops[eng].append((list(wd.items()), None, None, 0))

    def emit(self):
        nc = self.nc
        sems = self.sems
        ops = self.ops
        sig = {e: set() for e in ENGINES}
        for e in ENGINES:
            for waits, fn, key, inc in ops[e]:
                for k, v in waits:
                    if k in sig:
                        sig[k].add(v)
        sigl = {e: sorted(sig[e]) for e in ENGINES}

        def rank(k, v):
            if k in sigl:
                return bisect.bisect_right(sigl[k], v)
            return v

        def replay(e, ename, lst):
            idx = 0
            for waits, fn, key, inc in lst:
                for k, v in waits:
                    e.wait_ge(sems[k], rank(k, v))
                if fn is not None:
                    ins = fn(e)
                    if key == ename:
                        idx += 1
                        if idx in sig[ename]:
                            ins.then_inc(sems[key], 1)
                    else:
                        ins.then_inc(sems[key], inc)

        with nc.Block() as block:
            @block.tensor
            def _(e):
                replay(e, PE, ops[PE])

            @block.scalar
            def _(e):
                replay(e, ACT, ops[ACT])

            @block.vector
            def _(e):
                replay(e, DVE, ops[DVE])

            @block.gpsimd
            def _(e):
                replay(e, POOL, ops[POOL])

            @block.sync
            def _(e):
                replay(e, SP, ops[SP])


def _ktile(m):
    n = m.shape[1]
    return np.ascontiguousarray(m.reshape(8, 128, n).transpose(1, 0, 2)).reshape(128, 8 * n)


def build_wall(w_in, w_ab, w_pb, w_out, w_up, w_down):
    tiles = []
    for i in range(4):
        tiles.append(_ktile(w_in[:, 256 * i:256 * i + 256]))
    kidx = w_in[:, 1672:1736]
    tiles.append(_ktile(np.concatenate([w_in[:, 1024:1152], kidx, kidx], axis=1)))
    for i in range(2):
        tiles.append(_ktile(w_in[:, 1152 + 256 * i:1152 + 256 * i + 256]))
    for i in range(2):
        tiles.append(_ktile(w_in[:, 1736 + 256 * i:1736 + 256 * i + 256]))
    for fc in range(8):
        tiles.append(_ktile(np.concatenate([w_in[:, 2248 + 128 * fc:2248 + 128 * fc + 128],
                                            w_in[:, 3272 + 128 * fc:3272 + 128 * fc + 128]], axis=1)))
        a = w_ab[:, 128 * fc:128 * fc + 128].reshape(4, 128, 128).transpose(1, 0, 2).reshape(128, 512)
        b = w_pb[:, 128 * fc:128 * fc + 128].reshape(4, 128, 128).transpose(1, 0, 2).reshape(128, 512)
        o = w_out[128 * fc:128 * fc + 128, :]
        tiles.append(np.concatenate([a, b, o], axis=1))
    for c in range(NCH):
        tiles.append(_ktile(np.concatenate([w_up[:, 128 * c:128 * c + 128],
                                            w_up[:, DFF + 128 * c:DFF + 128 * c + 128]], axis=1)))
        if c % 2 == 0:
            tiles.append(np.concatenate([w_down[128 * c:128 * c + 128, :],
                                         w_down[128 * (c + 1):128 * (c + 1) + 128, :]], axis=1))
    wall = np.concatenate(tiles, axis=1)
    assert wall.shape == (128, TOT), wall.shape
    return np.ascontiguousarray(wall, dtype=np.float32)


def build_consts():
    cf = np.zeros((128, 128 + 2 + 64 + 256 + 64), np.float32)
    cf[:, :128] = np.eye(128, dtype=np.float32)
    fq = (np.float32(THETA) ** (-np.arange(16, dtype=np.float32) / np.float32(16))).astype(np.float32)
    fi = (np.float32(THETA) ** (-np.arange(8, dtype=np.float32) / np.float32(8))).astype(np.float32)
    cf[0:16, 128] = -fq
    cf[16:32, 128] = fq
    for base in (0, 64):
        cf[base:base + 8, 129] = -fi
        cf[base + 8:base + 16, 129] = fi
    for g, w in enumerate((2, 4, 8, 16)):
        t = np.arange(16)
        cf[:, 130 + 16 * g:130 + 16 * g + 16] = (1.0 / np.minimum(t + 1, w)).astype(np.float32)[None, :]
    pm = np.zeros((128, 256), np.float32)
    for m in range(16):
        pm[m + 16, m] = 1.0
        pm[m, m + 16] = 1.0
    for base in (0, 64):
        for m in range(8):
            pm[base + m + 8, 128 + base + m] = 1.0
            pm[base + m, 128 + base + m + 8] = 1.0
    cf[:, 194:450] = pm
    for g, w in enumerate((2, 4, 8, 16)):
        t = np.arange(16)
        cf[:, 450 + 16 * g:450 + 16 * g + 16] = (w / np.minimum(t + 1, w)).astype(np.float32)[None, :]
    return cf, pm.astype(ml_dtypes.bfloat16)


class _Stop(Exception):
    pass


def build_program(dbg=False, cut=99, nblocks=NSEQ * NBLK, order=None):
    nc = bass.Bass("TRN2", target_bir_lowering=False)
    x_d = nc.dram_tensor("x", [NSEQ, S, D], F32, kind="ExternalInput").ap()
    pos_d = nc.dram_tensor("pos", [NSEQ, S], I32, kind="ExternalInput").ap()
    wall_d = nc.dram_tensor("wall", [128, TOT], F32, kind="ExternalInput").ap()
    wsb_d = nc.dram_tensor("wsb", [128, 1088], F32, kind="ExternalInput").ap()
    wsf_d = nc.dram_tensor("wsf", [128, 180], F32, kind="ExternalInput").ap()
    cf_d = nc.dram_tensor("cf", [128, 514], F32, kind="ExternalInput").ap()
    lnp_d = nc.dram_tensor("lnp", [4, D], F32, kind="ExternalInput").ap()
    out_d = nc.dram_tensor("out", [NSEQ, S, D], F32, kind="ExternalOutput").ap()
    wbf_d = nc.dram_tensor("wbf", [128, TOT], BF16).ap()
    dbg_d = None
    if dbg:
        dbg_d = nc.dram_tensor("dbg", [8, 128, 2048], F32, kind="ExternalOutput").ap()

    Chan.ALL = []
    S_ = Sched(nc)
    es = ExitStack()

    def sb(name, shape, dt):
        return nc.alloc_sbuf_tensor(name, shape, dt).ap()

    def ps(name):
        return nc.alloc_psum_tensor(name, [128, 512], F32).ap()

    ident = sb("ident", [128, 514], F32)
    pmb = sb("pmb", [128, 256], BF16)
    wsb = sb("wsb_s", [128, 1088], BF16)
    wsf = sb("wsf_s", [128, 180], F32)
    lnp = sb("lnp_s", [128, 4, D], F32)
    ring = [sb("ring%d" % i, [128, TILE], BF16) for i in range(RING)]
    kT = sb("kT", [128, S], BF16)
    kiT = sb("kiT", [128, S], BF16)
    vaug = sb("vaug", [128, 16, H * 65], BF16)
    xt = [sb("xt%d" % i, [128, D], F32) for i in range(4)]
    xT = sb("xT", [128, 8, NB], BF16)
    qT = sb("qT", [128, 8, NB], BF16)
    qiT = sb("qiT", [128, 4, NB], BF16)
    ckvp = sb("ckvp", [128, NB], BF16)
    prebf = [sb("prebf%d" % i, [128, NB], BF16) for i in range(2)]
    widx = sb("widx", [128, 2, 8], F32)
    cosq = sb("cosq", [128, NB], F32)
    sinq = sb("sinq", [128, NB], F32)
    cosi = sb("cosi", [128, NB], F32)
    sini = sb("sini", [128, NB], F32)
    rt1 = [sb("rt1_%d" % i, [128, NB], F32) for i in range(2)]
    rt2 = [sb("rt2_%d" % i, [128, NB], F32) for i in range(2)]
    ubuf = [sb("ubuf%d" % g, [128, 16 + NB], F32) for g in range(4)]
    pa = sb("pa", [128, 16 + NB], F32)
    pb_ = sb("pb", [128, 16 + NB], F32)
    pooled = [sb("pooled%d" % i, [128, NB], BF16) for i in range(4)]
    pmix = sb("pmix", [128, 4, NB], BF16)
    Ibuf = [sb("Ibuf%d" % i, [128, S], F32) for i in range(2)]
    dgw = sb("dgw", [128, 2, H, 128], BF16)
    maskb2 = [sb("maskb%d" % i, [128, S], BF16) for i in range(2)]
    maskT = sb("maskT", [128, 16, 128], BF16)
    Ebuf2 = [sb("Ebuf%d" % i, [128, 16, 256], BF16) for i in range(2)]
    idb = sb("idb", [128, 128], BF16)
    junk = Ebuf2[1].rearrange("p k n -> p (k n)")
    Rb = [Ebuf2[0].rearrange("p k n -> p (k n)")[:, 512 * i:512 * (i + 1)] for i in range(8)]
    _scr = Ibuf[1]
    tr = [_scr[:, NB * i:NB * (i + 1)] for i in range(4)]
    posf = _scr[:, NB * 4:NB * 5]
    tki = _scr[:, NB * 5:NB * 6].bitcast(I32)
    posi = _scr[:, NB * 6:NB * 7].bitcast(I32)
    sarg = [Ibuf[0][:, NB * i:NB * (i + 1)] for i in range(4)]
    bis = sb("bis", [128, 16], F32)
    bisc = sb("bisc", [128, 16], F32)
    rden = sb("rden", [128, 8], F32)
    attok = sb("attok", [128, 512], BF16)
    attnT = sb("attnT", [128, 4, NB], BF16)
    sgA = [sb("sgA%d" % i, [128, NB], F32) for i in range(2)]
    sgB = [sb("sgB%d" % i, [128, NB], F32) for i in range(2)]
    g1 = [sb("g1_%d" % i, [128, NB], F32) for i in range(2)]
    g2 = [sb("g2_%d" % i, [128, NB], F32) for i in range(2)]
    Gfc = [sb("Gfc%d" % i, [128, NB], BF16) for i in range(2)]
    pre = [sb("pre%d" % i, [128, D], F32) for i in range(2)]
    x1 = pre
    x1T = sb("x1T", [128, 8, NB], BF16)
    lnst = sb("lnst", [128, 12], F32)
    lnmv = sb("lnmv", [128, 4], F32)
    hraw = [sb("hraw%d" % i, [128, 2, 2 + NB], F32) for i in range(2)]
    halo = sb("halo", [128, NCH, 2, 2], F32)
    aconv = [sb("aconv%d" % i, [128, 2, NB], F32) for i in range(2)]
    actc = [sb("actc%d" % i, [128, NB], BF16) for i in range(4)]

    wbank = [ps("wbank%d" % i) for i in range(4)]
    abank = [ps("abank%d" % i) for i in range(4)]

    B = {}

    def nb(name):
        b = Buf(name)
        B[name] = b
        return b

    b_const = nb("const")
    b_ring = [nb("ring%d" % i) for i in range(RING)]
    c_ring = [Chan(S_, "ring%d" % i) for i in range(RING)]
    b_wbf = [nb("wbf%d" % i) for i in range(NTILES)]
    c_cast = [Chan(S_, "cast%d" % i) for i in range(4)]
    c_const = Chan(S_, "const")
    b_kT = [nb("kT%d" % i) for i in range(16)]
    b_kiT = [nb("kiT%d" % i) for i in range(16)]
    b_v = [nb("v%d" % i) for i in range(16)]
    b_xt = [nb("xt%d" % i) for i in range(4)]
    c_xt = [Chan(S_, "xt%d" % i) for i in range(4)]
    b_pos = B["Ebuf0"] if False else nb("posi_unused")
    c_pos = Chan(S_, "pos")
    b_ot = [nb("ot%d" % i) for i in range(2)]
    c_ot = [Chan(S_, "ot%d" % i) for i in range(2)]
    b_wb = [nb("wbank%d" % i) for i in range(4)]
    b_ab = [nb("abank%d" % i) for i in range(4)]
    for b in b_wb + b_ab:
        b.excl = True
    names = ["xT", "qT", "qiT", "ckvp", "widx", "cosq", "sinq", "cosi", "sini",
             "pa", "pb", "pooled0", "pooled1", "pooled2", "pooled3", "pmix", "maskT", "idb", "rden", "attok", "attnT",
             "x1T", "lnst", "lnmv", "halo"]
    for n in names:
        nb(n)
    for i in range(4):
        nb("ubuf%d" % i)
    for i in range(2):
        for n in ["rt1_", "rt2_", "Ibuf", "sgA", "sgB", "g1_", "g2_", "Gfc", "pre", "hraw",
                  "aconv", "Ebuf", "prebf", "maskb", "bis"]:
            nb("%s%d" % (n, i))

    for n in ["tr0", "tr1", "tr2", "tr3", "posf", "tki"]:
        B[n] = B["Ibuf1"]
    B["sarg"] = B["Ibuf0"]
    B["x1_0"] = B["pre0"]
    B["x1_1"] = B["pre1"]
    for i in range(4):
        nb("actc%d" % i)
    for i in range(8):
        nb("Rb%d" % i)
    nb("dgw0")
    nb("dgw1")
    ist = {"ri": 0, "fresh": True}
    for i in range(2):
        nb("ac0_%d" % i)
        nb("ac1_%d" % i)
    wb_rr = [0]

    ALLB = (0, 1)
    allbank = wbank + abank
    b_all = b_wb + b_ab

    wb_ctr = {}

    def next_wb(extra=(), only=None):
        pool = tuple(only) if only is not None else tuple([0, 1, 2, 3] + [4 + a for a in extra])
        wb_ctr[pool] = wb_ctr.get(pool, 0) + 1
        i = pool[wb_ctr[pool] % len(pool)]
        return allbank[i], b_all[i]

    def cdma(eng, out, in_, chan=c_const):
        S_.op(eng, lambda e: e.dma_start(out=out, in_=in_), writes=[b_const], chan=chan)

    cdma(SP, ident, cf_d)
    cdma(SP, wsf, wsf_d)
    for i in range(4):
        cdma(SP, lnp[:, i, :], lnp_d[i, :].partition_broadcast(128))
    c_const2 = Chan(S_, "const2")
    b_const2 = nb("const2")
    S_.op(POOL, lambda e: e.dma_start(out=wsb, in_=wsb_d), writes=[b_const2], chan=c_const2)
    b_const.w = (c_const.key, c_const.cum, {})
    b_pm = nb("pm")
    nb("biszero")
    nb("junk0")
    nb("junk1")
    S_.op(DVE, lambda e: e.memset(bis, 0.0), writes=[B["biszero"], B["bis0"], B["bis1"]])
    for gi_ in range(16):
        S_.op(DVE, (lambda gi_: lambda e: e.memset(bisc[:, gi_:gi_ + 1], float(128 * (gi_ + 1) - 511)))(gi_),
              writes=[B["biszero"]])
    S_.op(DVE, lambda e: e.tensor_copy(out=idb, in_=ident[:, 0:128]), reads=[b_const], writes=[B["idb"]])
    S_.op(DVE, lambda e: e.tensor_copy(out=pmb, in_=ident[:, 194:450]), reads=[b_const], writes=[b_pm])
    for n in range(NTILES):
        ch = c_cast[min(3, n // 15)]
        S_.op(POOL, (lambda n: lambda e: e.dma_start(
            out=wbf_d[:, n * TILE:(n + 1) * TILE].rearrange("p (a n) -> p a n", a=2),
            in_=wall_d[:, n * TILE:(n + 1) * TILE].rearrange("p (a n) -> p a n", a=2)))(n),
              writes=[b_wbf[n]], chan=ch)
    for n in range(NTILES):
        ch = c_cast[min(3, n // 15)]
        b_wbf[n].w = (ch.key, ch.cum, {})
    S_.op(POOL, lambda e: e.memset(pa, 0.0), writes=[B["pa"]])
    S_.op(POOL, lambda e: e.memset(pb_, 0.0), writes=[B["pb"]])

    seq_tiles = order if order is not None else []
    total_uses = len(seq_tiles)
    ws = {"issued": 0, "next": 0, "rec": []}

    def w_issue_upto(k):
        while ws["issued"] < min(k, total_uses):
            u = ws["issued"]
            n = seq_tiles[u]
            slot = u % RING
            S_.op(SP, (lambda n, slot: lambda e: e.dma_start(out=ring[slot],
                                                               in_=wbf_d[:, n * TILE:(n + 1) * TILE]))(n, slot),
                  reads=[b_wbf[n]], writes=[b_ring[slot]], chan=c_ring[slot])
            ws["issued"] += 1

    def wget(tile):
        if order is None:
            ws["rec"].append(tile)
            return ring[0], b_ring[0]
        u = ws["next"]
        assert seq_tiles[u] == tile, (u, tile, seq_tiles[u])
        w_issue_upto(u + RING - 2)
        ws["next"] += 1
        slot = u % RING
        return ring[slot], b_ring[slot]

    def load_x(seq, j, tt, slot):
        r0 = j * NB + tt * 128
        S_.op(SP, lambda e: e.dma_start(out=xt[slot], in_=x_d[seq, r0:r0 + 128, :]),
              writes=[b_xt[slot]], chan=c_xt[slot])

    wuv = wsb[:, 0:512]
    wpool = wsb[:, 512:1024].rearrange("p (g d) -> p g d", g=4)
    widxW = wsb[:, 1024:1088].rearrange("p (k n) -> p k n", k=8)
    pscale = wsf[:, 0:4]
    convw = wsf[:, 4:136].rearrange("p (c k) -> p c k", k=3)
    convb = wsf[:, 136:180]
    idf = ident[:, 0:128]
    fSq = ident[:, 128:129]
    fSi = ident[:, 129:130]
    invc0 = ident[:, 130:194].rearrange("p (g t) -> p g t", g=4)
    invc0w = ident[:, 450:514].rearrange("p (g t) -> p g t", g=4)
    Pmq = pmb[:, 0:128]
    Pmi = pmb[:, 128:256]

    dbg_n = [0]

    def tap(ap_sb, buf, ncols, eng=SP):
        if not dbg:
            return
        i = dbg_n[0]
        dbg_n[0] += 1
        ch = Chan(S_, "dbg%d" % i)
        S_.op(eng, lambda e: e.dma_start(out=dbg_d[i, :, 0:ncols], in_=ap_sb), reads=[buf], chan=ch)
        taps.append(ch)

    taps = []

    def rope_tables(seq, j, sin_ops):
        t0 = j * NB
        S_.op(SP, lambda e: e.dma_start(out=posi, in_=pos_d[seq, t0:t0 + NB].partition_broadcast(128)),
              writes=[B["Ibuf1"]], chan=c_pos)
        S_.op(DVE, lambda e: e.tensor_copy(out=posf, in_=posi), reads=[B["Ibuf1"]], writes=[B["posf"]])
        for (fS, ct, st, cn, sn) in ((fSq, cosq, sinq, "cosq", "sinq"), (fSi, cosi, sini, "cosi", "sini")):
            a0, a1, a2, a3 = tr
            ba = [B["tr0"], B["tr1"], B["tr2"], B["tr3"]]
            S_.op(DVE, lambda e, fS=fS: e.tensor_scalar(out=a0, in0=posf, scalar1=fS, scalar2=None,
                                                        op0=ALU.mult),
                  reads=[B["posf"], b_const], writes=[ba[0]])
            S_.op(DVE, lambda e: e.tensor_scalar(out=a1, in0=a0, scalar1=float(1.0 / TWO_PI), scalar2=0.5,
                                                 op0=ALU.mult, op1=ALU.add), reads=[ba[0]], writes=[ba[1]])
            S_.op(DVE, lambda e: e.tensor_copy(out=tki, in_=a1), reads=[ba[1]], writes=[B["tki"]])
            S_.op(DVE, lambda e: e.tensor_copy(out=a1, in_=tki), reads=[B["tki"]], writes=[ba[1]])
            S_.op(DVE, lambda e: e.scalar_tensor_tensor(out=a2, in0=a1, scalar=-C1, in1=a0,
                                                        op0=ALU.mult, op1=ALU.add),
                  reads=[ba[1], ba[0]], writes=[ba[2]])
            S_.op(DVE, lambda e: e.scalar_tensor_tensor(out=a0, in0=a1, scalar=-C2, in1=a2,
                                                        op0=ALU.mult, op1=ALU.add),
                  reads=[ba[1], ba[2]], writes=[ba[0]])
            S_.op(DVE, lambda e: e.tensor_scalar(out=a1, in0=a0, scalar1=float(-np.pi), scalar2=float(TWO_PI),
                                                 op0=ALU.is_lt, op1=ALU.mult), reads=[ba[0]], writes=[ba[1]])
            S_.op(DVE, lambda e: e.tensor_tensor(out=a0, in0=a0, in1=a1, op=ALU.add),
                  reads=[ba[0], ba[1]], writes=[ba[0]])
            S_.op(DVE, lambda e: e.tensor_scalar(out=a2, in0=a0, scalar1=float(np.pi / 2), scalar2=None,
                                                 op0=ALU.add), reads=[ba[0]], writes=[ba[2]])
            S_.op(DVE, lambda e: e.tensor_scalar(out=a1, in0=a2, scalar1=float(np.pi), scalar2=float(-TWO_PI),
                                                 op0=ALU.is_gt, op1=ALU.mult), reads=[ba[2]], writes=[ba[1]])
            S_.op(DVE, lambda e: e.tensor_tensor(out=a2, in0=a2, in1=a1, op=ALU.add),
                  reads=[ba[2], ba[1]], writes=[ba[2]])
            S_.op(DVE, lambda e: e.tensor_scalar(out=a0, in0=a0, scalar1=-PI_SAFE, scalar2=PI_SAFE,
                                                 op0=ALU.max, op1=ALU.min), reads=[ba[0]], writes=[ba[0]])
            S_.op(DVE, lambda e: e.tensor_scalar(out=a2, in0=a2, scalar1=-PI_SAFE, scalar2=PI_SAFE,
                                                 op0=ALU.max, op1=ALU.min), reads=[ba[2]], writes=[ba[2]])
            k0 = 0 if fS is fSq else 1
            S_.op(DVE, lambda e, k0=k0: e.tensor_copy(out=sarg[2 * k0], in_=a0), reads=[ba[0]], writes=[B["sarg"]])
            S_.op(DVE, lambda e, k0=k0: e.tensor_copy(out=sarg[2 * k0 + 1], in_=a2), reads=[ba[2]], writes=[B["sarg"]])
            sin_ops.append((st, sn, 2 * k0))
            sin_ops.append((ct, cn, 2 * k0 + 1))

    def rope_sins(sin_ops):
        for (dst, name, k) in sin_ops:
            S_.op(ACT, lambda e, dst=dst, k=k: e.activation(out=dst, in_=sarg[k], func=AF.Sin),
                  reads=[B["sarg"]], writes=[B[name]])


    def x_transposes(xslots):
        for tt in range(2):
            xs = xslots[tt]
            for g4 in range(2):
                wbk, bwb = next_wb(ALLB)
                for q in range(4):
                    fc = 4 * g4 + q
                    S_.op(PE, lambda e, wbk=wbk, q=q, fc=fc, xs=xs: e.transpose(
                        out=wbk[:, q * 128:(q + 1) * 128], in_=xt[xs][:, fc * 128:(fc + 1) * 128], identity=idf),
                        reads=[b_xt[xs], b_const], writes=[bwb])
                S_.op(ACT, lambda e, wbk=wbk, g4=g4, tt=tt: e.activation(
                    out=xT[:, 4 * g4:4 * g4 + 4, tt * 128:(tt + 1) * 128],
                    in_=wbk.rearrange("p (q t) -> p q t", q=4), func=AF.Copy),
                    reads=[bwb], writes=[B["xT"]])

    def p_gen(seq, j):
        t0 = j * NB
        first = (j == 0)
        tabs = {"q": (Pmq, cosq, sinq, "cosq", "sinq"), "i": (Pmi, cosi, sini, "cosi", "sini")}
        jobs = []
        for i in range(2):
            for q in range(2):
                jobs.append((7 + i, 128 * q, "pool", 2 * i + q))
        for i in range(4):
            for q in range(2):
                jobs.append((i, 128 * q, "rope", (qT[:, 2 * i + q, :], [B["qT"]], "q", None)))
        jobs.append((4, 0, "rope", (kT[:, t0:t0 + NB], [b_kT[2 * j], b_kT[2 * j + 1]], "q", (ckvp, B["ckvp"]))))
        jobs.append((4, 128, "rope", (kiT[:, t0:t0 + NB], [b_kiT[2 * j], b_kiT[2 * j + 1]], "i", None)))
        for i in range(2):
            for q in range(2):
                jobs.append((5 + i, 128 * q, "rope", (qiT[:, 2 * i + q, :], [B["qiT"]], "i", None)))

        def part2(ctx):
            (wbk, bwb, k, dst, bdst, tk) = ctx
            Pm, ct, st, cn, sn = tabs[tk]
            i = k % 2
            pbf, bpbf = prebf[i], B["prebf%d" % i]
            wb2, bwb2 = next_wb(ALLB)
            S_.op(PE, lambda e: e.matmul(wb2[:, 0:NB], lhsT=Pm, rhs=pbf, start=True, stop=True),
                  reads=[bpbf, b_pm], writes=[bwb2])
            S_.op(DVE, lambda e: e.tensor_tensor(out=rt1[i], in0=wbk[:, 0:NB], in1=ct, op=ALU.mult),
                  reads=[bwb, B[cn]], writes=[B["rt1_%d" % i]])
            S_.op(DVE, lambda e: e.tensor_tensor(out=rt2[i], in0=wb2[:, 0:NB], in1=st, op=ALU.mult),
                  reads=[bwb2, B[sn]], writes=[B["rt2_%d" % i]])
            eng = POOL if k % 2 == 0 else DVE
            S_.op(eng, lambda e: e.tensor_tensor(out=dst, in0=rt1[i], in1=rt2[i], op=ALU.add),
                  reads=[B["rt1_%d" % i], B["rt2_%d" % i]], writes=bdst)

        def pool_adds(g):
            w = 2 ** (g + 1)
            bu = B["ubuf%d" % g]
            src, bsrc = ubuf[g], bu
            L = 16 + NB
            sh = 1
            kk = 0
            while sh < w:
                dst, bd = [(pa, B["pa"]), (pb_, B["pb"])][kk % 2]
                kk += 1
                S_.op(POOL, lambda e, dst=dst, src=src, sh=sh: e.tensor_tensor(
                    out=dst[:, sh:L], in0=src[:, sh:L], in1=src[:, 0:L - sh], op=ALU.add),
                    reads=[bsrc], writes=[bd])
                src, bsrc = dst, bd
                sh *= 2
            pl, bpl = pooled[g], B["pooled%d" % g]
            if first:
                S_.op(POOL, lambda e, src=src: e.tensor_tensor(out=src[:, 16:32], in0=src[:, 16:32],
                                                               in1=invc0w[:, g, :], op=ALU.mult),
                      reads=[bsrc, b_const], writes=[bsrc])
            S_.op(POOL, lambda e, src=src: e.tensor_scalar(out=src[:, 16:L], in0=src[:, 16:L],
                                                           scalar1=float(1.0 / w), scalar2=0.0,
                                                           op0=ALU.mult, op1=ALU.add),
                  reads=[bsrc], writes=[bsrc])
            S_.op(POOL, lambda e, src=src: e.tensor_tensor(out=pl, in0=src[:, 16:L], in1=ubuf[g][:, 16:L],
                                                           op=ALU.subtract),
                  reads=[bsrc, bu], writes=[bpl])
            S_.op(POOL, lambda e: e.tensor_copy(out=ubuf[g][:, 0:16], in_=ubuf[g][:, NB:NB + 16]),
                  reads=[bu, bsrc], writes=[bu])

        def pool_mm(g):
            pl, bpl = pooled[g], B["pooled%d" % g]
            wbk, bwb = next_wb(ALLB)
            S_.op(PE, lambda e: e.matmul(wbk[:, 0:NB], lhsT=wpool[:, g, :], rhs=pl, start=True, stop=True),
                  reads=[bpl, b_const2], writes=[bwb])
            S_.op(ACT, lambda e: e.activation(out=pmix[:, g, :], in_=wbk[:, 0:NB], func=AF.Identity,
                                              scale=pscale[:, g:g + 1]),
                  reads=[bwb, b_const], writes=[B["pmix"]])

        for tt in range(2):
            wbk, bwb = next_wb(ALLB)
            for kc in range(8):
                S_.op(PE, lambda e, kc=kc, tt=tt, wbk=wbk: e.matmul(
                    wbk[:, 0:8], lhsT=xT[:, kc, tt * 128:(tt + 1) * 128], rhs=widxW[:, kc, :],
                    start=(kc == 0), stop=(kc == 7)), reads=[B["xT"], b_const2], writes=[bwb])
            S_.op(ACT, lambda e, tt=tt, wbk=wbk: e.activation(out=widx[:, tt, :], in_=wbk[:, 0:8], func=AF.Copy,
                                                              scale=float(8.0 ** -0.5 * 64.0 ** -0.5)),
                  reads=[bwb], writes=[B["widx"]])
        for tt in range(2):
            for h in range(H):
                S_.op(DVE, lambda e, h=h, tt=tt: e.tensor_scalar(out=dgw[:, tt, h, :], in0=idf,
                                                                 scalar1=widx[:, tt, h:h + 1], scalar2=None,
                                                                 op0=ALU.mult),
                      reads=[B["widx"], b_const], writes=[B["dgw%d" % tt]])
        cur_tile, wt, bwt = None, None, None
        pend = None
        k = 0
        pool_sched = {}
        for ji, (tile, coff, kind, arg) in enumerate(jobs):
            if tile != cur_tile:
                wt, bwt = wget(tile)
                cur_tile = tile
            wbk, bwb = next_wb(ALLB)
            wt3 = wt.rearrange("p (k n) -> p k n", k=8)
            for kc in range(8):
                S_.op(PE, lambda e, kc=kc, wbk=wbk, wt3=wt3, coff=coff: e.matmul(
                    wbk[:, 0:NB], lhsT=wt3[:, kc, coff:coff + 128], rhs=xT[:, kc, :],
                    start=(kc == 0), stop=(kc == 7)), reads=[bwt, B["xT"]], writes=[bwb])
            if kind == "rope":
                dst, bdst, tk, keep = arg
                i = k % 2
                pbf, bpbf = prebf[i], B["prebf%d" % i]
                S_.op(ACT, lambda e, pbf=pbf, wbk=wbk: e.activation(out=pbf, in_=wbk[:, 0:NB], func=AF.Copy),
                      reads=[bwb], writes=[bpbf])
                if keep is not None:
                    S_.op(POOL, lambda e, pbf=pbf, keep=keep: e.tensor_copy(out=keep[0], in_=pbf), reads=[bpbf],
                          writes=[keep[1]])
                if pend is not None:
                    part2(pend)
                pend = (wbk, bwb, k, dst, bdst, tk)
                k += 1
            else:
                g = arg
                if first:
                    S_.op(POOL, lambda e, g=g: e.memset(ubuf[g][:, 0:16], 0.0), writes=[B["ubuf%d" % g]])
                S_.op(ACT, lambda e, wbk=wbk, g=g: e.activation(out=ubuf[g][:, 16:16 + NB], in_=wbk[:, 0:NB],
                                                                func=AF.Copy),
                      reads=[bwb], writes=[B["ubuf%d" % g]])
                pool_sched.setdefault(ji + 1, []).append((pool_adds, g))
                pool_sched.setdefault(14 + g, []).append((pool_mm, g))
            for (fn_, g_) in pool_sched.pop(ji, []):
                fn_(g_)
            yield
        if pend is not None:
            part2(pend)
        for ji in sorted(pool_sched):
            for (fn_, g_) in pool_sched[ji]:
                fn_(g_)
        for tt in range(2):
            gi = 2 * j + tt
            wbk, bwb = next_wb(ALLB)
            S_.op(PE, lambda e, tt=tt, wbk=wbk: e.matmul(wbk[:, 0:512], lhsT=ckvp[:, tt * 128:(tt + 1) * 128],
                                                         rhs=wuv, start=True, stop=True),
                  reads=[B["ckvp"], b_const2], writes=[bwb])
            v3 = vaug[:, gi, :].rearrange("p (h e) -> p h e", e=65)
            S_.op(ACT, lambda e, wbk=wbk, v3=v3: e.activation(out=v3[:, :, 0:64],
                                                              in_=wbk.rearrange("p (h e) -> p h e", e=64),
                                                              func=AF.Copy),
                  reads=[bwb], writes=[b_v[gi]])
            S_.op(POOL, lambda e, v3=v3: e.memset(v3[:, :, 64:65], 1.0), writes=[b_v[gi]])
        yield

    def indexer_all(j):
        ISK = 3
        pend = []

        def acc(ctx):
            (tt, ks, cols, h, R, bR, bbk, bbb) = ctx
            S_.op(PE, lambda e: e.matmul(bbk[:, 0:cols], lhsT=dgw[:, tt, h, :], rhs=R[:, 0:cols],
                                         start=(h == 0), stop=(h == H - 1)),
                  reads=[B["dgw%d" % tt], bR, B["Ebuf0"]], writes=[bbb])
            if h == H - 1:
                Ib, bI = Ibuf[tt], B["Ibuf%d" % tt]
                S_.op(ACT, lambda e: e.activation(out=Ib[:, 512 * ks:512 * ks + cols], in_=bbk[:, 0:cols],
                                                  func=AF.Copy), reads=[bbb], writes=[bI])

        for tt in range(2):
            gi = 2 * j + tt
            nkb = gi + 1
            Nk = 128 * nkb
            nks = (Nk + 511) // 512
            kread = [b_kiT[i] for i in range(nkb)]
            for ks in range(nks):
                cols = min(512, Nk - 512 * ks)
                bbk, bbb = next_wb(only=(4, 5))
                for h in range(H):
                    m, base = h // 2, 64 * (h % 2)
                    wbk, bwb = next_wb()
                    S_.op(PE, lambda e, wbk=wbk, m=m, base=base, ks=ks, cols=cols, tt=tt: e.matmul(
                        wbk[:, 0:cols], lhsT=qiT[base:base + 64, m, tt * 128:(tt + 1) * 128],
                        rhs=kiT[base:base + 64, 512 * ks:512 * ks + cols], start=True, stop=True),
                        reads=[B["qiT"]] + kread, writes=[bwb])
                    ri = ist["ri"]
                    ist["ri"] += 1
                    R, bR = Rb[ri % 8], B["Rb%d" % (ri % 8)]
                    wr = [bR] + ([B["Ebuf0"]] if ist["fresh"] else [])
                    rd = [bwb] + ([] if ist["fresh"] else [B["Ebuf0"]])
                    ist["fresh"] = False
                    if h % 2 == 0:
                        S_.op(ACT, lambda e, wbk=wbk, R=R, cols=cols: e.activation(
                            out=R[:, 0:cols], in_=wbk[:, 0:cols], func=AF.Relu), reads=rd, writes=wr)
                    else:
                        S_.op(DVE, lambda e, wbk=wbk, R=R, cols=cols: e.tensor_scalar(
                            out=R[:, 0:cols], in0=wbk[:, 0:cols], scalar1=0.0, scalar2=None, op0=ALU.max),
                            reads=rd, writes=wr)
                    pend.append((tt, ks, cols, h, R, bR, bbk, bbb))
                    if len(pend) > ISK:
                        acc(pend.pop(0))
                yield
        while pend:
            acc(pend.pop(0))
        for tt in range(2):
            Nk = 128 * (2 * j + tt + 1)
            Ib, bI = Ibuf[tt], B["Ibuf%d" % tt]
            S_.op(DVE, lambda e, Ib=Ib, Nk=Nk: e.memset(Ib[0:64, Nk - 64:Nk], -1e30), reads=[bI], writes=[bI])
        yield

    def bisect_gen(j, tt):
        gi = 2 * j + tt
        Nk = 128 * (gi + 1)
        Ib, bI = Ibuf[tt], B["Ibuf%d" % tt]
        bb = B["bis%d" % tt]
        mk, bmk = maskb2[tt], B["maskb%d" % tt]
        bj = B["junk%d" % tt]
        jk = junk[:, 2048 * tt:2048 * tt + 2048]
        if tt == 0:
            cnt, dd, mid = bis[:, 0:1], bis[:, 1:2], bis[:, 2:3]
            if gi < 2:
                S_.op(DVE, lambda e: e.memset(mid, -1e29), writes=[bb])
            else:
                S_.op(DVE, lambda e: e.memset(mid, 0.0), writes=[bb, bj, B["Ebuf1"]])
                step = BSTEP0
                for it in range(NBIS):
                    S_.op(DVE, lambda e: e.tensor_scalar(
                        out=jk[:, 0:Nk], in0=Ib[:, 0:Nk], scalar1=mid, scalar2=None, op0=ALU.is_ge,
                        op1=ALU.add, accum_out=cnt), reads=[bI, bb], writes=[bj, bb])
                    S_.op(DVE, lambda e, step=step: e.tensor_scalar(
                        out=dd, in0=cnt, scalar1=float(TOPK) - 0.5, scalar2=2.0 * step, op0=ALU.is_ge,
                        op1=ALU.mult), reads=[bb], writes=[bb])
                    S_.op(DVE, lambda e, step=step: e.scalar_tensor_tensor(
                        out=mid, in0=dd, scalar=-step, in1=mid, op0=ALU.add, op1=ALU.add),
                        reads=[bb], writes=[bb])
                    step *= 0.5
                    yield
            S_.op(DVE, lambda e: e.tensor_scalar(out=mk[:, 0:Nk], in0=Ib[:, 0:Nk], scalar1=mid,
                                                 scalar2=None, op0=ALU.is_ge),
                  reads=[bI, bb, bj], writes=[bmk] + ([B["Ebuf1"]] if gi >= 2 else []))
        else:
            sacc, dd, mid = bis[:, 4:5], bis[:, 5:6], bis[:, 6:7]
            ng = [bis[:, 8:9], bis[:, 9:10]]
            if gi < 2:
                S_.op(DVE, lambda e: e.memset(mid, -1e29), writes=[bb])
            else:
                S_.op(ACT, lambda e: e.activation(out=ng[0], in_=bis[:, 7:8], func=AF.Copy, scale=0.0),
                      reads=[B["biszero"]], writes=[bb, bj, B["Ebuf1"]])
                step = BSTEP0
                for it in range(NBIS):
                    n0, n1 = ng[it % 2], ng[(it + 1) % 2]
                    S_.op(ACT, lambda e, n0=n0: e.activation(
                        out=jk[:, 0:Nk], in_=Ib[:, 0:Nk], func=AF.Sign, bias=n0, scale=1.0, accum_out=sacc),
                        reads=[bI, bb], writes=[bj, bb])
                    S_.op(ACT, lambda e: e.activation(out=dd, in_=sacc, func=AF.Sign, bias=bisc[:, gi:gi + 1],
                                                      scale=1.0),
                          reads=[bb, B["biszero"]], writes=[bb])
                    S_.op(ACT, lambda e, n0=n0, n1=n1, step=step: e.activation(
                        out=n1, in_=dd, func=AF.Identity, scale=-step, bias=n0), reads=[bb], writes=[bb])
                    step *= 0.5
                    yield
                nf = ng[NBIS % 2]
                S_.op(ACT, lambda e, nf=nf: e.activation(out=mid, in_=nf, func=AF.Copy, scale=-1.0),
                      reads=[bb], writes=[bb])
            S_.op(DVE, lambda e: e.tensor_scalar(out=mk[:, 0:Nk], in0=Ib[:, 0:Nk], scalar1=mid,
                                                 scalar2=None, op0=ALU.is_ge),
                  reads=[bI, bb, bj], writes=[bmk] + ([B["Ebuf1"]] if gi >= 2 else []))
        yield

    def attention(j, tt):
        gi = 2 * j + tt
        nkb = gi + 1
        mk, bmk = maskb2[tt], B["maskb%d" % tt]
        AB = (2, 3)
        for k8 in range(0, nkb, 8):
            nq = min(8, nkb - k8)
            wbk, bwb = next_wb(AB)
            wbb = wbk.bitcast(BF16)
            for q in range(nq):
                kb = k8 + q
                S_.op(PE, lambda e, wbb=wbb, q=q, kb=kb: e.transpose(
                    out=wbb[:, q * 128:(q + 1) * 128], in_=mk[:, kb * 128:(kb + 1) * 128], identity=idb),
                    reads=[bmk, B["idb"]], writes=[bwb])
            S_.op(ACT, lambda e, wbb=wbb, k8=k8, nq=nq: e.activation(
                out=maskT[:, k8:k8 + nq, :], in_=wbb[:, 0:128 * nq].rearrange("p (q t) -> p q t", q=nq),
                func=AF.Identity, scale=30000.0, bias=-30000.0), reads=[bwb], writes=[B["maskT"]])
        kTr = [b_kT[i] for i in range(nkb)]
        vr = [b_v[i] for i in range(nkb)]

        def pv(hp):
            Eb, bE = Ebuf2[hp % 2], B["Ebuf%d" % (hp % 2)]
            for hh in range(2):
                h = 2 * hp + hh
                ob, bob = abank[h // 4], b_ab[h // 4]
                for kb in range(nkb):
                    S_.op(PE, lambda e, ob=ob, h=h, hh=hh, kb=kb, Eb=Eb: e.matmul(
                        ob[:, (h % 4) * 65:(h % 4) * 65 + 65], lhsT=Eb[:, kb, hh * 128:(hh + 1) * 128],
                        rhs=vaug[:, kb, h * 65:(h + 1) * 65], start=(kb == 0), stop=(kb == nkb - 1)),
                        reads=[bE] + vr, writes=[bob])

        pend = None
        for hp in range(4):
            Eb, bE = Ebuf2[hp % 2], B["Ebuf%d" % (hp % 2)]
            for k2 in range(0, nkb, 2):
                nq = min(2, nkb - k2)
                wbk, bwb = next_wb(AB)
                for q in range(nq):
                    kb = k2 + q
                    o3 = wbk[:, q * 256:(q + 1) * 256].rearrange("p (a t) -> p a t", a=2)
                    S_.op(PE, lambda e, o3=o3, kb=kb, hp=hp: e.matmul(
                        o3, lhsT=kT[:, kb * 128:(kb + 1) * 128],
                        rhs=qT[:, 2 * hp:2 * hp + 2, tt * 128:(tt + 1) * 128], start=True, stop=False),
                        reads=[B["qT"]] + kTr, writes=[bwb])
                    S_.op(PE, lambda e, o3=o3, kb=kb: e.matmul(
                        o3, lhsT=idb, rhs=maskT[:, kb:kb + 1, :].to_broadcast([128, 2, 128]),
                        start=False, stop=True), reads=[B["maskT"], B["idb"]], writes=[bwb])
                S_.op(ACT, lambda e, wbk=wbk, k2=k2, nq=nq, Eb=Eb: e.activation(
                    out=Eb[:, k2:k2 + nq, :], in_=wbk[:, 0:256 * nq].rearrange("p (q n) -> p q n", q=nq),
                    func=AF.Exp, scale=float(128.0 ** -0.5)), reads=[bwb], writes=[bE])
            if pend is not None:
                pv(pend)
            pend = hp
        pv(pend)
        for half in range(2):
            ob, bob = abank[half], b_ab[half]
            o3 = ob[:, 0:260].rearrange("p (h e) -> p h e", e=65)
            S_.op(DVE, lambda e, o3=o3, half=half: e.reciprocal(
                out=rden[:, 4 * half:4 * half + 4].unsqueeze(2), in_=o3[:, :, 64:65]),
                reads=[bob], writes=[B["rden"]])
            S_.op(DVE, lambda e, o3=o3, half=half: e.tensor_tensor(
                out=attok[:, 256 * half:256 * half + 256].rearrange("p (h e) -> p h e", e=64),
                in0=o3[:, :, 0:64],
                in1=rden[:, 4 * half:4 * half + 4].unsqueeze(2).to_broadcast([128, 4, 64]), op=ALU.mult),
                reads=[bob, B["rden"]], writes=[B["attok"]])
        wbk, bwb = next_wb(AB)
        wbb = wbk.bitcast(BF16)
        for q in range(4):
            S_.op(PE, lambda e, wbb=wbb, q=q: e.transpose(out=wbb[:, q * 128:(q + 1) * 128],
                                                          in_=attok[:, q * 128:(q + 1) * 128], identity=idb),
                  reads=[B["attok"], B["idb"]], writes=[bwb])
        S_.op(ACT, lambda e, wbb=wbb: e.activation(
            out=attnT[:, :, tt * 128:(tt + 1) * 128], in_=wbb[:, 0:512].rearrange("p (q t) -> p q t", q=4),
            func=AF.Copy), reads=[bwb], writes=[B["attnT"]])

    def layernorm(src, bsrc, dst, bdst, gi_, bi_, aff=POOL):
        for half in range(2):
            S_.op(DVE, lambda e, half=half: e.bn_stats(out=lnst[:, 6 * half:6 * half + 6],
                                                       in_=src[:, 512 * half:512 * half + 512]),
                  reads=[bsrc], writes=[B["lnst"]])
        S_.op(DVE, lambda e: e.bn_aggr(out=lnmv[:, 0:2], in_=lnst), reads=[B["lnst"]], writes=[B["lnmv"]])
        S_.op(DVE, lambda e: e.tensor_scalar(out=lnmv[:, 2:3], in0=lnmv[:, 1:2], scalar1=LN_EPS, scalar2=None,
                                             op0=ALU.add), reads=[B["lnmv"]], writes=[B["lnmv"]])
        S_.op(ACT, lambda e: e.activation(out=lnmv[:, 2:3], in_=lnmv[:, 2:3], func=AF.Sqrt),
              reads=[B["lnmv"]], writes=[B["lnmv"]])
        S_.op(DVE, lambda e: e.reciprocal(out=lnmv[:, 2:3], in_=lnmv[:, 2:3]), reads=[B["lnmv"]],
              writes=[B["lnmv"]])
        S_.op(DVE, lambda e: e.scalar_tensor_tensor(out=lnmv[:, 3:4], in0=lnmv[:, 0:1], scalar=-1.0,
                                                    in1=lnmv[:, 2:3], op0=ALU.mult, op1=ALU.mult),
              reads=[B["lnmv"]], writes=[B["lnmv"]])
        S_.op(ACT, lambda e: e.activation(out=dst, in_=src, func=AF.Identity, scale=lnmv[:, 2:3],
                                          bias=lnmv[:, 3:4]), reads=[bsrc, B["lnmv"]], writes=[bdst])
        S_.op(aff, lambda e: e.tensor_tensor(out=dst, in0=dst, in1=lnp[:, gi_, :], op=ALU.mult),
              reads=[bdst, b_const], writes=[bdst])
        S_.op(aff, lambda e: e.tensor_tensor(out=dst, in0=dst, in1=lnp[:, bi_, :], op=ALU.add),
              reads=[bdst, b_const], writes=[bdst])

    def c_phase(seq, j, xslots):
        def outproj(ctx):
            fc, i2, wo, bwm = ctx
            for tt in range(2):
                for half in range(2):
                    ai = 2 * tt + half
                    S_.op(PE, lambda e, ai=ai, tt=tt, half=half: e.matmul(
                        abank[ai][:, 0:512], lhsT=Gfc[i2][:, tt * 128:(tt + 1) * 128],
                        rhs=wo[:, half * 512:(half + 1) * 512], start=(fc == 0), stop=(fc == 7)),
                        reads=[B["Gfc%d" % i2], bwm], writes=[b_ab[ai]])

        pend = None
        for fc in range(8):
            wg, bwg = wget(9 + 2 * fc)
            wm, bwm = wget(10 + 2 * fc)
            wg3 = wg.rearrange("p (k n) -> p k n", k=8)
            wab3 = wm[:, 0:512].rearrange("p (k n) -> p k n", k=4)
            wpb3 = wm[:, 512:1024].rearrange("p (k n) -> p k n", k=4)
            wo = wm[:, 1024:2048]
            i2 = fc % 2
            bkA, bbA = next_wb()
            bkB, bbB = next_wb()
            for a in range(2):
                for kc in range(8):
                    S_.op(PE, lambda e, a=a, kc=kc, bkA=bkA, wg3=wg3: e.matmul(
                        bkA[:, a * NB:(a + 1) * NB], lhsT=wg3[:, kc, a * 128:(a + 1) * 128], rhs=xT[:, kc, :],
                        start=(kc == 0), stop=(kc == 7)), reads=[bwg, B["xT"]], writes=[bbA])
            for kc in range(4):
                S_.op(PE, lambda e, kc=kc, bkB=bkB, wab3=wab3: e.matmul(
                    bkB[:, 0:NB], lhsT=wab3[:, kc, :], rhs=attnT[:, kc, :], start=(kc == 0), stop=(kc == 3)),
                    reads=[bwm, B["attnT"]], writes=[bbB])
            for kc in range(4):
                S_.op(PE, lambda e, kc=kc, bkB=bkB, wpb3=wpb3: e.matmul(
                    bkB[:, NB:2 * NB], lhsT=wpb3[:, kc, :], rhs=pmix[:, kc, :], start=(kc == 0), stop=(kc == 3)),
                    reads=[bwm, B["pmix"]], writes=[bbB])
            S_.op(ACT, lambda e, bkA=bkA, i2=i2: e.activation(out=sgA[i2], in_=bkA[:, 0:NB], func=AF.Sigmoid),
                  reads=[bbA], writes=[B["sgA%d" % i2]])
            S_.op(ACT, lambda e, bkA=bkA, i2=i2: e.activation(out=sgB[i2], in_=bkA[:, NB:2 * NB], func=AF.Sigmoid),
                  reads=[bbA], writes=[B["sgB%d" % i2]])
            S_.op(DVE, lambda e, bkB=bkB, i2=i2: e.tensor_tensor(out=g1[i2], in0=bkB[:, 0:NB], in1=sgA[i2],
                                                                 op=ALU.mult),
                  reads=[bbB, B["sgA%d" % i2]], writes=[B["g1_%d" % i2]])
            S_.op(DVE, lambda e, bkB=bkB, i2=i2: e.tensor_tensor(out=g2[i2], in0=bkB[:, NB:2 * NB], in1=sgB[i2],
                                                                 op=ALU.mult),
                  reads=[bbB, B["sgB%d" % i2]], writes=[B["g2_%d" % i2]])
            S_.op(POOL, lambda e, i2=i2: e.tensor_tensor(out=Gfc[i2], in0=g1[i2], in1=g2[i2], op=ALU.add),
                  reads=[B["g1_%d" % i2], B["g2_%d" % i2]], writes=[B["Gfc%d" % i2]])
            if pend is not None:
                outproj(pend)
            pend = (fc, i2, wo, bwm)
        outproj(pend)
        for tt in range(2):
            xs = xslots[tt]
            for half in range(2):
                ai = 2 * tt + half
                S_.op(DVE, lambda e, ai=ai, tt=tt, half=half, xs=xs: e.scalar_tensor_tensor(
                    out=pre[tt][:, 512 * half:512 * half + 512], in0=xt[xs][:, 512 * half:512 * half + 512],
                    scalar=float(ALPHA), in1=abank[ai][:, 0:512], op0=ALU.mult, op1=ALU.add),
                    reads=[b_xt[xs], b_ab[ai]], writes=[B["pre%d" % tt]])

    def c_tail(seq, j):
        for tt in range(2):
            layernorm(pre[tt], B["pre%d" % tt], pre[tt], B["pre%d" % tt], 0, 1, aff=POOL)
            yield
            for g4 in range(2):
                wbk, bwb = next_wb(only=(6, 7))
                for q in range(4):
                    fc = 4 * g4 + q
                    S_.op(PE, lambda e, wbk=wbk, q=q, fc=fc, tt=tt: e.transpose(
                        out=wbk[:, q * 128:(q + 1) * 128], in_=pre[tt][:, fc * 128:(fc + 1) * 128], identity=idf),
                        reads=[B["pre%d" % tt], b_const], writes=[bwb])
                S_.op(ACT, lambda e, wbk=wbk, g4=g4, tt=tt: e.activation(
                    out=x1T[:, 4 * g4:4 * g4 + 4, tt * 128:(tt + 1) * 128],
                    in_=wbk.rearrange("p (q t) -> p q t", q=4), func=AF.Copy),
                    reads=[bwb], writes=[B["x1T"]])
            yield

    def ffn_gen(seq, j):
        t0 = j * NB
        first = (j == 0)
        st = {"wd": None, "bwd": None}

        def down(c):
            if c % 2 == 0:
                st["wd"], st["bwd"] = wget(25 + c + (c + 1) // 2 + 1)
            wd, bwd = st["wd"], st["bwd"]
            i4 = c % 4
            wdc = wd[:, (c % 2) * 1024:(c % 2) * 1024 + 1024]
            for tt in range(2):
                for half in range(2):
                    ai = 2 * tt + half
                    S_.op(PE, lambda e, ai=ai, tt=tt, half=half: e.matmul(
                        abank[ai][:, 0:512], lhsT=actc[i4][:, tt * 128:(tt + 1) * 128],
                        rhs=wdc[:, half * 512:(half + 1) * 512], start=(c == 0), stop=(c == NCH - 1)),
                        reads=[B["actc%d" % i4], bwd], writes=[b_ab[ai]])

        hbs = {}

        def s0(c):
            wu, bwu = wget(25 + c + (c + 1) // 2)
            wu3 = wu.rearrange("p (k n) -> p k n", k=8)
            hb, bhb = next_wb()
            hbs[c] = (hb, bhb)
            for a in range(2):
                for kc in range(8):
                    S_.op(PE, lambda e, a=a, kc=kc: e.matmul(
                        hb[:, a * NB:(a + 1) * NB], lhsT=wu3[:, kc, a * 128:(a + 1) * 128], rhs=x1T[:, kc, :],
                        start=(kc == 0), stop=(kc == 7)), reads=[bwu, B["x1T"]], writes=[bhb])

        def s1(c):
            hb, bhb = hbs.pop(c)
            i2 = c % 2
            hr, bhr = hraw[i2], B["hraw%d" % i2]
            if first:
                S_.op(POOL, lambda e: e.memset(halo[:, c, :, :], 0.0), writes=[B["halo"]])
            S_.op(ACT, lambda e: e.activation(out=hr[:, :, 0:2], in_=halo[:, c, :, :], func=AF.Copy),
                  reads=[B["halo"]], writes=[bhr])
            S_.op(ACT, lambda e: e.activation(out=hr[:, :, 2:2 + NB], in_=hb.rearrange("p (a n) -> p a n", a=2),
                                              func=AF.Copy), reads=[bhb], writes=[bhr])
            S_.op(ACT, lambda e: e.activation(out=halo[:, c, :, :], in_=hr[:, :, NB:NB + 2], func=AF.Copy),
                  reads=[bhr], writes=[B["halo"]])

        def s23(c):
            i2 = c % 2
            hr, bhr = hraw[i2], B["hraw%d" % i2]
            ac = aconv[i2]
            for a in range(2):
                cc = c + NCH * a
                bac = B["ac%d_%d" % (a, i2)]
                S_.op(POOL, lambda e, a=a, cc=cc: e.tensor_scalar(
                    out=ac[:, a, :], in0=hr[:, a, 2:2 + NB], scalar1=convw[:, cc, 2:3], scalar2=convb[:, cc:cc + 1],
                    op0=ALU.mult, op1=ALU.add), reads=[bhr, b_const], writes=[bac])
            for a in range(2):
                cc = c + NCH * a
                bac = B["ac%d_%d" % (a, i2)]
                S_.op(DVE, lambda e, a=a, cc=cc: e.scalar_tensor_tensor(
                    out=ac[:, a, :], in0=hr[:, a, 1:1 + NB], scalar=convw[:, cc, 1:2], in1=ac[:, a, :],
                    op0=ALU.mult, op1=ALU.add), reads=[bhr, b_const, bac], writes=[bac])
                S_.op(DVE, lambda e, a=a, cc=cc: e.scalar_tensor_tensor(
                    out=ac[:, a, :], in0=hr[:, a, 0:NB], scalar=convw[:, cc, 0:1], in1=ac[:, a, :],
                    op0=ALU.mult, op1=ALU.add), reads=[bhr, b_const, bac], writes=[bac])

        def s45(c):
            i2 = c % 2
            ac = aconv[i2]
            bg, bv = B["ac0_%d" % i2], B["ac1_%d" % i2]
            S_.op(ACT, lambda e: e.activation(out=ac[:, 0, :], in_=ac[:, 0, :], func=AF.Silu),
                  reads=[bg], writes=[bg])
            S_.op(POOL, lambda e: e.tensor_tensor(out=actc[c % 4], in0=ac[:, 0, :], in1=ac[:, 1, :], op=ALU.mult),
                  reads=[bg, bv], writes=[B["actc%d" % (c % 4)]])

        for i in range(NCH + 1 + FSKEW):
            if i < NCH:
                s0(i)
                s1(i)
                s23(i)
            if 0 <= i - 1 < NCH:
                s45(i - 1)
            if 0 <= i - 1 - FSKEW < NCH:
                down(i - 1 - FSKEW)
            yield
        for tt in range(2):
            for half in range(2):
                ai = 2 * tt + half
                S_.op(DVE, lambda e, ai=ai, tt=tt, half=half: e.scalar_tensor_tensor(
                    out=pre[tt][:, 512 * half:512 * half + 512], in0=pre[tt][:, 512 * half:512 * half + 512],
                    scalar=float(ALPHA), in1=abank[ai][:, 0:512], op0=ALU.mult, op1=ALU.add),
                    reads=[b_ab[ai]], writes=[B["pre%d" % tt]])
            layernorm(pre[tt], B["pre%d" % tt], pre[tt], B["pre%d" % tt], 2, 3)
            r0 = t0 + tt * 128
            S_.op(SP, lambda e, tt=tt, r0=r0: e.dma_start(out=out_d[seq, r0:r0 + 128, :], in_=pre[tt]),
                  reads=[B["pre%d" % tt]], chan=c_ot[tt])
        yield

    def drain(g, n=10 ** 9):
        k = 0
        if g is None:
            return False
        for _ in g:
            k += 1
            if k >= n:
                return True
        return False

    blocks = [(s, j) for s in range(NSEQ) for j in range(NBLK)][:nblocks]
    import itertools
    load_x(blocks[0][0], blocks[0][1], 0, 0)
    load_x(blocks[0][0], blocks[0][1], 1, 1)
    prev_ffn = None
    prev_tail = None
    sin_ops = []
    rope_tables(blocks[0][0], blocks[0][1], sin_ops)
    rope_sins(sin_ops)
    for bi, (s, j) in enumerate(blocks):
        cur = (0, 1) if bi % 2 == 0 else (2, 3)
        nxt = (2, 3) if bi % 2 == 0 else (0, 1)
        S_.phase = "P%d" % bi
        x_transposes(cur)
        pg = p_gen(s, j)
        drain(pg, 6)
        if bi + 1 < len(blocks):
            s2, j2 = blocks[bi + 1]
            load_x(s2, j2, 0, nxt[0])
            load_x(s2, j2, 1, nxt[1])
        drain(pg)
        S_.phase = "I%d" % bi
        ist["fresh"] = True
        ig = indexer_all(j)
        alive_i, alive_t = True, prev_tail is not None
        while alive_i or alive_t:
            if alive_i:
                alive_i = drain(ig, 1)
            if alive_t:
                alive_t = drain(prev_tail, 1)
        prev_tail = None
        bg0 = bisect_gen(j, 0)
        bg1 = bisect_gen(j, 1)
        fg = prev_ffn
        alive_0, alive_1, alive_f = True, True, fg is not None
        while alive_0 or alive_1 or alive_f:
            if alive_f:
                S_.phase = "F%d" % (bi - 1)
                alive_f = drain(fg, 1)
            S_.phase = "B%d" % bi
            if alive_0:
                alive_0 = drain(bg0, 1)
            if alive_1:
                alive_1 = drain(bg1, 1)
        S_.phase = "A%d" % bi
        sin_ops = []
        if bi + 1 < len(blocks):
            rope_tables(blocks[bi + 1][0], blocks[bi + 1][1], sin_ops)
        attention(j, 0)
        attention(j, 1)
        rope_sins(sin_ops)
        S_.phase = "C%d" % bi
        c_phase(s, j, cur)
        prev_tail = c_tail(s, j)
        prev_ffn = ffn_gen(s, j)
    S_.phase = "C%d" % (len(blocks) - 1)
    drain(prev_tail)
    S_.phase = "F%d" % (len(blocks) - 1)
    drain(prev_ffn)

    fin = [(c.key, c.cum, {}) for c in Chan.ALL if c.cum > 0]
    S_.wait_tokens(SP, fin)
    if order is None:
        return ws["rec"]
    import os
    if os.environ.get("DUMP_LABELS"):
        import json
        json.dump(S_.labels, open(os.environ["DUMP_LABELS"], "w"))
    S_.emit()
    return nc


_CACHE = {}


def _prep_small(w_in, w_uv, w_pool, pool_scale, conv_w, conv_b):
    wsb = np.zeros((128, 1088), np.float32)
    wsb[:, 0:512] = w_uv.transpose(1, 0, 2).reshape(128, 512)
    wsb[:, 512:1024] = w_pool.transpose(1, 0, 2).reshape(128, 512)
    wsb[:, 1024:1088] = w_in[:, 1664:1672].reshape(8, 128, 8).transpose(1, 0, 2).reshape(128, 64)
    wsf = np.zeros((128, 180), np.float32)
    wsf[:, 0:4] = pool_scale.reshape(4, 128).T
    wsf[:, 4:136] = conv_w.reshape(3, 44, 128).transpose(2, 1, 0).reshape(128, 132)
    wsf[:, 136:180] = conv_b.reshape(44, 128).T
    return wsb, wsf


def kernel(x, positions, w_in, w_uv, w_attn_branch, w_pool, pool_scale, w_pool_branch, w_out,
           ln1_g, ln1_b, conv_w, conv_b, w_ffn_up, w_ffn_down, ln2_g, ln2_b, _dbg=False, _cut=99,
           _nblocks=NSEQ * NBLK, _ncores=8):
    x = np.asarray(x, np.float32)
    positions = np.asarray(positions, np.int32)
    f = lambda a: np.asarray(a, np.float32)[0]
    wall = build_wall(f(w_in), f(w_attn_branch), f(w_pool_branch), f(w_out), f(w_ffn_up), f(w_ffn_down))
    wsb, wsf = _prep_small(f(w_in), f(w_uv), f(w_pool), f(pool_scale), f(conv_w), f(conv_b))
    cf, cb = build_consts()
    lnp = np.ascontiguousarray(np.stack([f(ln1_g), f(ln1_b), f(ln2_g), f(ln2_b)]), dtype=np.float32)
    key = (bool(_dbg), _cut, _nblocks)
    if key not in _CACHE:
        order = build_program(dbg=_dbg, cut=_cut, nblocks=_nblocks, order=None)
        _CACHE[key] = build_program(dbg=_dbg, cut=_cut, nblocks=_nblocks, order=order)
    nc = _CACHE[key]
    in_maps = []
    for c in range(8):
        in_maps.append({
            "x": np.ascontiguousarray(x[2 * c:2 * c + 2]),
            "pos": np.ascontiguousarray(positions[2 * c:2 * c + 2]),
            "wall": wall, "wsb": wsb, "wsf": wsf, "cf": cf, "lnp": lnp,
        })
    res = run_bass_kernel_spmd(nc, in_maps[:_ncores], core_ids=list(range(_ncores)))
    out = np.concatenate([np.asarray(r["out"]) for r in res.results], axis=0).astype(np.float32)
    if _dbg:
        kernel.dbg = [np.asarray(r["dbg"]) for r in res.results]
    return out
```

```python
import bisect
from contextlib import ExitStack

import numpy as np
import ml_dtypes

import concourse.bass as bass
import concourse.mybir as mybir
from concourse.bass_utils import run_bass_kernel_spmd

F32 = mybir.dt.float32
BF16 = mybir.dt.bfloat16
I32 = mybir.dt.int32
AF = mybir.ActivationFunctionType
ALU = mybir.AluOpType

PE, ACT, DVE, POOL, SP = "pe", "act", "dve", "pool", "sp"
ENGINES = (PE, ACT, DVE, POOL, SP)
SELF_SYNC = {PE: False, ACT: True, DVE: True, POOL: True, SP: False}

D = 1024
S = 2048
NSEQ = 2
NB = 256
NBLK = S // NB
H = 8
DFF = 2816
NCH = DFF // 128
ALPHA = 2.0 ** 0.25
LN_EPS = 1e-5
TOPK = 256
THETA = 500000.0
TILE = 2048
NTILES = 9 + 16 + 22 + 11
TOT = NTILES * TILE
RING = 8
NBIS = 18
BSTEP0 = 4.0
FSKEW = 2
TWO_PI = 2.0 * np.pi
C1 = 6.28125
C2 = float(TWO_PI - 6.28125)
PI_SAFE = 3.1415925


class Buf:
    __slots__ = ("name", "w", "r", "excl")

    def __init__(self, name, excl=False):
        self.name = name
        self.w = None
        self.r = []
        self.excl = excl


class Chan:
    ALL = []

    def __init__(self, sched, name):
        Chan.ALL.append(self)
        self.key = "dma:" + name
        self.sem = sched.new_sem(name)
        self.cum = 0
        sched.sems[self.key] = self.sem


class Sched:
    def __init__(self, nc):
        self.nc = nc
        self.ops = {e: [] for e in ENGINES}
        self.cnt = {e: 0 for e in ENGINES}
        self.clock = {e: {} for e in ENGINES}
        self.sems = {}
        self.phase = "pro"
        self.labels = {e: [] for e in ENGINES}
        self._semctx = []
        for e in ENGINES:
            self.sems[e] = self.new_sem("eng_" + e)

    def new_sem(self, name):
        ctx = self.nc.semaphore(name)
        s = ctx.__enter__()
        self._semctx.append(ctx)
        return s

    def close(self):
        for c in reversed(self._semctx):
            c.__exit__(None, None, None)

    def _need(self, eng, tok, waits):
        if tok is None:
            return
        key, val, snap = tok
        if key == eng and not SELF_SYNC[eng]:
            return
        clk = self.clock[eng]
        if clk.get(key, 0) >= val:
            return
        waits.append((key, val))
        clk[key] = val
        for k, v in snap.items():
            if clk.get(k, 0) < v:
                clk[k] = v

    def op(self, eng, fn, reads=(), writes=(), chan=None):
        if any(b.excl for b in reads):
            writes = list(writes) + [b for b in reads if b.excl]
            reads = [b for b in reads if not b.excl]
        waits = []
        for b in reads:
            self._need(eng, b.w, waits)
        for b in writes:
            self._need(eng, b.w, waits)
            for t in b.r:
                self._need(eng, t, waits)
        wd = {}
        for k, v in waits:
            if wd.get(k, 0) < v:
                wd[k] = v
        if chan is None:
            self.cnt[eng] += 1
            val = self.cnt[eng]
            key = eng
            inc = 1
        else:
            chan.cum += 16
            val = chan.cum
            key = chan.key
            inc = 16
        tok = (key, val, dict(self.clock[eng]))
        self.ops[eng].append((list(wd.items()), fn, key, inc))
        if chan is None:
            self.labels[eng].append(self.phase)
        for b in reads:
            b.r.append(tok)
        for b in writes:
            b.w = tok
            b.r = []
        return tok

    def wait_tokens(self, eng, toks):
        waits = []
        for t in toks:
            self._need(eng, t, waits)
        wd = {}
        for k, v in waits:
            if wd.get(k, 0) < v:
                wd[k] = v
        if wd:
            self.ops[eng].append((list(wd.items()), None, None, 0))

    def emit(self):
        nc = self.nc
        sems = self.sems
        ops = self.ops
        sig = {e: set() for e in ENGINES}
        for e in ENGINES:
            for waits, fn, key, inc in ops[e]:
                for k, v in waits:
                    if k in sig:
                        sig[k].add(v)
        sigl = {e: sorted(sig[e]) for e in ENGINES}

        def rank(k, v):
            if k in sigl:
                return bisect.bisect_right(sigl[k], v)
            return v

        def replay(e, ename, lst):
            idx = 0
            for waits, fn, key, inc in lst:
                for k, v in waits:
                    e.wait_ge(sems[k], rank(k, v))
                if fn is not None:
                    ins = fn(e)
                    if key == ename:
                        idx += 1
                        if idx in sig[ename]:
                            ins.then_inc(sems[key], 1)
                    else:
                        ins.then_inc(sems[key], inc)

        with nc.Block() as block:
            @block.tensor
            def _(e):
                replay(e, PE, ops[PE])

            @block.scalar
            def _(e):
                replay(e, ACT, ops[ACT])

            @block.vector
            def _(e):
                replay(e, DVE, ops[DVE])

            @block.gpsimd
            def _(e):
                replay(e, POOL, ops[POOL])

            @block.sync
            def _(e):
                replay(e, SP, ops[SP])


def _ktile(m):
    n = m.shape[1]
    return np.ascontiguousarray(m.reshape(8, 128, n).transpose(1, 0, 2)).reshape(128, 8 * n)


def build_wall(w_in, w_ab, w_pb, w_out, w_up, w_down):
    tiles = []
    for i in range(4):
        tiles.append(_ktile(w_in[:, 256 * i:256 * i + 256]))
    kidx = w_in[:, 1672:1736]
    tiles.append(_ktile(np.concatenate([w_in[:, 1024:1152], kidx, kidx], axis=1)))
    for i in range(2):
        tiles.append(_ktile(w_in[:, 1152 + 256 * i:1152 + 256 * i + 256]))
    for i in range(2):
        tiles.append(_ktile(w_in[:, 1736 + 256 * i:1736 + 256 * i + 256]))
    for fc in range(8):
        tiles.append(_ktile(np.concatenate([w_in[:, 2248 + 128 * fc:2248 + 128 * fc + 128],
                                            w_in[:, 3272 + 128 * fc:3272 + 128 * fc + 128]], axis=1)))
        a = w_ab[:, 128 * fc:128 * fc + 128].reshape(4, 128, 128).transpose(1, 0, 2).reshape(128, 512)
        b = w_pb[:, 128 * fc:128 * fc + 128].reshape(4, 128, 128).transpose(1, 0, 2).reshape(128, 512)
        o = w_out[128 * fc:128 * fc + 128, :]
        tiles.append(np.concatenate([a, b, o], axis=1))
    for c in range(NCH):
        tiles.append(_ktile(np.concatenate([w_up[:, 128 * c:128 * c + 128],
                                            w_up[:, DFF + 128 * c:DFF + 128 * c + 128]], axis=1)))
        if c % 2 == 0:
            tiles.append(np.concatenate([w_down[128 * c:128 * c + 128, :],
                                         w_down[128 * (c + 1):128 * (c + 1) + 128, :]], axis=1))
    wall = np.concatenate(tiles, axis=1)
    assert wall.shape == (128, TOT), wall.shape
    return np.ascontiguousarray(wall, dtype=np.float32)


def build_consts():
    cf = np.zeros((128, 128 + 2 + 64 + 256 + 64), np.float32)
    cf[:, :128] = np.eye(128, dtype=np.float32)
    fq = (np.float32(THETA) ** (-np.arange(16, dtype=np.float32) / np.float32(16))).astype(np.float32)
    fi = (np.float32(THETA) ** (-np.arange(8, dtype=np.float32) / np.float32(8))).astype(np.float32)
    cf[0:16, 128] = -fq
    cf[16:32, 128] = fq
    for base in (0, 64):
        cf[base:base + 8, 129] = -fi
        cf[base + 8:base + 16, 129] = fi
    for g, w in enumerate((2, 4, 8, 16)):
        t = np.arange(16)
        cf[:, 130 + 16 * g:130 + 16 * g + 16] = (1.0 / np.minimum(t + 1, w)).astype(np.float32)[None, :]
    pm = np.zeros((128, 256), np.float32)
    for m in range(16):
        pm[m + 16, m] = 1.0
        pm[m, m + 16] = 1.0
    for base in (0, 64):
        for m in range(8):
            pm[base + m + 8, 128 + base + m] = 1.0
            pm[base + m, 128 + base + m + 8] = 1.0
    cf[:, 194:450] = pm
    for g, w in enumerate((2, 4, 8, 16)):
        t = np.arange(16)
        cf[:, 450 + 16 * g:450 + 16 * g + 16] = (w / np.minimum(t + 1, w)).astype(np.float32)[None, :]
    return cf, pm.astype(ml_dtypes.bfloat16)


class _Stop(Exception):
    pass


def build_program(dbg=False, cut=99, nblocks=NSEQ * NBLK, order=None):
    nc = bass.Bass("TRN2", target_bir_lowering=False)
    x_d = nc.dram_tensor("x", [NSEQ, S, D], F32, kind="ExternalInput").ap()
    pos_d = nc.dram_tensor("pos", [NSEQ, S], I32, kind="ExternalInput").ap()
    wall_d = nc.dram_tensor("wall", [128, TOT], F32, kind="ExternalInput").ap()
    wsb_d = nc.dram_tensor("wsb", [128, 1088], F32, kind="ExternalInput").ap()
    wsf_d = nc.dram_tensor("wsf", [128, 180], F32, kind="ExternalInput").ap()
    cf_d = nc.dram_tensor("cf", [128, 514], F32, kind="ExternalInput").ap()
    lnp_d = nc.dram_tensor("lnp", [4, D], F32, kind="ExternalInput").ap()
    out_d = nc.dram_tensor("out", [NSEQ, S, D], F32, kind="ExternalOutput").ap()
    wbf_d = nc.dram_tensor("wbf", [128, TOT], BF16).ap()
    dbg_d = None
    if dbg:
        dbg_d = nc.dram_tensor("dbg", [8, 128, 2048], F32, kind="ExternalOutput").ap()

    Chan.ALL = []
    S_ = Sched(nc)
    es = ExitStack()

    def sb(name, shape, dt):
        return nc.alloc_sbuf_tensor(name, shape, dt).ap()

    def ps(name):
        return nc.alloc_psum_tensor(name, [128, 512], F32).ap()

    ident = sb("ident", [128, 514], F32)
    pmb = sb("pmb", [128, 256], BF16)
    wsb = sb("wsb_s", [128, 1088], BF16)
    wsf = sb("wsf_s", [128, 180], F32)
    lnp = sb("lnp_s", [128, 4, D], F32)
    ring = [sb("ring%d" % i, [128, TILE], BF16) for i in range(RING)]
    kT = sb("kT", [128, S], BF16)
    kiT = sb("kiT", [128, S], BF16)
    vaug = sb("vaug", [128, 16, H * 65], BF16)
    xt = [sb("xt%d" % i, [128, D], F32) for i in range(4)]
    xT = sb("xT", [128, 8, NB], BF16)
    qT = sb("qT", [128, 8, NB], BF16)
    qiT = sb("qiT", [128, 4, NB], BF16)
    ckvp = sb("ckvp", [128, NB], BF16)
    prebf = [sb("prebf%d" % i, [128, NB], BF16) for i in range(2)]
    widx = sb("widx", [128, 2, 8], F32)
    cosq = sb("cosq", [128, NB], F32)
    sinq = sb("sinq", [128, NB], F32)
    cosi = sb("cosi", [128, NB], F32)
    sini = sb("sini", [128, NB], F32)
    rt1 = [sb("rt1_%d" % i, [128, NB], F32) for i in range(2)]
    rt2 = [sb("rt2_%d" % i, [128, NB], F32) for i in range(2)]
    ubuf = [sb("ubuf%d" % g, [128, 16 + NB], F32) for g in range(4)]
    pa = sb("pa", [128, 16 + NB], F32)
    pb_ = sb("pb", [128, 16 + NB], F32)
    pooled = [sb("pooled%d" % i, [128, NB], BF16) for i in range(4)]
    pmix = sb("pmix", [128, 4, NB], BF16)
    Ibuf = [sb("Ibuf%d" % i, [128, S], F32) for i in range(2)]
    dgw = sb("dgw", [128, 2, H, 128], BF16)
    maskb2 = [sb("maskb%d" % i, [128, S], BF16) for i in range(2)]
    maskT = sb("maskT", [128, 16, 128], BF16)
    Ebuf2 = [sb("Ebuf%d" % i, [128, 16, 256], BF16) for i in range(2)]
    idb = sb("idb", [128, 128], BF16)
    junk = Ebuf2[1].rearrange("p k n -> p (k n)")
    Rb = [Ebuf2[0].rearrange("p k n -> p (k n)")[:, 512 * i:512 * (i + 1)] for i in range(8)]
    _scr = Ibuf[1]
    tr = [_scr[:, NB * i:NB * (i + 1)] for i in range(4)]
    posf = _scr[:, NB * 4:NB * 5]
    tki = _scr[:, NB * 5:NB * 6].bitcast(I32)
    posi = _scr[:, NB * 6:NB * 7].bitcast(I32)
    sarg = [Ibuf[0][:, NB * i:NB * (i + 1)] for i in range(4)]
    bis = sb("bis", [128, 16], F32)
    bisc = sb("bisc", [128, 16], F32)
    rden = sb("rden", [128, 8], F32)
    attok = sb("attok", [128, 512], BF16)
    attnT = sb("attnT", [128, 4, NB], BF16)
    sgA = [sb("sgA%d" % i, [128, NB], F32) for i in range(2)]
    sgB = [sb("sgB%d" % i, [128, NB], F32) for i in range(2)]
    g1 = [sb("g1_%d" % i, [128, NB], F32) for i in range(2)]
    g2 = [sb("g2_%d" % i, [128, NB], F32) for i in range(2)]
    Gfc = [sb("Gfc%d" % i, [128, NB], BF16) for i in range(2)]
    pre = [sb("pre%d" % i, [128, D], F32) for i in range(2)]
    x1 = pre
    x1T = sb("x1T", [128, 8, NB], BF16)
    lnst = sb("lnst", [128, 12], F32)
    lnmv = sb("lnmv", [128, 4], F32)
    hraw = [sb("hraw%d" % i, [128, 2, 2 + NB], F32) for i in range(2)]
    halo = sb("halo", [128, NCH, 2, 2], F32)
    ctmp = sb("ctmp", [128, NB], F32)
    aconv = [sb("aconv%d" % i, [128, 2, NB], F32) for i in range(2)]
    actc = [sb("actc%d" % i, [128, NB], BF16) for i in range(4)]

    wbank = [ps("wbank%d" % i) for i in range(4)]
    abank = [ps("abank%d" % i) for i in range(4)]

    B = {}

    def nb(name):
        b = Buf(name)
        B[name] = b
        return b

    b_const = nb("const")
    b_ring = [nb("ring%d" % i) for i in range(RING)]
    c_ring = [Chan(S_, "ring%d" % i) for i in range(RING)]
    b_wbf = [nb("wbf%d" % i) for i in range(NTILES)]
    c_cast = [Chan(S_, "cast%d" % i) for i in range(4)]
    c_const = Chan(S_, "const")
    b_kT = [nb("kT%d" % i) for i in range(16)]
    b_kiT = [nb("kiT%d" % i) for i in range(16)]
    b_v = [nb("v%d" % i) for i in range(16)]
    b_xt = [nb("xt%d" % i) for i in range(4)]
    c_xt = [Chan(S_, "xt%d" % i) for i in range(4)]
    b_pos = B["Ebuf0"] if False else nb("posi_unused")
    c_pos = Chan(S_, "pos")
    b_ot = [nb("ot%d" % i) for i in range(2)]
    c_ot = [Chan(S_, "ot%d" % i) for i in range(2)]
    b_wb = [nb("wbank%d" % i) for i in range(4)]
    b_ab = [nb("abank%d" % i) for i in range(4)]
    for b in b_wb + b_ab:
        b.excl = True
    names = ["xT", "qT", "qiT", "ckvp", "widx", "cosq", "sinq", "cosi", "sini",
             "pa", "pb", "pooled0", "pooled1", "pooled2", "pooled3", "pmix", "maskT", "idb", "rden", "attok", "attnT",
             "x1T", "lnst", "lnmv", "halo"]
    for n in names:
        nb(n)
    for i in range(4):
        nb("ubuf%d" % i)
    for i in range(2):
        for n in ["rt1_", "rt2_", "Ibuf", "sgA", "sgB", "g1_", "g2_", "Gfc", "pre", "hraw",
                  "aconv", "Ebuf", "prebf", "maskb", "bis"]:
            nb("%s%d" % (n, i))

    for n in ["tr0", "tr1", "tr2", "tr3", "posf", "tki"]:
        B[n] = B["Ibuf1"]
    B["sarg"] = B["Ibuf0"]
    B["x1_0"] = B["pre0"]
    B["x1_1"] = B["pre1"]
    for i in range(4):
        nb("actc%d" % i)
    for i in range(8):
        nb("Rb%d" % i)
    nb("dgw0")
    nb("dgw1")
    ist = {"ri": 0, "fresh": True}
    nb("ctmp")
    for i in range(2):
        nb("ac0_%d" % i)
        nb("ac1_%d" % i)
    wb_rr = [0]

    ALLB = (0, 1)
    allbank = wbank + abank
    b_all = b_wb + b_ab

    wb_ctr = {}

    def next_wb(extra=(), only=None):
        pool = tuple(only) if only is not None else tuple([0, 1, 2, 3] + [4 + a for a in extra])
        wb_ctr[pool] = wb_ctr.get(pool, 0) + 1
        i = pool[wb_ctr[pool] % len(pool)]
        return allbank[i], b_all[i]

    def cdma(eng, out, in_, chan=c_const):
        S_.op(eng, lambda e: e.dma_start(out=out, in_=in_), writes=[b_const], chan=chan)

    cdma(SP, ident, cf_d)
    cdma(SP, wsf, wsf_d)
    for i in range(4):
        cdma(SP, lnp[:, i, :], lnp_d[i, :].partition_broadcast(128))
    c_const2 = Chan(S_, "const2")
    b_const2 = nb("const2")
    S_.op(POOL, lambda e: e.dma_start(out=wsb, in_=wsb_d), writes=[b_const2], chan=c_const2)
    b_const.w = (c_const.key, c_const.cum, {})
    b_pm = nb("pm")
    nb("biszero")
    nb("junk0")
    nb("junk1")
    S_.op(DVE, lambda e: e.memset(bis, 0.0), writes=[B["biszero"], B["bis0"], B["bis1"]])
    for gi_ in range(16):
        S_.op(DVE, (lambda gi_: lambda e: e.memset(bisc[:, gi_:gi_ + 1], float(128 * (gi_ + 1) - 511)))(gi_),
              writes=[B["biszero"]])
    S_.op(DVE, lambda e: e.tensor_copy(out=idb, in_=ident[:, 0:128]), reads=[b_const], writes=[B["idb"]])
    S_.op(DVE, lambda e: e.tensor_copy(out=pmb, in_=ident[:, 194:450]), reads=[b_const], writes=[b_pm])
    for n in range(NTILES):
        ch = c_cast[min(3, n // 15)]
        S_.op(POOL, (lambda n: lambda e: e.dma_start(
            out=wbf_d[:, n * TILE:(n + 1) * TILE].rearrange("p (a n) -> p a n", a=2),
            in_=wall_d[:, n * TILE:(n + 1) * TILE].rearrange("p (a n) -> p a n", a=2)))(n),
              writes=[b_wbf[n]], chan=ch)
    for n in range(NTILES):
        ch = c_cast[min(3, n // 15)]
        b_wbf[n].w = (ch.key, ch.cum, {})
    S_.op(POOL, lambda e: e.memset(pa, 0.0), writes=[B["pa"]])
    S_.op(POOL, lambda e: e.memset(pb_, 0.0), writes=[B["pb"]])

    seq_tiles = order if order is not None else []
    total_uses = len(seq_tiles)
    ws = {"issued": 0, "next": 0, "rec": []}

    def w_issue_upto(k):
        while ws["issued"] < min(k, total_uses):
            u = ws["issued"]
            n = seq_tiles[u]
            slot = u % RING
            S_.op(SP, (lambda n, slot: lambda e: e.dma_start(out=ring[slot],
                                                               in_=wbf_d[:, n * TILE:(n + 1) * TILE]))(n, slot),
                  reads=[b_wbf[n]], writes=[b_ring[slot]], chan=c_ring[slot])
            ws["issued"] += 1

    def wget(tile):
        if order is None:
            ws["rec"].append(tile)
            return ring[0], b_ring[0]
        u = ws["next"]
        assert seq_tiles[u] == tile, (u, tile, seq_tiles[u])
        w_issue_upto(u + RING - 2)
        ws["next"] += 1
        slot = u % RING
        return ring[slot], b_ring[slot]

    def load_x(seq, j, tt, slot):
        r0 = j * NB + tt * 128
        S_.op(SP, lambda e: e.dma_start(out=xt[slot], in_=x_d[seq, r0:r0 + 128, :]),
              writes=[b_xt[slot]], chan=c_xt[slot])

    wuv = wsb[:, 0:512]
    wpool = wsb[:, 512:1024].rearrange("p (g d) -> p g d", g=4)
    widxW = wsb[:, 1024:1088].rearrange("p (k n) -> p k n", k=8)
    pscale = wsf[:, 0:4]
    convw = wsf[:, 4:136].rearrange("p (c k) -> p c k", k=3)
    convb = wsf[:, 136:180]
    idf = ident[:, 0:128]
    fSq = ident[:, 128:129]
    fSi = ident[:, 129:130]
    invc0 = ident[:, 130:194].rearrange("p (g t) -> p g t", g=4)
    invc0w = ident[:, 450:514].rearrange("p (g t) -> p g t", g=4)
    Pmq = pmb[:, 0:128]
    Pmi = pmb[:, 128:256]

    dbg_n = [0]

    def tap(ap_sb, buf, ncols, eng=SP):
        if not dbg:
            return
        i = dbg_n[0]
        dbg_n[0] += 1
        ch = Chan(S_, "dbg%d" % i)
        S_.op(eng, lambda e: e.dma_start(out=dbg_d[i, :, 0:ncols], in_=ap_sb), reads=[buf], chan=ch)
        taps.append(ch)

    taps = []

    def rope_tables(seq, j, sin_ops):
        t0 = j * NB
        S_.op(SP, lambda e: e.dma_start(out=posi, in_=pos_d[seq, t0:t0 + NB].partition_broadcast(128)),
              writes=[B["Ibuf1"]], chan=c_pos)
        S_.op(DVE, lambda e: e.tensor_copy(out=posf, in_=posi), reads=[B["Ibuf1"]], writes=[B["posf"]])
        for (fS, ct, st, cn, sn) in ((fSq, cosq, sinq, "cosq", "sinq"), (fSi, cosi, sini, "cosi", "sini")):
            a0, a1, a2, a3 = tr
            ba = [B["tr0"], B["tr1"], B["tr2"], B["tr3"]]
            S_.op(DVE, lambda e, fS=fS: e.tensor_scalar(out=a0, in0=posf, scalar1=fS, scalar2=None,
                                                        op0=ALU.mult),
                  reads=[B["posf"], b_const], writes=[ba[0]])
            S_.op(DVE, lambda e: e.tensor_scalar(out=a1, in0=a0, scalar1=float(1.0 / TWO_PI), scalar2=0.5,
                                                 op0=ALU.mult, op1=ALU.add), reads=[ba[0]], writes=[ba[1]])
            S_.op(DVE, lambda e: e.tensor_copy(out=tki, in_=a1), reads=[ba[1]], writes=[B["tki"]])
            S_.op(DVE, lambda e: e.tensor_copy(out=a1, in_=tki), reads=[B["tki"]], writes=[ba[1]])
            S_.op(DVE, lambda e: e.scalar_tensor_tensor(out=a2, in0=a1, scalar=-C1, in1=a0,
                                                        op0=ALU.mult, op1=ALU.add),
                  reads=[ba[1], ba[0]], writes=[ba[2]])
            S_.op(DVE, lambda e: e.scalar_tensor_tensor(out=a0, in0=a1, scalar=-C2, in1=a2,
                                                        op0=ALU.mult, op1=ALU.add),
                  reads=[ba[1], ba[2]], writes=[ba[0]])
            S_.op(DVE, lambda e: e.tensor_scalar(out=a1, in0=a0, scalar1=float(-np.pi), scalar2=float(TWO_PI),
                                                 op0=ALU.is_lt, op1=ALU.mult), reads=[ba[0]], writes=[ba[1]])
            S_.op(DVE, lambda e: e.tensor_tensor(out=a0, in0=a0, in1=a1, op=ALU.add),
                  reads=[ba[0], ba[1]], writes=[ba[0]])
            S_.op(DVE, lambda e: e.tensor_scalar(out=a2, in0=a0, scalar1=float(np.pi / 2), scalar2=None,
                                                 op0=ALU.add), reads=[ba[0]], writes=[ba[2]])
            S_.op(DVE, lambda e: e.tensor_scalar(out=a1, in0=a2, scalar1=float(np.pi), scalar2=float(-TWO_PI),
                                                 op0=ALU.is_gt, op1=ALU.mult), reads=[ba[2]], writes=[ba[1]])
            S_.op(DVE, lambda e: e.tensor_tensor(out=a2, in0=a2, in1=a1, op=ALU.add),
                  reads=[ba[2], ba[1]], writes=[ba[2]])
            S_.op(DVE, lambda e: e.tensor_scalar(out=a0, in0=a0, scalar1=-PI_SAFE, scalar2=PI_SAFE,
                                                 op0=ALU.max, op1=ALU.min), reads=[ba[0]], writes=[ba[0]])
            S_.op(DVE, lambda e: e.tensor_scalar(out=a2, in0=a2, scalar1=-PI_SAFE, scalar2=PI_SAFE,
                                                 op0=ALU.max, op1=ALU.min), reads=[ba[2]], writes=[ba[2]])
            k0 = 0 if fS is fSq else 1
            S_.op(DVE, lambda e, k0=k0: e.tensor_copy(out=sarg[2 * k0], in_=a0), reads=[ba[0]], writes=[B["sarg"]])
            S_.op(DVE, lambda e, k0=k0: e.tensor_copy(out=sarg[2 * k0 + 1], in_=a2), reads=[ba[2]], writes=[B["sarg"]])
            sin_ops.append((st, sn, 2 * k0))
            sin_ops.append((ct, cn, 2 * k0 + 1))

    def rope_sins(sin_ops):
        for (dst, name, k) in sin_ops:
            S_.op(ACT, lambda e, dst=dst, k=k: e.activation(out=dst, in_=sarg[k], func=AF.Sin),
                  reads=[B["sarg"]], writes=[B[name]])


    def x_transposes(xslots):
        for tt in range(2):
            xs = xslots[tt]
            for g4 in range(2):
                wbk, bwb = next_wb(ALLB)
                for q in range(4):
                    fc = 4 * g4 + q
                    S_.op(PE, lambda e, wbk=wbk, q=q, fc=fc, xs=xs: e.transpose(
                        out=wbk[:, q * 128:(q + 1) * 128], in_=xt[xs][:, fc * 128:(fc + 1) * 128], identity=idf),
                        reads=[b_xt[xs], b_const], writes=[bwb])
                S_.op(ACT, lambda e, wbk=wbk, g4=g4, tt=tt: e.activation(
                    out=xT[:, 4 * g4:4 * g4 + 4, tt * 128:(tt + 1) * 128],
                    in_=wbk.rearrange("p (q t) -> p q t", q=4), func=AF.Copy),
                    reads=[bwb], writes=[B["xT"]])

    def p_gen(seq, j):
        t0 = j * NB
        first = (j == 0)
        tabs = {"q": (Pmq, cosq, sinq, "cosq", "sinq"), "i": (Pmi, cosi, sini, "cosi", "sini")}
        jobs = []
        for i in range(2):
            for q in range(2):
                jobs.append((7 + i, 128 * q, "pool", 2 * i + q))
        for i in range(4):
            for q in range(2):
                jobs.append((i, 128 * q, "rope", (qT[:, 2 * i + q, :], [B["qT"]], "q", None)))
        jobs.append((4, 0, "rope", (kT[:, t0:t0 + NB], [b_kT[2 * j], b_kT[2 * j + 1]], "q", (ckvp, B["ckvp"]))))
        jobs.append((4, 128, "rope", (kiT[:, t0:t0 + NB], [b_kiT[2 * j], b_kiT[2 * j + 1]], "i", None)))
        for i in range(2):
            for q in range(2):
                jobs.append((5 + i, 128 * q, "rope", (qiT[:, 2 * i + q, :], [B["qiT"]], "i", None)))

        def part2(ctx):
            (wbk, bwb, k, dst, bdst, tk) = ctx
            Pm, ct, st, cn, sn = tabs[tk]
            i = k % 2
            pbf, bpbf = prebf[i], B["prebf%d" % i]
            wb2, bwb2 = next_wb(ALLB)
            S_.op(PE, lambda e: e.matmul(wb2[:, 0:NB], lhsT=Pm, rhs=pbf, start=True, stop=True),
                  reads=[bpbf, b_pm], writes=[bwb2])
            S_.op(DVE, lambda e: e.tensor_tensor(out=rt1[i], in0=wbk[:, 0:NB], in1=ct, op=ALU.mult),
                  reads=[bwb, B[cn]], writes=[B["rt1_%d" % i]])
            S_.op(DVE, lambda e: e.tensor_tensor(out=rt2[i], in0=wb2[:, 0:NB], in1=st, op=ALU.mult),
                  reads=[bwb2, B[sn]], writes=[B["rt2_%d" % i]])
            eng = POOL if k % 2 == 0 else DVE
            S_.op(eng, lambda e: e.tensor_tensor(out=dst, in0=rt1[i], in1=rt2[i], op=ALU.add),
                  reads=[B["rt1_%d" % i], B["rt2_%d" % i]], writes=bdst)

        def pool_adds(g):
            w = 2 ** (g + 1)
            bu = B["ubuf%d" % g]
            src, bsrc = ubuf[g], bu
            L = 16 + NB
            sh = 1
            kk = 0
            while sh < w:
                dst, bd = [(pa, B["pa"]), (pb_, B["pb"])][kk % 2]
                kk += 1
                S_.op(POOL, lambda e, dst=dst, src=src, sh=sh: e.tensor_tensor(
                    out=dst[:, sh:L], in0=src[:, sh:L], in1=src[:, 0:L - sh], op=ALU.add),
                    reads=[bsrc], writes=[bd])
                src, bsrc = dst, bd
                sh *= 2
            pl, bpl = pooled[g], B["pooled%d" % g]
            if first:
                S_.op(POOL, lambda e, src=src: e.tensor_tensor(out=src[:, 16:32], in0=src[:, 16:32],
                                                               in1=invc0w[:, g, :], op=ALU.mult),
                      reads=[bsrc, b_const], writes=[bsrc])
            S_.op(POOL, lambda e, src=src: e.tensor_scalar(out=src[:, 16:L], in0=src[:, 16:L],
                                                           scalar1=float(1.0 / w), scalar2=0.0,
                                                           op0=ALU.mult, op1=ALU.add),
                  reads=[bsrc], writes=[bsrc])
            S_.op(POOL, lambda e, src=src: e.tensor_tensor(out=pl, in0=src[:, 16:L], in1=ubuf[g][:, 16:L],
                                                           op=ALU.subtract),
                  reads=[bsrc, bu], writes=[bpl])
            S_.op(POOL, lambda e: e.tensor_copy(out=ubuf[g][:, 0:16], in_=ubuf[g][:, NB:NB + 16]),
                  reads=[bu, bsrc], writes=[bu])

        def pool_mm(g):
            pl, bpl = pooled[g], B["pooled%d" % g]
            wbk, bwb = next_wb(ALLB)
            S_.op(PE, lambda e: e.matmul(wbk[:, 0:NB], lhsT=wpool[:, g, :], rhs=pl, start=True, stop=True),
                  reads=[bpl, b_const2], writes=[bwb])
            S_.op(ACT, lambda e: e.activation(out=pmix[:, g, :], in_=wbk[:, 0:NB], func=AF.Identity,
                                              scale=pscale[:, g:g + 1]),
                  reads=[bwb, b_const], writes=[B["pmix"]])

        for tt in range(2):
            wbk, bwb = next_wb(ALLB)
            for kc in range(8):
                S_.op(PE, lambda e, kc=kc, tt=tt, wbk=wbk: e.matmul(
                    wbk[:, 0:8], lhsT=xT[:, kc, tt * 128:(tt + 1) * 128], rhs=widxW[:, kc, :],
                    start=(kc == 0), stop=(kc == 7)), reads=[B["xT"], b_const2], writes=[bwb])
            S_.op(ACT, lambda e, tt=tt, wbk=wbk: e.activation(out=widx[:, tt, :], in_=wbk[:, 0:8], func=AF.Copy,
                                                              scale=float(8.0 ** -0.5 * 64.0 ** -0.5)),
                  reads=[bwb], writes=[B["widx"]])
        for tt in range(2):
            for h in range(H):
                S_.op(DVE, lambda e, h=h, tt=tt: e.tensor_scalar(out=dgw[:, tt, h, :], in0=idf,
                                                                 scalar1=widx[:, tt, h:h + 1], scalar2=None,
                                                                 op0=ALU.mult),
                      reads=[B["widx"], b_const], writes=[B["dgw%d" % tt]])
        cur_tile, wt, bwt = None, None, None
        pend = None
        k = 0
        pool_sched = {}
        for ji, (tile, coff, kind, arg) in enumerate(jobs):
            if tile != cur_tile:
                wt, bwt = wget(tile)
                cur_tile = tile
            wbk, bwb = next_wb(ALLB)
            wt3 = wt.rearrange("p (k n) -> p k n", k=8)
            for kc in range(8):
                S_.op(PE, lambda e, kc=kc, wbk=wbk, wt3=wt3, coff=coff: e.matmul(
                    wbk[:, 0:NB], lhsT=wt3[:, kc, coff:coff + 128], rhs=xT[:, kc, :],
                    start=(kc == 0), stop=(kc == 7)), reads=[bwt, B["xT"]], writes=[bwb])
            if kind == "rope":
                dst, bdst, tk, keep = arg
                i = k % 2
                pbf, bpbf = prebf[i], B["prebf%d" % i]
                S_.op(ACT, lambda e, pbf=pbf, wbk=wbk: e.activation(out=pbf, in_=wbk[:, 0:NB], func=AF.Copy),
                      reads=[bwb], writes=[bpbf])
                if keep is not None:
                    S_.op(POOL, lambda e, pbf=pbf, keep=keep: e.tensor_copy(out=keep[0], in_=pbf), reads=[bpbf],
                          writes=[keep[1]])
                if pend is not None:
                    part2(pend)
                pend = (wbk, bwb, k, dst, bdst, tk)
                k += 1
            else:
                g = arg
                if first:
                    S_.op(POOL, lambda e, g=g: e.memset(ubuf[g][:, 0:16], 0.0), writes=[B["ubuf%d" % g]])
                S_.op(ACT, lambda e, wbk=wbk, g=g: e.activation(out=ubuf[g][:, 16:16 + NB], in_=wbk[:, 0:NB],
                                                                func=AF.Copy),
                      reads=[bwb], writes=[B["ubuf%d" % g]])
                pool_sched.setdefault(ji + 1, []).append((pool_adds, g))
                pool_sched.setdefault(14 + g, []).append((pool_mm, g))
            for (fn_, g_) in pool_sched.pop(ji, []):
                fn_(g_)
            yield
        if pend is not None:
            part2(pend)
        for ji in sorted(pool_sched):
            for (fn_, g_) in pool_sched[ji]:
                fn_(g_)
        for tt in range(2):
            gi = 2 * j + tt
            wbk, bwb = next_wb(ALLB)
            S_.op(PE, lambda e, tt=tt, wbk=wbk: e.matmul(wbk[:, 0:512], lhsT=ckvp[:, tt * 128:(tt + 1) * 128],
                                                         rhs=wuv, start=True, stop=True),
                  reads=[B["ckvp"], b_const2], writes=[bwb])
            v3 = vaug[:, gi, :].rearrange("p (h e) -> p h e", e=65)
            S_.op(ACT, lambda e, wbk=wbk, v3=v3: e.activation(out=v3[:, :, 0:64],
                                                              in_=wbk.rearrange("p (h e) -> p h e", e=64),
                                                              func=AF.Copy),
                  reads=[bwb], writes=[b_v[gi]])
            S_.op(POOL, lambda e, v3=v3: e.memset(v3[:, :, 64:65], 1.0), writes=[b_v[gi]])
        yield

    def indexer(j, tt):
        gi = 2 * j + tt
        nkb = gi + 1
        Nk = 128 * nkb
        Ib, bI = Ibuf[tt], B["Ibuf%d" % tt]
        nks = (Nk + 511) // 512
        kread = [b_kiT[i] for i in range(nkb)]
        bdg = B["dgw%d" % tt]
        ISK = 3
        for ks in range(nks):
            cols = min(512, Nk - 512 * ks)
            bbk, bbb = next_wb(only=(4, 5))
            pend = []

            def acc(ctx, bbk=bbk, bbb=bbb, cols=cols):
                (h, R, bR) = ctx
                S_.op(PE, lambda e: e.matmul(bbk[:, 0:cols], lhsT=dgw[:, tt, h, :], rhs=R[:, 0:cols],
                                             start=(h == 0), stop=(h == H - 1)),
                      reads=[bdg, bR, B["Ebuf0"]], writes=[bbb])

            for h in range(H):
                m, base = h // 2, 64 * (h % 2)
                wbk, bwb = next_wb()
                S_.op(PE, lambda e, wbk=wbk, m=m, base=base, ks=ks, cols=cols: e.matmul(
                    wbk[:, 0:cols], lhsT=qiT[base:base + 64, m, tt * 128:(tt + 1) * 128],
                    rhs=kiT[base:base + 64, 512 * ks:512 * ks + cols], start=True, stop=True),
                    reads=[B["qiT"]] + kread, writes=[bwb])
                ri = ist["ri"]
                ist["ri"] += 1
                R, bR = Rb[ri % 8], B["Rb%d" % (ri % 8)]
                wr = [bR] + ([B["Ebuf0"]] if ist["fresh"] else [])
                rd = [bwb] + ([] if ist["fresh"] else [B["Ebuf0"]])
                ist["fresh"] = False
                if h % 2 == 0:
                    S_.op(ACT, lambda e, wbk=wbk, R=R, cols=cols: e.activation(
                        out=R[:, 0:cols], in_=wbk[:, 0:cols], func=AF.Relu), reads=rd, writes=wr)
                else:
                    S_.op(DVE, lambda e, wbk=wbk, R=R, cols=cols: e.tensor_scalar(
                        out=R[:, 0:cols], in0=wbk[:, 0:cols], scalar1=0.0, scalar2=None, op0=ALU.max),
                        reads=rd, writes=wr)
                pend.append((h, R, bR))
                if len(pend) > ISK:
                    acc(pend.pop(0))
            while pend:
                acc(pend.pop(0))
            S_.op(ACT, lambda e, ks=ks, cols=cols, bbk=bbk: e.activation(
                out=Ib[:, 512 * ks:512 * ks + cols], in_=bbk[:, 0:cols], func=AF.Copy),
                reads=[bbb], writes=[bI])
            yield
        S_.op(DVE, lambda e, Ib=Ib, Nk=Nk: e.memset(Ib[0:64, Nk - 64:Nk], -1e30), reads=[bI], writes=[bI])
        yield

    def bisect_gen(j, tt):
        gi = 2 * j + tt
        Nk = 128 * (gi + 1)
        Ib, bI = Ibuf[tt], B["Ibuf%d" % tt]
        bb = B["bis%d" % tt]
        mk, bmk = maskb2[tt], B["maskb%d" % tt]
        bj = B["junk%d" % tt]
        jk = junk[:, 2048 * tt:2048 * tt + 2048]
        if tt == 0:
            cnt, dd, mid = bis[:, 0:1], bis[:, 1:2], bis[:, 2:3]
            if gi < 2:
                S_.op(DVE, lambda e: e.memset(mid, -1e29), writes=[bb])
            else:
                S_.op(DVE, lambda e: e.memset(mid, 0.0), writes=[bb, bj, B["Ebuf1"]])
                step = BSTEP0
                for it in range(NBIS):
                    S_.op(DVE, lambda e: e.tensor_scalar(
                        out=jk[:, 0:Nk], in0=Ib[:, 0:Nk], scalar1=mid, scalar2=None, op0=ALU.is_ge,
                        op1=ALU.add, accum_out=cnt), reads=[bI, bb], writes=[bj, bb])
                    S_.op(DVE, lambda e, step=step: e.tensor_scalar(
                        out=dd, in0=cnt, scalar1=float(TOPK) - 0.5, scalar2=2.0 * step, op0=ALU.is_ge,
                        op1=ALU.mult), reads=[bb], writes=[bb])
                    S_.op(DVE, lambda e, step=step: e.scalar_tensor_tensor(
                        out=mid, in0=dd, scalar=-step, in1=mid, op0=ALU.add, op1=ALU.add),
                        reads=[bb], writes=[bb])
                    step *= 0.5
                    yield
            S_.op(DVE, lambda e: e.tensor_scalar(out=mk[:, 0:Nk], in0=Ib[:, 0:Nk], scalar1=mid,
                                                 scalar2=None, op0=ALU.is_ge),
                  reads=[bI, bb, bj], writes=[bmk] + ([B["Ebuf1"]] if gi >= 2 else []))
        else:
            sacc, dd, mid = bis[:, 4:5], bis[:, 5:6], bis[:, 6:7]
            ng = [bis[:, 8:9], bis[:, 9:10]]
            if gi < 2:
                S_.op(DVE, lambda e: e.memset(mid, -1e29), writes=[bb])
            else:
                S_.op(ACT, lambda e: e.activation(out=ng[0], in_=bis[:, 7:8], func=AF.Copy, scale=0.0),
                      reads=[B["biszero"]], writes=[bb, bj, B["Ebuf1"]])
                step = BSTEP0
                for it in range(NBIS):
                    n0, n1 = ng[it % 2], ng[(it + 1) % 2]
                    S_.op(ACT, lambda e, n0=n0: e.activation(
                        out=jk[:, 0:Nk], in_=Ib[:, 0:Nk], func=AF.Sign, bias=n0, scale=1.0, accum_out=sacc),
                        reads=[bI, bb], writes=[bj, bb])
                    S_.op(ACT, lambda e: e.activation(out=dd, in_=sacc, func=AF.Sign, bias=bisc[:, gi:gi + 1],
                                                      scale=1.0),
                          reads=[bb, B["biszero"]], writes=[bb])
                    S_.op(ACT, lambda e, n0=n0, n1=n1, step=step: e.activation(
                        out=n1, in_=dd, func=AF.Identity, scale=-step, bias=n0), reads=[bb], writes=[bb])
                    step *= 0.5
                    yield
                nf = ng[NBIS % 2]
                S_.op(ACT, lambda e, nf=nf: e.activation(out=mid, in_=nf, func=AF.Copy, scale=-1.0),
                      reads=[bb], writes=[bb])
            S_.op(DVE, lambda e: e.tensor_scalar(out=mk[:, 0:Nk], in0=Ib[:, 0:Nk], scalar1=mid,
                                                 scalar2=None, op0=ALU.is_ge),
                  reads=[bI, bb, bj], writes=[bmk] + ([B["Ebuf1"]] if gi >= 2 else []))
        yield

    AB = (2, 3)

    def mask_prep(j, tt):
        gi = 2 * j + tt
        nkb = gi + 1
        mk, bmk = maskb2[tt], B["maskb%d" % tt]
        for k8 in range(0, nkb, 8):
            nq = min(8, nkb - k8)
            wbk, bwb = next_wb(AB)
            wbb = wbk.bitcast(BF16)
            for q in range(nq):
                kb = k8 + q
                S_.op(PE, lambda e, wbb=wbb, q=q, kb=kb: e.transpose(
                    out=wbb[:, q * 128:(q + 1) * 128], in_=mk[:, kb * 128:(kb + 1) * 128], identity=idb),
                    reads=[bmk, B["idb"]], writes=[bwb])
            S_.op(ACT, lambda e, wbb=wbb, k8=k8, nq=nq: e.activation(
                out=maskT[:, k8:k8 + nq, :], in_=wbb[:, 0:128 * nq].rearrange("p (q t) -> p q t", q=nq),
                func=AF.Identity, scale=30000.0, bias=-30000.0), reads=[bwb], writes=[B["maskT"]])

    def attention(j, tt, after_last_qk=None):
        gi = 2 * j + tt
        nkb = gi + 1
        kTr = [b_kT[i] for i in range(nkb)]
        vr = [b_v[i] for i in range(nkb)]

        def pv(hp):
            Eb, bE = Ebuf2[hp % 2], B["Ebuf%d" % (hp % 2)]
            for hh in range(2):
                h = 2 * hp + hh
                ob, bob = abank[h // 4], b_ab[h // 4]
                for kb in range(nkb):
                    S_.op(PE, lambda e, ob=ob, h=h, hh=hh, kb=kb, Eb=Eb: e.matmul(
                        ob[:, (h % 4) * 65:(h % 4) * 65 + 65], lhsT=Eb[:, kb, hh * 128:(hh + 1) * 128],
                        rhs=vaug[:, kb, h * 65:(h + 1) * 65], start=(kb == 0), stop=(kb == nkb - 1)),
                        reads=[bE] + vr, writes=[bob])

        pend = None
        for hp in range(4):
            Eb, bE = Ebuf2[hp % 2], B["Ebuf%d" % (hp % 2)]
            for k2 in range(0, nkb, 2):
                nq = min(2, nkb - k2)
                wbk, bwb = next_wb(AB)
                for q in range(nq):
                    kb = k2 + q
                    o3 = wbk[:, q * 256:(q + 1) * 256].rearrange("p (a t) -> p a t", a=2)
                    S_.op(PE, lambda e, o3=o3, kb=kb, hp=hp: e.matmul(
                        o3, lhsT=kT[:, kb * 128:(kb + 1) * 128],
                        rhs=qT[:, 2 * hp:2 * hp + 2, tt * 128:(tt + 1) * 128], start=True, stop=False),
                        reads=[B["qT"]] + kTr, writes=[bwb])
                    S_.op(PE, lambda e, o3=o3, kb=kb: e.matmul(
                        o3, lhsT=idb, rhs=maskT[:, kb:kb + 1, :].to_broadcast([128, 2, 128]),
                        start=False, stop=True), reads=[B["maskT"], B["idb"]], writes=[bwb])
                S_.op(ACT, lambda e, wbk=wbk, k2=k2, nq=nq, Eb=Eb: e.activation(
                    out=Eb[:, k2:k2 + nq, :], in_=wbk[:, 0:256 * nq].rearrange("p (q n) -> p q n", q=nq),
                    func=AF.Exp, scale=float(128.0 ** -0.5)), reads=[bwb], writes=[bE])
            if hp == 3 and after_last_qk is not None:
                after_last_qk()
            if pend is not None:
                pv(pend)
            pend = hp
        pv(pend)
        for half in range(2):
            ob, bob = abank[half], b_ab[half]
            o3 = ob[:, 0:260].rearrange("p (h e) -> p h e", e=65)
            S_.op(DVE, lambda e, o3=o3, half=half: e.reciprocal(
                out=rden[:, 4 * half:4 * half + 4].unsqueeze(2), in_=o3[:, :, 64:65]),
                reads=[bob], writes=[B["rden"]])
            S_.op(DVE, lambda e, o3=o3, half=half: e.tensor_tensor(
                out=attok[:, 256 * half:256 * half + 256].rearrange("p (h e) -> p h e", e=64),
                in0=o3[:, :, 0:64],
                in1=rden[:, 4 * half:4 * half + 4].unsqueeze(2).to_broadcast([128, 4, 64]), op=ALU.mult),
                reads=[bob, B["rden"]], writes=[B["attok"]])
        wbk, bwb = next_wb(AB)
        wbb = wbk.bitcast(BF16)
        for q in range(4):
            S_.op(PE, lambda e, wbb=wbb, q=q: e.transpose(out=wbb[:, q * 128:(q + 1) * 128],
                                                          in_=attok[:, q * 128:(q + 1) * 128], identity=idb),
                  reads=[B["attok"], B["idb"]], writes=[bwb])
        S_.op(ACT, lambda e, wbb=wbb: e.activation(
            out=attnT[:, :, tt * 128:(tt + 1) * 128], in_=wbb[:, 0:512].rearrange("p (q t) -> p q t", q=4),
            func=AF.Copy), reads=[bwb], writes=[B["attnT"]])

    def layernorm(src, bsrc, dst, bdst, gi_, bi_, aff=POOL):
        for half in range(2):
            S_.op(DVE, lambda e, half=half: e.bn_stats(out=lnst[:, 6 * half:6 * half + 6],
                                                       in_=src[:, 512 * half:512 * half + 512]),
                  reads=[bsrc], writes=[B["lnst"]])
        S_.op(DVE, lambda e: e.bn_aggr(out=lnmv[:, 0:2], in_=lnst), reads=[B["lnst"]], writes=[B["lnmv"]])
        S_.op(DVE, lambda e: e.tensor_scalar(out=lnmv[:, 2:3], in0=lnmv[:, 1:2], scalar1=LN_EPS, scalar2=None,
                                             op0=ALU.add), reads=[B["lnmv"]], writes=[B["lnmv"]])
        S_.op(ACT, lambda e: e.activation(out=lnmv[:, 2:3], in_=lnmv[:, 2:3], func=AF.Sqrt),
              reads=[B["lnmv"]], writes=[B["lnmv"]])
        S_.op(DVE, lambda e: e.reciprocal(out=lnmv[:, 2:3], in_=lnmv[:, 2:3]), reads=[B["lnmv"]],
              writes=[B["lnmv"]])
        S_.op(DVE, lambda e: e.scalar_tensor_tensor(out=lnmv[:, 3:4], in0=lnmv[:, 0:1], scalar=-1.0,
                                                    in1=lnmv[:, 2:3], op0=ALU.mult, op1=ALU.mult),
              reads=[B["lnmv"]], writes=[B["lnmv"]])
        S_.op(ACT, lambda e: e.activation(out=dst, in_=src, func=AF.Identity, scale=lnmv[:, 2:3],
                                          bias=lnmv[:, 3:4]), reads=[bsrc, B["lnmv"]], writes=[bdst])
        S_.op(aff, lambda e: e.tensor_tensor(out=dst, in0=dst, in1=lnp[:, gi_, :], op=ALU.mult),
              reads=[bdst, b_const], writes=[bdst])
        S_.op(aff, lambda e: e.tensor_tensor(out=dst, in0=dst, in1=lnp[:, bi_, :], op=ALU.add),
              reads=[bdst, b_const], writes=[bdst])

    def c_phase(seq, j, xslots):
        def outproj(ctx):
            fc, i2, wo, bwm = ctx
            for tt in range(2):
                for half in range(2):
                    ai = 2 * tt + half
                    S_.op(PE, lambda e, ai=ai, tt=tt, half=half: e.matmul(
                        abank[ai][:, 0:512], lhsT=Gfc[i2][:, tt * 128:(tt + 1) * 128],
                        rhs=wo[:, half * 512:(half + 1) * 512], start=(fc == 0), stop=(fc == 7)),
                        reads=[B["Gfc%d" % i2], bwm], writes=[b_ab[ai]])

        pend = None
        for fc in range(8):
            wg, bwg = wget(9 + 2 * fc)
            wm, bwm = wget(10 + 2 * fc)
            wg3 = wg.rearrange("p (k n) -> p k n", k=8)
            wab3 = wm[:, 0:512].rearrange("p (k n) -> p k n", k=4)
            wpb3 = wm[:, 512:1024].rearrange("p (k n) -> p k n", k=4)
            wo = wm[:, 1024:2048]
            i2 = fc % 2
            bkA, bbA = next_wb()
            bkB, bbB = next_wb()
            for a in range(2):
                for kc in range(8):
                    S_.op(PE, lambda e, a=a, kc=kc, bkA=bkA, wg3=wg3: e.matmul(
                        bkA[:, a * NB:(a + 1) * NB], lhsT=wg3[:, kc, a * 128:(a + 1) * 128], rhs=xT[:, kc, :],
                        start=(kc == 0), stop=(kc == 7)), reads=[bwg, B["xT"]], writes=[bbA])
            for kc in range(4):
                S_.op(PE, lambda e, kc=kc, bkB=bkB, wab3=wab3: e.matmul(
                    bkB[:, 0:NB], lhsT=wab3[:, kc, :], rhs=attnT[:, kc, :], start=(kc == 0), stop=(kc == 3)),
                    reads=[bwm, B["attnT"]], writes=[bbB])
            for kc in range(4):
                S_.op(PE, lambda e, kc=kc, bkB=bkB, wpb3=wpb3: e.matmul(
                    bkB[:, NB:2 * NB], lhsT=wpb3[:, kc, :], rhs=pmix[:, kc, :], start=(kc == 0), stop=(kc == 3)),
                    reads=[bwm, B["pmix"]], writes=[bbB])
            S_.op(ACT, lambda e, bkA=bkA, i2=i2: e.activation(out=sgA[i2], in_=bkA[:, 0:NB], func=AF.Sigmoid),
                  reads=[bbA], writes=[B["sgA%d" % i2]])
            S_.op(ACT, lambda e, bkA=bkA, i2=i2: e.activation(out=sgB[i2], in_=bkA[:, NB:2 * NB], func=AF.Sigmoid),
                  reads=[bbA], writes=[B["sgB%d" % i2]])
            S_.op(DVE, lambda e, bkB=bkB, i2=i2: e.tensor_tensor(out=g1[i2], in0=bkB[:, 0:NB], in1=sgA[i2],
                                                                 op=ALU.mult),
                  reads=[bbB, B["sgA%d" % i2]], writes=[B["g1_%d" % i2]])
            S_.op(DVE, lambda e, bkB=bkB, i2=i2: e.tensor_tensor(out=g2[i2], in0=bkB[:, NB:2 * NB], in1=sgB[i2],
                                                                 op=ALU.mult),
                  reads=[bbB, B["sgB%d" % i2]], writes=[B["g2_%d" % i2]])
            S_.op(POOL, lambda e, i2=i2: e.tensor_tensor(out=Gfc[i2], in0=g1[i2], in1=g2[i2], op=ALU.add),
                  reads=[B["g1_%d" % i2], B["g2_%d" % i2]], writes=[B["Gfc%d" % i2]])
            if pend is not None:
                outproj(pend)
            pend = (fc, i2, wo, bwm)
        outproj(pend)
        for tt in range(2):
            xs = xslots[tt]
            for half in range(2):
                ai = 2 * tt + half
                S_.op(DVE, lambda e, ai=ai, tt=tt, half=half, xs=xs: e.scalar_tensor_tensor(
                    out=pre[tt][:, 512 * half:512 * half + 512], in0=xt[xs][:, 512 * half:512 * half + 512],
                    scalar=float(ALPHA), in1=abank[ai][:, 0:512], op0=ALU.mult, op1=ALU.add),
                    reads=[b_xt[xs], b_ab[ai]], writes=[B["pre%d" % tt]])

    def c_tail(seq, j):
        for tt in range(2):
            layernorm(pre[tt], B["pre%d" % tt], pre[tt], B["pre%d" % tt], 0, 1, aff=POOL)
            yield
            for g4 in range(2):
                wbk, bwb = next_wb(only=(6, 7))
                for q in range(4):
                    fc = 4 * g4 + q
                    S_.op(PE, lambda e, wbk=wbk, q=q, fc=fc, tt=tt: e.transpose(
                        out=wbk[:, q * 128:(q + 1) * 128], in_=pre[tt][:, fc * 128:(fc + 1) * 128], identity=idf),
                        reads=[B["pre%d" % tt], b_const], writes=[bwb])
                S_.op(ACT, lambda e, wbk=wbk, g4=g4, tt=tt: e.activation(
                    out=x1T[:, 4 * g4:4 * g4 + 4, tt * 128:(tt + 1) * 128],
                    in_=wbk.rearrange("p (q t) -> p q t", q=4), func=AF.Copy),
                    reads=[bwb], writes=[B["x1T"]])
            yield

    def ffn_gen(seq, j):
        t0 = j * NB
        first = (j == 0)
        st = {"wd": None, "bwd": None}

        def down(c):
            if c % 2 == 0:
                st["wd"], st["bwd"] = wget(25 + c + (c + 1) // 2 + 1)
            wd, bwd = st["wd"], st["bwd"]
            i4 = c % 4
            wdc = wd[:, (c % 2) * 1024:(c % 2) * 1024 + 1024]
            for tt in range(2):
                for half in range(2):
                    ai = 2 * tt + half
                    S_.op(PE, lambda e, ai=ai, tt=tt, half=half: e.matmul(
                        abank[ai][:, 0:512], lhsT=actc[i4][:, tt * 128:(tt + 1) * 128],
                        rhs=wdc[:, half * 512:(half + 1) * 512], start=(c == 0), stop=(c == NCH - 1)),
                        reads=[B["actc%d" % i4], bwd], writes=[b_ab[ai]])

        hbs = {}

        def s0(c):
            wu, bwu = wget(25 + c + (c + 1) // 2)
            wu3 = wu.rearrange("p (k n) -> p k n", k=8)
            hb, bhb = next_wb()
            hbs[c] = (hb, bhb)
            for a in range(2):
                for kc in range(8):
                    S_.op(PE, lambda e, a=a, kc=kc: e.matmul(
                        hb[:, a * NB:(a + 1) * NB], lhsT=wu3[:, kc, a * 128:(a + 1) * 128], rhs=x1T[:, kc, :],
                        start=(kc == 0), stop=(kc == 7)), reads=[bwu, B["x1T"]], writes=[bhb])

        def s1(c):
            hb, bhb = hbs.pop(c)
            i2 = c % 2
            hr, bhr = hraw[i2], B["hraw%d" % i2]
            if first:
                S_.op(POOL, lambda e: e.memset(halo[:, c, :, :], 0.0), writes=[B["halo"]])
            S_.op(ACT, lambda e: e.activation(out=hr[:, :, 0:2], in_=halo[:, c, :, :], func=AF.Copy),
                  reads=[B["halo"]], writes=[bhr])
            S_.op(ACT, lambda e: e.activation(out=hr[:, :, 2:2 + NB], in_=hb.rearrange("p (a n) -> p a n", a=2),
                                              func=AF.Copy), reads=[bhb], writes=[bhr])
            S_.op(ACT, lambda e: e.activation(out=halo[:, c, :, :], in_=hr[:, :, NB:NB + 2], func=AF.Copy),
                  reads=[bhr], writes=[B["halo"]])

        def s23(c):
            i2 = c % 2
            hr, bhr = hraw[i2], B["hraw%d" % i2]
            ac = aconv[i2]
            for a in range(2):
                cc = c + NCH * a
                bac = B["ac%d_%d" % (a, i2)]
                S_.op(POOL, lambda e, a=a, cc=cc: e.tensor_scalar(
                    out=ac[:, a, :], in0=hr[:, a, 2:2 + NB], scalar1=convw[:, cc, 2:3], scalar2=convb[:, cc:cc + 1],
                    op0=ALU.mult, op1=ALU.add), reads=[bhr, b_const], writes=[bac])
            ccv = c + NCH
            bacv = B["ac1_%d" % i2]
            S_.op(POOL, lambda e: e.tensor_scalar(out=ctmp, in0=hr[:, 1, 1:1 + NB], scalar1=convw[:, ccv, 1:2],
                                                  scalar2=0.0, op0=ALU.mult, op1=ALU.add),
                  reads=[bhr, b_const], writes=[B["ctmp"]])
            S_.op(POOL, lambda e: e.tensor_tensor(out=ac[:, 1, :], in0=ac[:, 1, :], in1=ctmp, op=ALU.add),
                  reads=[B["ctmp"], bacv], writes=[bacv])
            for a in range(2):
                cc = c + NCH * a
                bac = B["ac%d_%d" % (a, i2)]
                if a == 0:
                    S_.op(DVE, lambda e, a=a, cc=cc: e.scalar_tensor_tensor(
                        out=ac[:, a, :], in0=hr[:, a, 1:1 + NB], scalar=convw[:, cc, 1:2], in1=ac[:, a, :],
                        op0=ALU.mult, op1=ALU.add), reads=[bhr, b_const, bac], writes=[bac])
                S_.op(DVE, lambda e, a=a, cc=cc: e.scalar_tensor_tensor(
                    out=ac[:, a, :], in0=hr[:, a, 0:NB], scalar=convw[:, cc, 0:1], in1=ac[:, a, :],
                    op0=ALU.mult, op1=ALU.add), reads=[bhr, b_const, bac], writes=[bac])

        def s45(c):
            i2 = c % 2
            ac = aconv[i2]
            bg, bv = B["ac0_%d" % i2], B["ac1_%d" % i2]
            S_.op(ACT, lambda e: e.activation(out=ac[:, 0, :], in_=ac[:, 0, :], func=AF.Silu),
                  reads=[bg], writes=[bg])
            S_.op(POOL, lambda e: e.tensor_tensor(out=actc[c % 4], in0=ac[:, 0, :], in1=ac[:, 1, :], op=ALU.mult),
                  reads=[bg, bv], writes=[B["actc%d" % (c % 4)]])

        for i in range(NCH + 1 + FSKEW):
            if i < NCH:
                s0(i)
                s1(i)
                s23(i)
            if 0 <= i - 1 < NCH:
                s45(i - 1)
            if 0 <= i - 1 - FSKEW < NCH:
                down(i - 1 - FSKEW)
            yield
        for tt in range(2):
            for half in range(2):
                ai = 2 * tt + half
                S_.op(DVE, lambda e, ai=ai, tt=tt, half=half: e.scalar_tensor_tensor(
                    out=pre[tt][:, 512 * half:512 * half + 512], in0=pre[tt][:, 512 * half:512 * half + 512],
                    scalar=float(ALPHA), in1=abank[ai][:, 0:512], op0=ALU.mult, op1=ALU.add),
                    reads=[b_ab[ai]], writes=[B["pre%d" % tt]])
            layernorm(pre[tt], B["pre%d" % tt], pre[tt], B["pre%d" % tt], 2, 3)
            r0 = t0 + tt * 128
            S_.op(SP, lambda e, tt=tt, r0=r0: e.dma_start(out=out_d[seq, r0:r0 + 128, :], in_=pre[tt]),
                  reads=[B["pre%d" % tt]], chan=c_ot[tt])
        yield

    def drain(g, n=10 ** 9):
        k = 0
        if g is None:
            return False
        for _ in g:
            k += 1
            if k >= n:
                return True
        return False

    blocks = [(s, j) for s in range(NSEQ) for j in range(NBLK)][:nblocks]
    import itertools
    load_x(blocks[0][0], blocks[0][1], 0, 0)
    load_x(blocks[0][0], blocks[0][1], 1, 1)
    prev_ffn = None
    prev_tail = None
    sin_ops = []
    rope_tables(blocks[0][0], blocks[0][1], sin_ops)
    rope_sins(sin_ops)
    for bi, (s, j) in enumerate(blocks):
        cur = (0, 1) if bi % 2 == 0 else (2, 3)
        nxt = (2, 3) if bi % 2 == 0 else (0, 1)
        S_.phase = "P%d" % bi
        x_transposes(cur)
        pg = p_gen(s, j)
        drain(pg, 6)
        if bi + 1 < len(blocks):
            s2, j2 = blocks[bi + 1]
            load_x(s2, j2, 0, nxt[0])
            load_x(s2, j2, 1, nxt[1])
        drain(pg)
        S_.phase = "I%d" % bi
        ist["fresh"] = True
        ig = itertools.chain(indexer(j, 0), indexer(j, 1))
        alive_i, alive_t = True, prev_tail is not None
        while alive_i or alive_t:
            if alive_i:
                alive_i = drain(ig, 1)
            if alive_t:
                alive_t = drain(prev_tail, 1)
        prev_tail = None
        bg0 = bisect_gen(j, 0)
        bg1 = bisect_gen(j, 1)
        fg = prev_ffn
        alive_0, alive_1, alive_f = True, True, fg is not None
        while alive_0 or alive_1 or alive_f:
            if alive_f:
                S_.phase = "F%d" % (bi - 1)
                alive_f = drain(fg, 1)
            S_.phase = "B%d" % bi
            if alive_0:
                alive_0 = drain(bg0, 1)
            if alive_1:
                alive_1 = drain(bg1, 1)
        S_.phase = "A%d" % bi
        sin_ops = []
        if bi + 1 < len(blocks):
            rope_tables(blocks[bi + 1][0], blocks[bi + 1][1], sin_ops)
        mask_prep(j, 0)
        attention(j, 0, after_last_qk=lambda: mask_prep(j, 1))
        attention(j, 1)
        rope_sins(sin_ops)
        S_.phase = "C%d" % bi
        c_phase(s, j, cur)
        prev_tail = c_tail(s, j)
        prev_ffn = ffn_gen(s, j)
    S_.phase = "C%d" % (len(blocks) - 1)
    drain(prev_tail)
    S_.phase = "F%d" % (len(blocks) - 1)
    drain(prev_ffn)

    fin = [(c.key, c.cum, {}) for c in Chan.ALL if c.cum > 0]
    S_.wait_tokens(SP, fin)
    if order is None:
        return ws["rec"]
    import os
    if os.environ.get("DUMP_LABELS"):
        import json
        json.dump(S_.labels, open(os.environ["DUMP_LABELS"], "w"))
    S_.emit()
    return nc


_CACHE = {}


def _prep_small(w_in, w_uv, w_pool, pool_scale, conv_w, conv_b):
    wsb = np.zeros((128, 1088), np.float32)
    wsb[:, 0:512] = w_uv.transpose(1, 0, 2).reshape(128, 512)
    wsb[:, 512:1024] = w_pool.transpose(1, 0, 2).reshape(128, 512)
    wsb[:, 1024:1088] = w_in[:, 1664:1672].reshape(8, 128, 8).transpose(1, 0, 2).reshape(128, 64)
    wsf = np.zeros((128, 180), np.float32)
    wsf[:, 0:4] = pool_scale.reshape(4, 128).T
    wsf[:, 4:136] = conv_w.reshape(3, 44, 128).transpose(2, 1, 0).reshape(128, 132)
    wsf[:, 136:180] = conv_b.reshape(44, 128).T
    return wsb, wsf


def kernel(x, positions, w_in, w_uv, w_attn_branch, w_pool, pool_scale, w_pool_branch, w_out,
           ln1_g, ln1_b, conv_w, conv_b, w_ffn_up, w_ffn_down, ln2_g, ln2_b, _dbg=False, _cut=99,
           _nblocks=NSEQ * NBLK, _ncores=8):
    x = np.asarray(x, np.float32)
    positions = np.asarray(positions, np.int32)
    f = lambda a: np.asarray(a, np.float32)[0]
    wall = build_wall(f(w_in), f(w_attn_branch), f(w_pool_branch), f(w_out), f(w_ffn_up), f(w_ffn_down))
    wsb, wsf = _prep_small(f(w_in), f(w_uv), f(w_pool), f(pool_scale), f(conv_w), f(conv_b))
    cf, cb = build_consts()
    lnp = np.ascontiguousarray(np.stack([f(ln1_g), f(ln1_b), f(ln2_g), f(ln2_b)]), dtype=np.float32)
    key = (bool(_dbg), _cut, _nblocks)
    if key not in _CACHE:
        order = build_program(dbg=_dbg, cut=_cut, nblocks=_nblocks, order=None)
        _CACHE[key] = build_program(dbg=_dbg, cut=_cut, nblocks=_nblocks, order=order)
    nc = _CACHE[key]
    in_maps = []
    for c in range(8):
        in_maps.append({
            "x": np.ascontiguousarray(x[2 * c:2 * c + 2]),
            "pos": np.ascontiguousarray(positions[2 * c:2 * c + 2]),
            "wall": wall, "wsb": wsb, "wsf": wsf, "cf": cf, "lnp": lnp,
        })
    res = run_bass_kernel_spmd(nc, in_maps[:_ncores], core_ids=list(range(_ncores)))
    out = np.concatenate([np.asarray(r["out"]) for r in res.results], axis=0).astype(np.float32)
    if _dbg:
        kernel.dbg = [np.asarray(r["dbg"]) for r in res.results]
    return out
```

```python
import bisect
from contextlib import ExitStack

import numpy as np
import ml_dtypes

import concourse.bass as bass
import concourse.mybir as mybir
from concourse.bass_utils import run_bass_kernel_spmd

F32 = mybir.dt.float32
BF16 = mybir.dt.bfloat16
I32 = mybir.dt.int32
AF = mybir.ActivationFunctionType
ALU = mybir.AluOpType

PE, ACT, DVE, POOL, SP = "pe", "act", "dve", "pool", "sp"
ENGINES = (PE, ACT, DVE, POOL, SP)
SELF_SYNC = {PE: False, ACT: True, DVE: True, POOL: True, SP: False}

D = 1024
S = 2048
NSEQ = 2
NB = 256
NBLK = S // NB
H = 8
DFF = 2816
NCH = DFF // 128
ALPHA = 2.0 ** 0.25
LN_EPS = 1e-5
TOPK = 256
THETA = 500000.0
TILE = 2048
NTILES = 9 + 16 + 22 + 11
TOT = NTILES * TILE
RING = 8
NBIS = 18
BSTEP0 = 4.0
FSKEW = 2
TWO_PI = 2.0 * np.pi
C1 = 6.28125
C2 = float(TWO_PI - 6.28125)
PI_SAFE = 3.1415925


class Buf:
    __slots__ = ("name", "w", "r", "excl")

    def __init__(self, name, excl=False):
        self.name = name
        self.w = None
        self.r = []
        self.excl = excl


class Chan:
    ALL = []

    def __init__(self, sched, name):
        Chan.ALL.append(self)
        self.key = "dma:" + name
        self.sem = sched.new_sem(name)
        self.cum = 0
        sched.sems[self.key] = self.sem


class Sched:
    def __init__(self, nc):
        self.nc = nc
        self.ops = {e: [] for e in ENGINES}
        self.cnt = {e: 0 for e in ENGINES}
        self.clock = {e: {} for e in ENGINES}
        self.sems = {}
        self.phase = "pro"
        self.labels = {e: [] for e in ENGINES}
        self._semctx = []
        for e in ENGINES:
            self.sems[e] = self.new_sem("eng_" + e)

    def new_sem(self, name):
        ctx = self.nc.semaphore(name)
        s = ctx.__enter__()
        self._semctx.append(ctx)
        return s

    def close(self):
        for c in reversed(self._semctx):
            c.__exit__(None, None, None)

    def _need(self, eng, tok, waits):
        if tok is None:
            return
        key, val, snap = tok
        if key == eng and not SELF_SYNC[eng]:
            return
        clk = self.clock[eng]
        if clk.get(key, 0) >= val:
            return
        waits.append((key, val))
        clk[key] = val
        for k, v in snap.items():
            if clk.get(k, 0) < v:
                clk[k] = v

    def op(self, eng, fn, reads=(), writes=(), chan=None):
        if any(b.excl for b in reads):
            writes = list(writes) + [b for b in reads if b.excl]
            reads = [b for b in reads if not b.excl]
        waits = []
        for b in reads:
            self._need(eng, b.w, waits)
        for b in writes:
            self._need(eng, b.w, waits)
            for t in b.r:
                self._need(eng, t, waits)
        wd = {}
        for k, v in waits:
            if wd.get(k, 0) < v:
                wd[k] = v
        if chan is None:
            self.cnt[eng] += 1
            val = self.cnt[eng]
            key = eng
            inc = 1
        else:
            chan.cum += 16
            val = chan.cum
            key = chan.key
            inc = 16
        tok = (key, val, dict(self.clock[eng]))
        self.ops[eng].append((list(wd.items()), fn, key, inc))
        if chan is None:
            self.labels[eng].append(self.phase)
        for b in reads:
            b.r.append(tok)
        for b in writes:
            b.w = tok
            b.r = []
        return tok

    def wait_tokens(self, eng, toks):
        waits = []
        for t in toks:
            self._need(eng, t, waits)
        wd = {}
        for k, v in waits:
            if wd.get(k, 0) < v:
                wd[k] = v
        if wd:
            self.ops[eng].append((list(wd.items()), None, None, 0))

    def emit(self):
        nc = self.nc
        sems = self.sems
        ops = self.ops
        sig = {e: set() for e in ENGINES}
        for e in ENGINES:
            for waits, fn, key, inc in ops[e]:
                for k, v in waits:
                    if k in sig:
                        sig[k].add(v)
        sigl = {e: sorted(sig[e]) for e in ENGINES}

        def rank(k, v):
            if k in sigl:
                return bisect.bisect_right(sigl[k], v)
            return v

        def replay(e, ename, lst):
            idx = 0
            for waits, fn, key, inc in lst:
                for k, v in waits:
                    e.wait_ge(sems[k], rank(k, v))
                if fn is not None:
                    ins = fn(e)
                    if key == ename:
                        idx += 1
                        if idx in sig[ename]:
                            ins.then_inc(sems[key], 1)
                    else:
                        ins.then_inc(sems[key], inc)

        with nc.Block() as block:
            @block.tensor
            def _(e):
                replay(e, PE, ops[PE])

            @block.scalar
            def _(e):
                replay(e, ACT, ops[ACT])

            @block.vector
            def _(e):
                replay(e, DVE, ops[DVE])

            @block.gpsimd
            def _(e):
                replay(e, POOL, ops[POOL])

            @block.sync
            def _(e):
                replay(e, SP, ops[SP])


def _ktile(m):
    n = m.shape[1]
    return np.ascontiguousarray(m.reshape(8, 128, n).transpose(1, 0, 2)).reshape(128, 8 * n)


def build_wall(w_in, w_ab, w_pb, w_out, w_up, w_down):
    tiles = []
    for i in range(4):
        tiles.append(_ktile(w_in[:, 256 * i:256 * i + 256]))
    kidx = w_in[:, 1672:1736]
    tiles.append(_ktile(np.concatenate([w_in[:, 1024:1152], kidx, kidx], axis=1)))
    for i in range(2):
        tiles.append(_ktile(w_in[:, 1152 + 256 * i:1152 + 256 * i + 256]))
    for i in range(2):
        tiles.append(_ktile(w_in[:, 1736 + 256 * i:1736 + 256 * i + 256]))
    for fc in range(8):
        tiles.append(_ktile(np.concatenate([w_in[:, 2248 + 128 * fc:2248 + 128 * fc + 128],
                                            w_in[:, 3272 + 128 * fc:3272 + 128 * fc + 128]], axis=1)))
        a = w_ab[:, 128 * fc:128 * fc + 128].reshape(4, 128, 128).transpose(1, 0, 2).reshape(128, 512)
        b = w_pb[:, 128 * fc:128 * fc + 128].reshape(4, 128, 128).transpose(1, 0, 2).reshape(128, 512)
        o = w_out[128 * fc:128 * fc + 128, :]
        tiles.append(np.concatenate([a, b, o], axis=1))
    for c in range(NCH):
        tiles.append(_ktile(np.concatenate([w_up[:, 128 * c:128 * c + 128],
                                            w_up[:, DFF + 128 * c:DFF + 128 * c + 128]], axis=1)))
        if c % 2 == 0:
            tiles.append(np.concatenate([w_down[128 * c:128 * c + 128, :],
                                         w_down[128 * (c + 1):128 * (c + 1) + 128, :]], axis=1))
    wall = np.concatenate(tiles, axis=1)
    assert wall.shape == (128, TOT), wall.shape
    return np.ascontiguousarray(wall, dtype=np.float32)


def build_consts():
    cf = np.zeros((128, 128 + 2 + 64 + 256 + 64), np.float32)
    cf[:, :128] = np.eye(128, dtype=np.float32)
    fq = (np.float32(THETA) ** (-np.arange(16, dtype=np.float32) / np.float32(16))).astype(np.float32)
    fi = (np.float32(THETA) ** (-np.arange(8, dtype=np.float32) / np.float32(8))).astype(np.float32)
    cf[0:16, 128] = -fq
    cf[16:32, 128] = fq
    for base in (0, 64):
        cf[base:base + 8, 129] = -fi
        cf[base + 8:base + 16, 129] = fi
    for g, w in enumerate((2, 4, 8, 16)):
        t = np.arange(16)
        cf[:, 130 + 16 * g:130 + 16 * g + 16] = (1.0 / np.minimum(t + 1, w)).astype(np.float32)[None, :]
    pm = np.zeros((128, 256), np.float32)
    for m in range(16):
        pm[m + 16, m] = 1.0
        pm[m, m + 16] = 1.0
    for base in (0, 64):
        for m in range(8):
            pm[base + m + 8, 128 + base + m] = 1.0
            pm[base + m, 128 + base + m + 8] = 1.0
    cf[:, 194:450] = pm
    for g, w in enumerate((2, 4, 8, 16)):
        t = np.arange(16)
        cf[:, 450 + 16 * g:450 + 16 * g + 16] = (w / np.minimum(t + 1, w)).astype(np.float32)[None, :]
    return cf, pm.astype(ml_dtypes.bfloat16)


class _Stop(Exception):
    pass


def build_program(dbg=False, cut=99, nblocks=NSEQ * NBLK, order=None):
    nc = bass.Bass("TRN2", target_bir_lowering=False)
    x_d = nc.dram_tensor("x", [NSEQ, S, D], F32, kind="ExternalInput").ap()
    pos_d = nc.dram_tensor("pos", [NSEQ, S], I32, kind="ExternalInput").ap()
    wall_d = nc.dram_tensor("wall", [128, TOT], F32, kind="ExternalInput").ap()
    wsb_d = nc.dram_tensor("wsb", [128, 1088], F32, kind="ExternalInput").ap()
    wsf_d = nc.dram_tensor("wsf", [128, 180], F32, kind="ExternalInput").ap()
    cf_d = nc.dram_tensor("cf", [128, 514], F32, kind="ExternalInput").ap()
    lnp_d = nc.dram_tensor("lnp", [4, D], F32, kind="ExternalInput").ap()
    out_d = nc.dram_tensor("out", [NSEQ, S, D], F32, kind="ExternalOutput").ap()
    wbf_d = nc.dram_tensor("wbf", [128, TOT], BF16).ap()
    dbg_d = None
    if dbg:
        dbg_d = nc.dram_tensor("dbg", [8, 128, 2048], F32, kind="ExternalOutput").ap()

    Chan.ALL = []
    S_ = Sched(nc)
    es = ExitStack()

    def sb(name, shape, dt):
        return nc.alloc_sbuf_tensor(name, shape, dt).ap()

    def ps(name):
        return nc.alloc_psum_tensor(name, [128, 512], F32).ap()

    ident = sb("ident", [128, 514], F32)
    pmb = sb("pmb", [128, 256], BF16)
    wsb = sb("wsb_s", [128, 1088], BF16)
    wsf = sb("wsf_s", [128, 180], F32)
    lnp = sb("lnp_s", [128, 4, D], F32)
    ring = [sb("ring%d" % i, [128, TILE], BF16) for i in range(RING)]
    kT = sb("kT", [128, S], BF16)
    kiT = sb("kiT", [128, S], BF16)
    vaug = sb("vaug", [128, 16, H * 65], BF16)
    xt = [sb("xt%d" % i, [128, D], F32) for i in range(4)]
    xT = sb("xT", [128, 8, NB], BF16)
    qT = sb("qT", [128, 8, NB], BF16)
    qiT = sb("qiT", [128, 4, NB], BF16)
    ckvp = sb("ckvp", [128, NB], BF16)
    prebf = [sb("prebf%d" % i, [128, NB], BF16) for i in range(2)]
    widx = sb("widx", [128, 2, 8], F32)
    cosq = sb("cosq", [128, NB], F32)
    sinq = sb("sinq", [128, NB], F32)
    cosi = sb("cosi", [128, NB], F32)
    sini = sb("sini", [128, NB], F32)
    rt1 = [sb("rt1_%d" % i, [128, NB], F32) for i in range(2)]
    rt2 = [sb("rt2_%d" % i, [128, NB], F32) for i in range(2)]
    ubuf = [sb("ubuf%d" % g, [128, 16 + NB], F32) for g in range(4)]
    pa = sb("pa", [128, 16 + NB], F32)
    pb_ = sb("pb", [128, 16 + NB], F32)
    pooled = [sb("pooled%d" % i, [128, NB], BF16) for i in range(4)]
    pmix = sb("pmix", [128, 4, NB], BF16)
    Ibuf = [sb("Ibuf%d" % i, [128, S], F32) for i in range(2)]
    dgw = sb("dgw", [128, 2, H, 128], BF16)
    maskb2 = [sb("maskb%d" % i, [128, S], BF16) for i in range(2)]
    maskT = sb("maskT", [128, 16, 128], BF16)
    Ebuf2 = [sb("Ebuf%d" % i, [128, 16, 256], BF16) for i in range(2)]
    idb = sb("idb", [128, 128], BF16)
    junk = Ebuf2[1].rearrange("p k n -> p (k n)")
    Rb = [Ebuf2[0].rearrange("p k n -> p (k n)")[:, 512 * i:512 * (i + 1)] for i in range(8)]
    _scr = Ibuf[1]
    tr = [_scr[:, NB * i:NB * (i + 1)] for i in range(4)]
    posf = _scr[:, NB * 4:NB * 5]
    tki = _scr[:, NB * 5:NB * 6].bitcast(I32)
    posi = _scr[:, NB * 6:NB * 7].bitcast(I32)
    sarg = [Ibuf[0][:, NB * i:NB * (i + 1)] for i in range(4)]
    bis = sb("bis", [128, 16], F32)
    bisc = sb("bisc", [128, 16], F32)
    rden = sb("rden", [128, 8], F32)
    attok = sb("attok", [128, 512], BF16)
    attnT = sb("attnT", [128, 4, NB], BF16)
    sgA = [sb("sgA%d" % i, [128, NB], F32) for i in range(2)]
    sgB = [sb("sgB%d" % i, [128, NB], F32) for i in range(2)]
    g1 = [sb("g1_%d" % i, [128, NB], F32) for i in range(2)]
    g2 = [sb("g2_%d" % i, [128, NB], F32) for i in range(2)]
    Gfc = [sb("Gfc%d" % i, [128, NB], BF16) for i in range(2)]
    pre = [sb("pre%d" % i, [128, D], F32) for i in range(2)]
    x1 = pre
    x1T = sb("x1T", [128, 8, NB], BF16)
    lnst = sb("lnst", [128, 12], F32)
    lnmv = sb("lnmv", [128, 4], F32)
    hraw = [sb("hraw%d" % i, [128, 2, 2 + NB], F32) for i in range(2)]
    halo = sb("halo", [128, NCH, 2, 2], F32)
    aconv = [sb("aconv%d" % i, [128, 2, NB], F32) for i in range(2)]
    actc = [sb("actc%d" % i, [128, NB], BF16) for i in range(4)]

    wbank = [ps("wbank%d" % i) for i in range(4)]
    abank = [ps("abank%d" % i) for i in range(4)]

    B = {}

    def nb(name):
        b = Buf(name)
        B[name] = b
        return b

    b_const = nb("const")
    b_ring = [nb("ring%d" % i) for i in range(RING)]
    c_ring = [Chan(S_, "ring%d" % i) for i in range(RING)]
    b_wbf = [nb("wbf%d" % i) for i in range(NTILES)]
    c_cast = [Chan(S_, "cast%d" % i) for i in range(4)]
    c_const = Chan(S_, "const")
    b_kT = [nb("kT%d" % i) for i in range(16)]
    b_kiT = [nb("kiT%d" % i) for i in range(16)]
    b_v = [nb("v%d" % i) for i in range(16)]
    b_xt = [nb("xt%d" % i) for i in range(4)]
    c_xt = [Chan(S_, "xt%d" % i) for i in range(4)]
    b_pos = B["Ebuf0"] if False else nb("posi_unused")
    c_pos = Chan(S_, "pos")
    b_ot = [nb("ot%d" % i) for i in range(2)]
    c_ot = [Chan(S_, "ot%d" % i) for i in range(2)]
    b_wb = [nb("wbank%d" % i) for i in range(4)]
    b_ab = [nb("abank%d" % i) for i in range(4)]
    for b in b_wb + b_ab:
        b.excl = True
    names = ["xT", "qT", "qiT", "ckvp", "widx", "cosq", "sinq", "cosi", "sini",
             "pa", "pb", "pooled0", "pooled1", "pooled2", "pooled3", "pmix", "maskT", "idb", "rden", "attok", "attnT",
             "x1T", "lnst", "lnmv", "halo"]
    for n in names:
        nb(n)
    for i in range(4):
        nb("ubuf%d" % i)
    for i in range(2):
        for n in ["rt1_", "rt2_", "Ibuf", "sgA", "sgB", "g1_", "g2_", "Gfc", "pre", "hraw",
                  "aconv", "Ebuf", "prebf", "maskb", "bis"]:
            nb("%s%d" % (n, i))

    for n in ["tr0", "tr1", "tr2", "tr3", "posf", "tki"]:
        B[n] = B["Ibuf1"]
    B["sarg"] = B["Ibuf0"]
    B["x1_0"] = B["pre0"]
    B["x1_1"] = B["pre1"]
    for i in range(4):
        nb("actc%d" % i)
    for i in range(8):
        nb("Rb%d" % i)
    nb("dgw0")
    nb("dgw1")
    ist = {"ri": 0, "fresh": True}
    for i in range(2):
        nb("ac0_%d" % i)
        nb("ac1_%d" % i)
    wb_rr = [0]

    ALLB = (0, 1)
    allbank = wbank + abank
    b_all = b_wb + b_ab

    wb_ctr = {}

    def next_wb(extra=(), only=None):
        pool = tuple(only) if only is not None else tuple([0, 1, 2, 3] + [4 + a for a in extra])
        wb_ctr[pool] = wb_ctr.get(pool, 0) + 1
        i = pool[wb_ctr[pool] % len(pool)]
        return allbank[i], b_all[i]

    def cdma(eng, out, in_, chan=c_const):
        S_.op(eng, lambda e: e.dma_start(out=out, in_=in_), writes=[b_const], chan=chan)

    cdma(SP, ident, cf_d)
    cdma(SP, wsf, wsf_d)
    for i in range(4):
        cdma(SP, lnp[:, i, :], lnp_d[i, :].partition_broadcast(128))
    c_const2 = Chan(S_, "const2")
    b_const2 = nb("const2")
    S_.op(POOL, lambda e: e.dma_start(out=wsb, in_=wsb_d), writes=[b_const2], chan=c_const2)
    b_const.w = (c_const.key, c_const.cum, {})
    b_pm = nb("pm")
    nb("biszero")
    nb("junk0")
    nb("junk1")
    S_.op(DVE, lambda e: e.memset(bis, 0.0), writes=[B["biszero"], B["bis0"], B["bis1"]])
    for gi_ in range(16):
        S_.op(DVE, (lambda gi_: lambda e: e.memset(bisc[:, gi_:gi_ + 1], float(128 * (gi_ + 1) - 511)))(gi_),
              writes=[B["biszero"]])
    S_.op(DVE, lambda e: e.tensor_copy(out=idb, in_=ident[:, 0:128]), reads=[b_const], writes=[B["idb"]])
    S_.op(DVE, lambda e: e.tensor_copy(out=pmb, in_=ident[:, 194:450]), reads=[b_const], writes=[b_pm])
    for n in range(NTILES):
        ch = c_cast[min(3, n // 15)]
        S_.op(POOL, (lambda n: lambda e: e.dma_start(
            out=wbf_d[:, n * TILE:(n + 1) * TILE].rearrange("p (a n) -> p a n", a=2),
            in_=wall_d[:, n * TILE:(n + 1) * TILE].rearrange("p (a n) -> p a n", a=2)))(n),
              writes=[b_wbf[n]], chan=ch)
    for n in range(NTILES):
        ch = c_cast[min(3, n // 15)]
        b_wbf[n].w = (ch.key, ch.cum, {})
    S_.op(POOL, lambda e: e.memset(pa, 0.0), writes=[B["pa"]])
    S_.op(POOL, lambda e: e.memset(pb_, 0.0), writes=[B["pb"]])

    seq_tiles = order if order is not None else []
    total_uses = len(seq_tiles)
    ws = {"issued": 0, "next": 0, "rec": []}

    def w_issue_upto(k):
        while ws["issued"] < min(k, total_uses):
            u = ws["issued"]
            n = seq_tiles[u]
            slot = u % RING
            S_.op(SP, (lambda n, slot: lambda e: e.dma_start(out=ring[slot],
                                                               in_=wbf_d[:, n * TILE:(n + 1) * TILE]))(n, slot),
                  reads=[b_wbf[n]], writes=[b_ring[slot]], chan=c_ring[slot])
            ws["issued"] += 1

    def wget(tile):
        if order is None:
            ws["rec"].append(tile)
            return ring[0], b_ring[0]
        u = ws["next"]
        assert seq_tiles[u] == tile, (u, tile, seq_tiles[u])
        w_issue_upto(u + RING - 2)
        ws["next"] += 1
        slot = u % RING
        return ring[slot], b_ring[slot]

    def load_x(seq, j, tt, slot):
        r0 = j * NB + tt * 128
        S_.op(SP, lambda e: e.dma_start(out=xt[slot], in_=x_d[seq, r0:r0 + 128, :]),
              writes=[b_xt[slot]], chan=c_xt[slot])

    wuv = wsb[:, 0:512]
    wpool = wsb[:, 512:1024].rearrange("p (g d) -> p g d", g=4)
    widxW = wsb[:, 1024:1088].rearrange("p (k n) -> p k n", k=8)
    pscale = wsf[:, 0:4]
    convw = wsf[:, 4:136].rearrange("p (c k) -> p c k", k=3)
    convb = wsf[:, 136:180]
    idf = ident[:, 0:128]
    fSq = ident[:, 128:129]
    fSi = ident[:, 129:130]
    invc0 = ident[:, 130:194].rearrange("p (g t) -> p g t", g=4)
    invc0w = ident[:, 450:514].rearrange("p (g t) -> p g t", g=4)
    Pmq = pmb[:, 0:128]
    Pmi = pmb[:, 128:256]

    dbg_n = [0]

    def tap(ap_sb, buf, ncols, eng=SP):
        if not dbg:
            return
        i = dbg_n[0]
        dbg_n[0] += 1
        ch = Chan(S_, "dbg%d" % i)
        S_.op(eng, lambda e: e.dma_start(out=dbg_d[i, :, 0:ncols], in_=ap_sb), reads=[buf], chan=ch)
        taps.append(ch)

    taps = []

    def rope_tables(seq, j, sin_ops):
        t0 = j * NB
        S_.op(SP, lambda e: e.dma_start(out=posi, in_=pos_d[seq, t0:t0 + NB].partition_broadcast(128)),
              writes=[B["Ibuf1"]], chan=c_pos)
        S_.op(DVE, lambda e: e.tensor_copy(out=posf, in_=posi), reads=[B["Ibuf1"]], writes=[B["posf"]])
        for (fS, ct, st, cn, sn) in ((fSq, cosq, sinq, "cosq", "sinq"), (fSi, cosi, sini, "cosi", "sini")):
            a0, a1, a2, a3 = tr
            ba = [B["tr0"], B["tr1"], B["tr2"], B["tr3"]]
            S_.op(DVE, lambda e, fS=fS: e.tensor_scalar(out=a0, in0=posf, scalar1=fS, scalar2=None,
                                                        op0=ALU.mult),
                  reads=[B["posf"], b_const], writes=[ba[0]])
            S_.op(DVE, lambda e: e.tensor_scalar(out=a1, in0=a0, scalar1=float(1.0 / TWO_PI), scalar2=0.5,
                                                 op0=ALU.mult, op1=ALU.add), reads=[ba[0]], writes=[ba[1]])
            S_.op(DVE, lambda e: e.tensor_copy(out=tki, in_=a1), reads=[ba[1]], writes=[B["tki"]])
            S_.op(DVE, lambda e: e.tensor_copy(out=a1, in_=tki), reads=[B["tki"]], writes=[ba[1]])
            S_.op(DVE, lambda e: e.scalar_tensor_tensor(out=a2, in0=a1, scalar=-C1, in1=a0,
                                                        op0=ALU.mult, op1=ALU.add),
                  reads=[ba[1], ba[0]], writes=[ba[2]])
            S_.op(DVE, lambda e: e.scalar_tensor_tensor(out=a0, in0=a1, scalar=-C2, in1=a2,
                                                        op0=ALU.mult, op1=ALU.add),
                  reads=[ba[1], ba[2]], writes=[ba[0]])
            S_.op(DVE, lambda e: e.tensor_scalar(out=a1, in0=a0, scalar1=float(-np.pi), scalar2=float(TWO_PI),
                                                 op0=ALU.is_lt, op1=ALU.mult), reads=[ba[0]], writes=[ba[1]])
            S_.op(DVE, lambda e: e.tensor_tensor(out=a0, in0=a0, in1=a1, op=ALU.add),
                  reads=[ba[0], ba[1]], writes=[ba[0]])
            S_.op(DVE, lambda e: e.tensor_scalar(out=a2, in0=a0, scalar1=float(np.pi / 2), scalar2=None,
                                                 op0=ALU.add), reads=[ba[0]], writes=[ba[2]])
            S_.op(DVE, lambda e: e.tensor_scalar(out=a1, in0=a2, scalar1=float(np.pi), scalar2=float(-TWO_PI),
                                                 op0=ALU.is_gt, op1=ALU.mult), reads=[ba[2]], writes=[ba[1]])
            S_.op(DVE, lambda e: e.tensor_tensor(out=a2, in0=a2, in1=a1, op=ALU.add),
                  reads=[ba[2], ba[1]], writes=[ba[2]])
            S_.op(DVE, lambda e: e.tensor_scalar(out=a0, in0=a0, scalar1=-PI_SAFE, scalar2=PI_SAFE,
                                                 op0=ALU.max, op1=ALU.min), reads=[ba[0]], writes=[ba[0]])
            S_.op(DVE, lambda e: e.tensor_scalar(out=a2, in0=a2, scalar1=-PI_SAFE, scalar2=PI_SAFE,
                                                 op0=ALU.max, op1=ALU.min), reads=[ba[2]], writes=[ba[2]])
            k0 = 0 if fS is fSq else 1
            S_.op(DVE, lambda e, k0=k0: e.tensor_copy(out=sarg[2 * k0], in_=a0), reads=[ba[0]], writes=[B["sarg"]])
            S_.op(DVE, lambda e, k0=k0: e.tensor_copy(out=sarg[2 * k0 + 1], in_=a2), reads=[ba[2]], writes=[B["sarg"]])
            sin_ops.append((st, sn, 2 * k0))
            sin_ops.append((ct, cn, 2 * k0 + 1))

    def rope_sins(sin_ops):
        for (dst, name, k) in sin_ops:
            S_.op(ACT, lambda e, dst=dst, k=k: e.activation(out=dst, in_=sarg[k], func=AF.Sin),
                  reads=[B["sarg"]], writes=[B[name]])


    def x_transposes(xslots):
        for tt in range(2):
            xs = xslots[tt]
            for g4 in range(2):
                wbk, bwb = next_wb(ALLB)
                for q in range(4):
                    fc = 4 * g4 + q
                    S_.op(PE, lambda e, wbk=wbk, q=q, fc=fc, xs=xs: e.transpose(
                        out=wbk[:, q * 128:(q + 1) * 128], in_=xt[xs][:, fc * 128:(fc + 1) * 128], identity=idf),
                        reads=[b_xt[xs], b_const], writes=[bwb])
                S_.op(ACT, lambda e, wbk=wbk, g4=g4, tt=tt: e.activation(
                    out=xT[:, 4 * g4:4 * g4 + 4, tt * 128:(tt + 1) * 128],
                    in_=wbk.rearrange("p (q t) -> p q t", q=4), func=AF.Copy),
                    reads=[bwb], writes=[B["xT"]])

    def p_gen(seq, j):
        t0 = j * NB
        first = (j == 0)
        tabs = {"q": (Pmq, cosq, sinq, "cosq", "sinq"), "i": (Pmi, cosi, sini, "cosi", "sini")}
        jobs = []
        for i in range(2):
            for q in range(2):
                jobs.append((7 + i, 128 * q, "pool", 2 * i + q))
        for i in range(4):
            for q in range(2):
                jobs.append((i, 128 * q, "rope", (qT[:, 2 * i + q, :], [B["qT"]], "q", None)))
        jobs.append((4, 0, "rope", (kT[:, t0:t0 + NB], [b_kT[2 * j], b_kT[2 * j + 1]], "q", (ckvp, B["ckvp"]))))
        jobs.append((4, 128, "rope", (kiT[:, t0:t0 + NB], [b_kiT[2 * j], b_kiT[2 * j + 1]], "i", None)))
        for i in range(2):
            for q in range(2):
                jobs.append((5 + i, 128 * q, "rope", (qiT[:, 2 * i + q, :], [B["qiT"]], "i", None)))

        def part2(ctx):
            (wbk, bwb, k, dst, bdst, tk) = ctx
            Pm, ct, st, cn, sn = tabs[tk]
            i = k % 2
            pbf, bpbf = prebf[i], B["prebf%d" % i]
            wb2, bwb2 = next_wb(ALLB)
            S_.op(PE, lambda e: e.matmul(wb2[:, 0:NB], lhsT=Pm, rhs=pbf, start=True, stop=True),
                  reads=[bpbf, b_pm], writes=[bwb2])
            S_.op(DVE, lambda e: e.tensor_tensor(out=rt1[i], in0=wbk[:, 0:NB], in1=ct, op=ALU.mult),
                  reads=[bwb, B[cn]], writes=[B["rt1_%d" % i]])
            S_.op(DVE, lambda e: e.tensor_tensor(out=rt2[i], in0=wb2[:, 0:NB], in1=st, op=ALU.mult),
                  reads=[bwb2, B[sn]], writes=[B["rt2_%d" % i]])
            eng = POOL if k % 2 == 0 else DVE
            S_.op(eng, lambda e: e.tensor_tensor(out=dst, in0=rt1[i], in1=rt2[i], op=ALU.add),
                  reads=[B["rt1_%d" % i], B["rt2_%d" % i]], writes=bdst)

        def pool_adds(g):
            w = 2 ** (g + 1)
            bu = B["ubuf%d" % g]
            src, bsrc = ubuf[g], bu
            L = 16 + NB
            sh = 1
            kk = 0
            while sh < w:
                dst, bd = [(pa, B["pa"]), (pb_, B["pb"])][kk % 2]
                kk += 1
                S_.op(POOL, lambda e, dst=dst, src=src, sh=sh: e.tensor_tensor(
                    out=dst[:, sh:L], in0=src[:, sh:L], in1=src[:, 0:L - sh], op=ALU.add),
                    reads=[bsrc], writes=[bd])
                src, bsrc = dst, bd
                sh *= 2
            pl, bpl = pooled[g], B["pooled%d" % g]
            if first:
                S_.op(POOL, lambda e, src=src: e.tensor_tensor(out=src[:, 16:32], in0=src[:, 16:32],
                                                               in1=invc0w[:, g, :], op=ALU.mult),
                      reads=[bsrc, b_const], writes=[bsrc])
            S_.op(POOL, lambda e, src=src: e.tensor_scalar(out=src[:, 16:L], in0=src[:, 16:L],
                                                           scalar1=float(1.0 / w), scalar2=0.0,
                                                           op0=ALU.mult, op1=ALU.add),
                  reads=[bsrc], writes=[bsrc])
            S_.op(POOL, lambda e, src=src: e.tensor_tensor(out=pl, in0=src[:, 16:L], in1=ubuf[g][:, 16:L],
                                                           op=ALU.subtract),
                  reads=[bsrc, bu], writes=[bpl])
            S_.op(POOL, lambda e: e.tensor_copy(out=ubuf[g][:, 0:16], in_=ubuf[g][:, NB:NB + 16]),
                  reads=[bu, bsrc], writes=[bu])

        def pool_mm(g):
            pl, bpl = pooled[g], B["pooled%d" % g]
            wbk, bwb = next_wb(ALLB)
            S_.op(PE, lambda e: e.matmul(wbk[:, 0:NB], lhsT=wpool[:, g, :], rhs=pl, start=True, stop=True),
                  reads=[bpl, b_const2], writes=[bwb])
            S_.op(ACT, lambda e: e.activation(out=pmix[:, g, :], in_=wbk[:, 0:NB], func=AF.Identity,
                                              scale=pscale[:, g:g + 1]),
                  reads=[bwb, b_const], writes=[B["pmix"]])

        for tt in range(2):
            wbk, bwb = next_wb(ALLB)
            for kc in range(8):
                S_.op(PE, lambda e, kc=kc, tt=tt, wbk=wbk: e.matmul(
                    wbk[:, 0:8], lhsT=xT[:, kc, tt * 128:(tt + 1) * 128], rhs=widxW[:, kc, :],
                    start=(kc == 0), stop=(kc == 7)), reads=[B["xT"], b_const2], writes=[bwb])
            S_.op(ACT, lambda e, tt=tt, wbk=wbk: e.activation(out=widx[:, tt, :], in_=wbk[:, 0:8], func=AF.Copy,
                                                              scale=float(8.0 ** -0.5 * 64.0 ** -0.5)),
                  reads=[bwb], writes=[B["widx"]])
        for tt in range(2):
            for h in range(H):
                S_.op(DVE, lambda e, h=h, tt=tt: e.tensor_scalar(out=dgw[:, tt, h, :], in0=idf,
                                                                 scalar1=widx[:, tt, h:h + 1], scalar2=None,
                                                                 op0=ALU.mult),
                      reads=[B["widx"], b_const], writes=[B["dgw%d" % tt]])
        cur_tile, wt, bwt = None, None, None
        pend = None
        k = 0
        pool_sched = {}
        for ji, (tile, coff, kind, arg) in enumerate(jobs):
            if tile != cur_tile:
                wt, bwt = wget(tile)
                cur_tile = tile
            wbk, bwb = next_wb(ALLB)
            wt3 = wt.rearrange("p (k n) -> p k n", k=8)
            for kc in range(8):
                S_.op(PE, lambda e, kc=kc, wbk=wbk, wt3=wt3, coff=coff: e.matmul(
                    wbk[:, 0:NB], lhsT=wt3[:, kc, coff:coff + 128], rhs=xT[:, kc, :],
                    start=(kc == 0), stop=(kc == 7)), reads=[bwt, B["xT"]], writes=[bwb])
            if kind == "rope":
                dst, bdst, tk, keep = arg
                i = k % 2
                pbf, bpbf = prebf[i], B["prebf%d" % i]
                S_.op(ACT, lambda e, pbf=pbf, wbk=wbk: e.activation(out=pbf, in_=wbk[:, 0:NB], func=AF.Copy),
                      reads=[bwb], writes=[bpbf])
                if keep is not None:
                    S_.op(POOL, lambda e, pbf=pbf, keep=keep: e.tensor_copy(out=keep[0], in_=pbf), reads=[bpbf],
                          writes=[keep[1]])
                if pend is not None:
                    part2(pend)
                pend = (wbk, bwb, k, dst, bdst, tk)
                k += 1
            else:
                g = arg
                if first:
                    S_.op(POOL, lambda e, g=g: e.memset(ubuf[g][:, 0:16], 0.0), writes=[B["ubuf%d" % g]])
                S_.op(ACT, lambda e, wbk=wbk, g=g: e.activation(out=ubuf[g][:, 16:16 + NB], in_=wbk[:, 0:NB],
                                                                func=AF.Copy),
                      reads=[bwb], writes=[B["ubuf%d" % g]])
                pool_sched.setdefault(ji + 1, []).append((pool_adds, g))
                pool_sched.setdefault(14 + g, []).append((pool_mm, g))
            for (fn_, g_) in pool_sched.pop(ji, []):
                fn_(g_)
            yield
        if pend is not None:
            part2(pend)
        for ji in sorted(pool_sched):
            for (fn_, g_) in pool_sched[ji]:
                fn_(g_)
        for tt in range(2):
            gi = 2 * j + tt
            wbk, bwb = next_wb(ALLB)
            S_.op(PE, lambda e, tt=tt, wbk=wbk: e.matmul(wbk[:, 0:512], lhsT=ckvp[:, tt * 128:(tt + 1) * 128],
                                                         rhs=wuv, start=True, stop=True),
                  reads=[B["ckvp"], b_const2], writes=[bwb])
            v3 = vaug[:, gi, :].rearrange("p (h e) -> p h e", e=65)
            S_.op(ACT, lambda e, wbk=wbk, v3=v3: e.activation(out=v3[:, :, 0:64],
                                                              in_=wbk.rearrange("p (h e) -> p h e", e=64),
                                                              func=AF.Copy),
                  reads=[bwb], writes=[b_v[gi]])
            S_.op(POOL, lambda e, v3=v3: e.memset(v3[:, :, 64:65], 1.0), writes=[b_v[gi]])
        yield

    def indexer(j, tt):
        gi = 2 * j + tt
        nkb = gi + 1
        Nk = 128 * nkb
        Ib, bI = Ibuf[tt], B["Ibuf%d" % tt]
        nks = (Nk + 511) // 512
        kread = [b_kiT[i] for i in range(nkb)]
        bdg = B["dgw%d" % tt]
        ISK = 3
        for ks in range(nks):
            cols = min(512, Nk - 512 * ks)
            bbk, bbb = next_wb(only=(4, 5))
            pend = []

            def acc(ctx, bbk=bbk, bbb=bbb, cols=cols):
                (h, R, bR) = ctx
                S_.op(PE, lambda e: e.matmul(bbk[:, 0:cols], lhsT=dgw[:, tt, h, :], rhs=R[:, 0:cols],
                                             start=(h == 0), stop=(h == H - 1)),
                      reads=[bdg, bR, B["Ebuf0"]], writes=[bbb])

            for h in range(H):
                m, base = h // 2, 64 * (h % 2)
                wbk, bwb = next_wb()
                S_.op(PE, lambda e, wbk=wbk, m=m, base=base, ks=ks, cols=cols: e.matmul(
                    wbk[:, 0:cols], lhsT=qiT[base:base + 64, m, tt * 128:(tt + 1) * 128],
                    rhs=kiT[base:base + 64, 512 * ks:512 * ks + cols], start=True, stop=True),
                    reads=[B["qiT"]] + kread, writes=[bwb])
                ri = ist["ri"]
                ist["ri"] += 1
                R, bR = Rb[ri % 8], B["Rb%d" % (ri % 8)]
                wr = [bR] + ([B["Ebuf0"]] if ist["fresh"] else [])
                rd = [bwb] + ([] if ist["fresh"] else [B["Ebuf0"]])
                ist["fresh"] = False
                if h % 2 == 0:
                    S_.op(ACT, lambda e, wbk=wbk, R=R, cols=cols: e.activation(
                        out=R[:, 0:cols], in_=wbk[:, 0:cols], func=AF.Relu), reads=rd, writes=wr)
                else:
                    S_.op(DVE, lambda e, wbk=wbk, R=R, cols=cols: e.tensor_scalar(
                        out=R[:, 0:cols], in0=wbk[:, 0:cols], scalar1=0.0, scalar2=None, op0=ALU.max),
                        reads=rd, writes=wr)
                pend.append((h, R, bR))
                if len(pend) > ISK:
                    acc(pend.pop(0))
            while pend:
                acc(pend.pop(0))
            S_.op(ACT, lambda e, ks=ks, cols=cols, bbk=bbk: e.activation(
                out=Ib[:, 512 * ks:512 * ks + cols], in_=bbk[:, 0:cols], func=AF.Copy),
                reads=[bbb], writes=[bI])
            yield
        S_.op(DVE, lambda e, Ib=Ib, Nk=Nk: e.memset(Ib[0:64, Nk - 64:Nk], -1e30), reads=[bI], writes=[bI])
        yield

    def bisect_gen(j, tt):
        gi = 2 * j + tt
        Nk = 128 * (gi + 1)
        Ib, bI = Ibuf[tt], B["Ibuf%d" % tt]
        bb = B["bis%d" % tt]
        mk, bmk = maskb2[tt], B["maskb%d" % tt]
        bj = B["junk%d" % tt]
        jk = junk[:, 2048 * tt:2048 * tt + 2048]
        if tt == 0:
            cnt, dd, mid = bis[:, 0:1], bis[:, 1:2], bis[:, 2:3]
            if gi < 2:
                S_.op(DVE, lambda e: e.memset(mid, -1e29), writes=[bb])
            else:
                S_.op(DVE, lambda e: e.memset(mid, 0.0), writes=[bb, bj, B["Ebuf1"]])
                step = BSTEP0
                for it in range(NBIS):
                    S_.op(DVE, lambda e: e.tensor_scalar(
                        out=jk[:, 0:Nk], in0=Ib[:, 0:Nk], scalar1=mid, scalar2=None, op0=ALU.is_ge,
                        op1=ALU.add, accum_out=cnt), reads=[bI, bb], writes=[bj, bb])
                    S_.op(DVE, lambda e, step=step: e.tensor_scalar(
                        out=dd, in0=cnt, scalar1=float(TOPK) - 0.5, scalar2=2.0 * step, op0=ALU.is_ge,
                        op1=ALU.mult), reads=[bb], writes=[bb])
                    S_.op(DVE, lambda e, step=step: e.scalar_tensor_tensor(
                        out=mid, in0=dd, scalar=-step, in1=mid, op0=ALU.add, op1=ALU.add),
                        reads=[bb], writes=[bb])
                    step *= 0.5
                    yield
            S_.op(DVE, lambda e: e.tensor_scalar(out=mk[:, 0:Nk], in0=Ib[:, 0:Nk], scalar1=mid,
                                                 scalar2=None, op0=ALU.is_ge),
                  reads=[bI, bb, bj], writes=[bmk] + ([B["Ebuf1"]] if gi >= 2 else []))
        else:
            sacc, dd, mid = bis[:, 4:5], bis[:, 5:6], bis[:, 6:7]
            ng = [bis[:, 8:9], bis[:, 9:10]]
            if gi < 2:
                S_.op(DVE, lambda e: e.memset(mid, -1e29), writes=[bb])
            else:
                S_.op(ACT, lambda e: e.activation(out=ng[0], in_=bis[:, 7:8], func=AF.Copy, scale=0.0),
                      reads=[B["biszero"]], writes=[bb, bj, B["Ebuf1"]])
                step = BSTEP0
                for it in range(NBIS):
                    n0, n1 = ng[it % 2], ng[(it + 1) % 2]
                    S_.op(ACT, lambda e, n0=n0: e.activation(
                        out=jk[:, 0:Nk], in_=Ib[:, 0:Nk], func=AF.Sign, bias=n0, scale=1.0, accum_out=sacc),
                        reads=[bI, bb], writes=[bj, bb])
                    S_.op(ACT, lambda e: e.activation(out=dd, in_=sacc, func=AF.Sign, bias=bisc[:, gi:gi + 1],
                                                      scale=1.0),
                          reads=[bb, B["biszero"]], writes=[bb])
                    S_.op(ACT, lambda e, n0=n0, n1=n1, step=step: e.activation(
                        out=n1, in_=dd, func=AF.Identity, scale=-step, bias=n0), reads=[bb], writes=[bb])
                    step *= 0.5
                    yield
                nf = ng[NBIS % 2]
                S_.op(ACT, lambda e, nf=nf: e.activation(out=mid, in_=nf, func=AF.Copy, scale=-1.0),
                      reads=[bb], writes=[bb])
            S_.op(DVE, lambda e: e.tensor_scalar(out=mk[:, 0:Nk], in0=Ib[:, 0:Nk], scalar1=mid,
                                                 scalar2=None, op0=ALU.is_ge),
                  reads=[bI, bb, bj], writes=[bmk] + ([B["Ebuf1"]] if gi >= 2 else []))
        yield

    AB = (2, 3)

    def mask_prep(j, tt):
        gi = 2 * j + tt
        nkb = gi + 1
        mk, bmk = maskb2[tt], B["maskb%d" % tt]
        for k8 in range(0, nkb, 8):
            nq = min(8, nkb - k8)
            wbk, bwb = next_wb(AB)
            wbb = wbk.bitcast(BF16)
            for q in range(nq):
                kb = k8 + q
                S_.op(PE, lambda e, wbb=wbb, q=q, kb=kb: e.transpose(
                    out=wbb[:, q * 128:(q + 1) * 128], in_=mk[:, kb * 128:(kb + 1) * 128], identity=idb),
                    reads=[bmk, B["idb"]], writes=[bwb])
            S_.op(ACT, lambda e, wbb=wbb, k8=k8, nq=nq: e.activation(
                out=maskT[:, k8:k8 + nq, :], in_=wbb[:, 0:128 * nq].rearrange("p (q t) -> p q t", q=nq),
                func=AF.Identity, scale=30000.0, bias=-30000.0), reads=[bwb], writes=[B["maskT"]])

    def attention(j, tt, after_last_qk=None):
        gi = 2 * j + tt
        nkb = gi + 1
        kTr = [b_kT[i] for i in range(nkb)]
        vr = [b_v[i] for i in range(nkb)]

        def pv(hp):
            Eb, bE = Ebuf2[hp % 2], B["Ebuf%d" % (hp % 2)]
            for hh in range(2):
                h = 2 * hp + hh
                ob, bob = abank[h // 4], b_ab[h // 4]
                for kb in range(nkb):
                    S_.op(PE, lambda e, ob=ob, h=h, hh=hh, kb=kb, Eb=Eb: e.matmul(
                        ob[:, (h % 4) * 65:(h % 4) * 65 + 65], lhsT=Eb[:, kb, hh * 128:(hh + 1) * 128],
                        rhs=vaug[:, kb, h * 65:(h + 1) * 65], start=(kb == 0), stop=(kb == nkb - 1)),
                        reads=[bE] + vr, writes=[bob])

        pend = None
        for hp in range(4):
            Eb, bE = Ebuf2[hp % 2], B["Ebuf%d" % (hp % 2)]
            for k2 in range(0, nkb, 2):
                nq = min(2, nkb - k2)
                wbk, bwb = next_wb(AB)
                for q in range(nq):
                    kb = k2 + q
                    o3 = wbk[:, q * 256:(q + 1) * 256].rearrange("p (a t) -> p a t", a=2)
                    S_.op(PE, lambda e, o3=o3, kb=kb, hp=hp: e.matmul(
                        o3, lhsT=kT[:, kb * 128:(kb + 1) * 128],
                        rhs=qT[:, 2 * hp:2 * hp + 2, tt * 128:(tt + 1) * 128], start=True, stop=False),
                        reads=[B["qT"]] + kTr, writes=[bwb])
                    S_.op(PE, lambda e, o3=o3, kb=kb: e.matmul(
                        o3, lhsT=idb, rhs=maskT[:, kb:kb + 1, :].to_broadcast([128, 2, 128]),
                        start=False, stop=True), reads=[B["maskT"], B["idb"]], writes=[bwb])
                S_.op(ACT, lambda e, wbk=wbk, k2=k2, nq=nq, Eb=Eb: e.activation(
                    out=Eb[:, k2:k2 + nq, :], in_=wbk[:, 0:256 * nq].rearrange("p (q n) -> p q n", q=nq),
                    func=AF.Exp, scale=float(128.0 ** -0.5)), reads=[bwb], writes=[bE])
            if hp == 3 and after_last_qk is not None:
                after_last_qk()
            if pend is not None:
                pv(pend)
            pend = hp
        pv(pend)
        for half in range(2):
            ob, bob = abank[half], b_ab[half]
            o3 = ob[:, 0:260].rearrange("p (h e) -> p h e", e=65)
            S_.op(DVE, lambda e, o3=o3, half=half: e.reciprocal(
                out=rden[:, 4 * half:4 * half + 4].unsqueeze(2), in_=o3[:, :, 64:65]),
                reads=[bob], writes=[B["rden"]])
            S_.op(DVE, lambda e, o3=o3, half=half: e.tensor_tensor(
                out=attok[:, 256 * half:256 * half + 256].rearrange("p (h e) -> p h e", e=64),
                in0=o3[:, :, 0:64],
                in1=rden[:, 4 * half:4 * half + 4].unsqueeze(2).to_broadcast([128, 4, 64]), op=ALU.mult),
                reads=[bob, B["rden"]], writes=[B["attok"]])
        wbk, bwb = next_wb(AB)
        wbb = wbk.bitcast(BF16)
        for q in range(4):
            S_.op(PE, lambda e, wbb=wbb, q=q: e.transpose(out=wbb[:, q * 128:(q + 1) * 128],
                                                          in_=attok[:, q * 128:(q + 1) * 128], identity=idb),
                  reads=[B["attok"], B["idb"]], writes=[bwb])
        S_.op(ACT, lambda e, wbb=wbb: e.activation(
            out=attnT[:, :, tt * 128:(tt + 1) * 128], in_=wbb[:, 0:512].rearrange("p (q t) -> p q t", q=4),
            func=AF.Copy), reads=[bwb], writes=[B["attnT"]])

    def layernorm(src, bsrc, dst, bdst, gi_, bi_, aff=POOL):
        for half in range(2):
            S_.op(DVE, lambda e, half=half: e.bn_stats(out=lnst[:, 6 * half:6 * half + 6],
                                                       in_=src[:, 512 * half:512 * half + 512]),
                  reads=[bsrc], writes=[B["lnst"]])
        S_.op(DVE, lambda e: e.bn_aggr(out=lnmv[:, 0:2], in_=lnst), reads=[B["lnst"]], writes=[B["lnmv"]])
        S_.op(DVE, lambda e: e.tensor_scalar(out=lnmv[:, 2:3], in0=lnmv[:, 1:2], scalar1=LN_EPS, scalar2=None,
                                             op0=ALU.add), reads=[B["lnmv"]], writes=[B["lnmv"]])
        S_.op(ACT, lambda e: e.activation(out=lnmv[:, 2:3], in_=lnmv[:, 2:3], func=AF.Sqrt),
              reads=[B["lnmv"]], writes=[B["lnmv"]])
        S_.op(DVE, lambda e: e.reciprocal(out=lnmv[:, 2:3], in_=lnmv[:, 2:3]), reads=[B["lnmv"]],
              writes=[B["lnmv"]])
        S_.op(DVE, lambda e: e.scalar_tensor_tensor(out=lnmv[:, 3:4], in0=lnmv[:, 0:1], scalar=-1.0,
                                                    in1=lnmv[:, 2:3], op0=ALU.mult, op1=ALU.mult),
              reads=[B["lnmv"]], writes=[B["lnmv"]])
        S_.op(ACT, lambda e: e.activation(out=dst, in_=src, func=AF.Identity, scale=lnmv[:, 2:3],
                                          bias=lnmv[:, 3:4]), reads=[bsrc, B["lnmv"]], writes=[bdst])
        S_.op(aff, lambda e: e.tensor_tensor(out=dst, in0=dst, in1=lnp[:, gi_, :], op=ALU.mult),
              reads=[bdst, b_const], writes=[bdst])
        S_.op(aff, lambda e: e.tensor_tensor(out=dst, in0=dst, in1=lnp[:, bi_, :], op=ALU.add),
              reads=[bdst, b_const], writes=[bdst])

    def c_phase(seq, j, xslots):
        def outproj(ctx):
            fc, i2, wo, bwm = ctx
            for tt in range(2):
                for half in range(2):
                    ai = 2 * tt + half
                    S_.op(PE, lambda e, ai=ai, tt=tt, half=half: e.matmul(
                        abank[ai][:, 0:512], lhsT=Gfc[i2][:, tt * 128:(tt + 1) * 128],
                        rhs=wo[:, half * 512:(half + 1) * 512], start=(fc == 0), stop=(fc == 7)),
                        reads=[B["Gfc%d" % i2], bwm], writes=[b_ab[ai]])

        pend = None
        for fc in range(8):
            wg, bwg = wget(9 + 2 * fc)
            wm, bwm = wget(10 + 2 * fc)
            wg3 = wg.rearrange("p (k n) -> p k n", k=8)
            wab3 = wm[:, 0:512].rearrange("p (k n) -> p k n", k=4)
            wpb3 = wm[:, 512:1024].rearrange("p (k n) -> p k n", k=4)
            wo = wm[:, 1024:2048]
            i2 = fc % 2
            bkA, bbA = next_wb()
            bkB, bbB = next_wb()
            for a in range(2):
                for kc in range(8):
                    S_.op(PE, lambda e, a=a, kc=kc, bkA=bkA, wg3=wg3: e.matmul(
                        bkA[:, a * NB:(a + 1) * NB], lhsT=wg3[:, kc, a * 128:(a + 1) * 128], rhs=xT[:, kc, :],
                        start=(kc == 0), stop=(kc == 7)), reads=[bwg, B["xT"]], writes=[bbA])
            for kc in range(4):
                S_.op(PE, lambda e, kc=kc, bkB=bkB, wab3=wab3: e.matmul(
                    bkB[:, 0:NB], lhsT=wab3[:, kc, :], rhs=attnT[:, kc, :], start=(kc == 0), stop=(kc == 3)),
                    reads=[bwm, B["attnT"]], writes=[bbB])
            for kc in range(4):
                S_.op(PE, lambda e, kc=kc, bkB=bkB, wpb3=wpb3: e.matmul(
                    bkB[:, NB:2 * NB], lhsT=wpb3[:, kc, :], rhs=pmix[:, kc, :], start=(kc == 0), stop=(kc == 3)),
                    reads=[bwm, B["pmix"]], writes=[bbB])
            S_.op(ACT, lambda e, bkA=bkA, i2=i2: e.activation(out=sgA[i2], in_=bkA[:, 0:NB], func=AF.Sigmoid),
                  reads=[bbA], writes=[B["sgA%d" % i2]])
            S_.op(ACT, lambda e, bkA=bkA, i2=i2: e.activation(out=sgB[i2], in_=bkA[:, NB:2 * NB], func=AF.Sigmoid),
                  reads=[bbA], writes=[B["sgB%d" % i2]])
            S_.op(DVE, lambda e, bkB=bkB, i2=i2: e.tensor_tensor(out=g1[i2], in0=bkB[:, 0:NB], in1=sgA[i2],
                                                                 op=ALU.mult),
                  reads=[bbB, B["sgA%d" % i2]], writes=[B["g1_%d" % i2]])
            S_.op(DVE, lambda e, bkB=bkB, i2=i2: e.tensor_tensor(out=g2[i2], in0=bkB[:, NB:2 * NB], in1=sgB[i2],
                                                                 op=ALU.mult),
                  reads=[bbB, B["sgB%d" % i2]], writes=[B["g2_%d" % i2]])
            S_.op(POOL, lambda e, i2=i2: e.tensor_tensor(out=Gfc[i2], in0=g1[i2], in1=g2[i2], op=ALU.add),
                  reads=[B["g1_%d" % i2], B["g2_%d" % i2]], writes=[B["Gfc%d" % i2]])
            if pend is not None:
                outproj(pend)
            pend = (fc, i2, wo, bwm)
        outproj(pend)
        for tt in range(2):
            xs = xslots[tt]
            for half in range(2):
                ai = 2 * tt + half
                S_.op(DVE, lambda e, ai=ai, tt=tt, half=half, xs=xs: e.scalar_tensor_tensor(
                    out=pre[tt][:, 512 * half:512 * half + 512], in0=xt[xs][:, 512 * half:512 * half + 512],
                    scalar=float(ALPHA), in1=abank[ai][:, 0:512], op0=ALU.mult, op1=ALU.add),
                    reads=[b_xt[xs], b_ab[ai]], writes=[B["pre%d" % tt]])

    def c_tail(seq, j):
        for tt in range(2):
            layernorm(pre[tt], B["pre%d" % tt], pre[tt], B["pre%d" % tt], 0, 1, aff=POOL)
            yield
            for g4 in range(2):
                wbk, bwb = next_wb(only=(6, 7))
                for q in range(4):
                    fc = 4 * g4 + q
                    S_.op(PE, lambda e, wbk=wbk, q=q, fc=fc, tt=tt: e.transpose(
                        out=wbk[:, q * 128:(q + 1) * 128], in_=pre[tt][:, fc * 128:(fc + 1) * 128], identity=idf),
                        reads=[B["pre%d" % tt], b_const], writes=[bwb])
                S_.op(ACT, lambda e, wbk=wbk, g4=g4, tt=tt: e.activation(
                    out=x1T[:, 4 * g4:4 * g4 + 4, tt * 128:(tt + 1) * 128],
                    in_=wbk.rearrange("p (q t) -> p q t", q=4), func=AF.Copy),
                    reads=[bwb], writes=[B["x1T"]])
            yield

    def ffn_gen(seq, j):
        t0 = j * NB
        first = (j == 0)
        st = {"wd": None, "bwd": None}

        def down(c):
            if c % 2 == 0:
                st["wd"], st["bwd"] = wget(25 + c + (c + 1) // 2 + 1)
            wd, bwd = st["wd"], st["bwd"]
            i4 = c % 4
            wdc = wd[:, (c % 2) * 1024:(c % 2) * 1024 + 1024]
            for tt in range(2):
                for half in range(2):
                    ai = 2 * tt + half
                    S_.op(PE, lambda e, ai=ai, tt=tt, half=half: e.matmul(
                        abank[ai][:, 0:512], lhsT=actc[i4][:, tt * 128:(tt + 1) * 128],
                        rhs=wdc[:, half * 512:(half + 1) * 512], start=(c == 0), stop=(c == NCH - 1)),
                        reads=[B["actc%d" % i4], bwd], writes=[b_ab[ai]])

        hbs = {}

        def s0(c):
            wu, bwu = wget(25 + c + (c + 1) // 2)
            wu3 = wu.rearrange("p (k n) -> p k n", k=8)
            hb, bhb = next_wb()
            hbs[c] = (hb, bhb)
            for a in range(2):
                for kc in range(8):
                    S_.op(PE, lambda e, a=a, kc=kc: e.matmul(
                        hb[:, a * NB:(a + 1) * NB], lhsT=wu3[:, kc, a * 128:(a + 1) * 128], rhs=x1T[:, kc, :],
                        start=(kc == 0), stop=(kc == 7)), reads=[bwu, B["x1T"]], writes=[bhb])

        def s1(c):
            hb, bhb = hbs.pop(c)
            i2 = c % 2
            hr, bhr = hraw[i2], B["hraw%d" % i2]
            if first:
                S_.op(POOL, lambda e: e.memset(halo[:, c, :, :], 0.0), writes=[B["halo"]])
            S_.op(ACT, lambda e: e.activation(out=hr[:, :, 0:2], in_=halo[:, c, :, :], func=AF.Copy),
                  reads=[B["halo"]], writes=[bhr])
            S_.op(ACT, lambda e: e.activation(out=hr[:, :, 2:2 + NB], in_=hb.rearrange("p (a n) -> p a n", a=2),
                                              func=AF.Copy), reads=[bhb], writes=[bhr])
            S_.op(ACT, lambda e: e.activation(out=halo[:, c, :, :], in_=hr[:, :, NB:NB + 2], func=AF.Copy),
                  reads=[bhr], writes=[B["halo"]])

        def s23(c):
            i2 = c % 2
            hr, bhr = hraw[i2], B["hraw%d" % i2]
            ac = aconv[i2]
            for a in range(2):
                cc = c + NCH * a
                bac = B["ac%d_%d" % (a, i2)]
                S_.op(POOL, lambda e, a=a, cc=cc: e.tensor_scalar(
                    out=ac[:, a, :], in0=hr[:, a, 2:2 + NB], scalar1=convw[:, cc, 2:3], scalar2=convb[:, cc:cc + 1],
                    op0=ALU.mult, op1=ALU.add), reads=[bhr, b_const], writes=[bac])
            for a in range(2):
                cc = c + NCH * a
                bac = B["ac%d_%d" % (a, i2)]
                S_.op(DVE, lambda e, a=a, cc=cc: e.scalar_tensor_tensor(
                    out=ac[:, a, :], in0=hr[:, a, 1:1 + NB], scalar=convw[:, cc, 1:2], in1=ac[:, a, :],
                    op0=ALU.mult, op1=ALU.add), reads=[bhr, b_const, bac], writes=[bac])
                S_.op(DVE, lambda e, a=a, cc=cc: e.scalar_tensor_tensor(
                    out=ac[:, a, :], in0=hr[:, a, 0:NB], scalar=convw[:, cc, 0:1], in1=ac[:, a, :],
                    op0=ALU.mult, op1=ALU.add), reads=[bhr, b_const, bac], writes=[bac])

        def s45(c):
            i2 = c % 2
            ac = aconv[i2]
            bg, bv = B["ac0_%d" % i2], B["ac1_%d" % i2]
            S_.op(ACT, lambda e: e.activation(out=ac[:, 0, :], in_=ac[:, 0, :], func=AF.Silu),
                  reads=[bg], writes=[bg])
            S_.op(POOL, lambda e: e.tensor_tensor(out=actc[c % 4], in0=ac[:, 0, :], in1=ac[:, 1, :], op=ALU.mult),
                  reads=[bg, bv], writes=[B["actc%d" % (c % 4)]])

        for i in range(NCH + 1 + FSKEW):
            if i < NCH:
                s0(i)
                s1(i)
                s23(i)
            if 0 <= i - 1 < NCH:
                s45(i - 1)
            if 0 <= i - 1 - FSKEW < NCH:
                down(i - 1 - FSKEW)
            yield
        for tt in range(2):
            for half in range(2):
                ai = 2 * tt + half
                S_.op(DVE, lambda e, ai=ai, tt=tt, half=half: e.scalar_tensor_tensor(
                    out=pre[tt][:, 512 * half:512 * half + 512], in0=pre[tt][:, 512 * half:512 * half + 512],
                    scalar=float(ALPHA), in1=abank[ai][:, 0:512], op0=ALU.mult, op1=ALU.add),
                    reads=[b_ab[ai]], writes=[B["pre%d" % tt]])
            layernorm(pre[tt], B["pre%d" % tt], pre[tt], B["pre%d" % tt], 2, 3)
            r0 = t0 + tt * 128
            S_.op(SP, lambda e, tt=tt, r0=r0: e.dma_start(out=out_d[seq, r0:r0 + 128, :], in_=pre[tt]),
                  reads=[B["pre%d" % tt]], chan=c_ot[tt])
        yield

    def drain(g, n=10 ** 9):
        k = 0
        if g is None:
            return False
        for _ in g:
            k += 1
            if k >= n:
                return True
        return False

    blocks = [(s, j) for s in range(NSEQ) for j in range(NBLK)][:nblocks]
    import itertools
    load_x(blocks[0][0], blocks[0][1], 0, 0)
    load_x(blocks[0][0], blocks[0][1], 1, 1)
    prev_ffn = None
    prev_tail = None
    sin_ops = []
    rope_tables(blocks[0][0], blocks[0][1], sin_ops)
    rope_sins(sin_ops)
    for bi, (s, j) in enumerate(blocks):
        cur = (0, 1) if bi % 2 == 0 else (2, 3)
        nxt = (2, 3) if bi % 2 == 0 else (0, 1)
        S_.phase = "P%d" % bi
        x_transposes(cur)
        pg = p_gen(s, j)
        drain(pg, 6)
        if bi + 1 < len(blocks):
            s2, j2 = blocks[bi + 1]
            load_x(s2, j2, 0, nxt[0])
            load_x(s2, j2, 1, nxt[1])
        drain(pg)
        S_.phase = "I%d" % bi
        ist["fresh"] = True
        ig = itertools.chain(indexer(j, 0), indexer(j, 1))
        alive_i, alive_t = True, prev_tail is not None
        while alive_i or alive_t:
            if alive_i:
                alive_i = drain(ig, 1)
            if alive_t:
                alive_t = drain(prev_tail, 1)
        prev_tail = None
        bg0 = bisect_gen(j, 0)
        bg1 = bisect_gen(j, 1)
        fg = prev_ffn
        alive_0, alive_1, alive_f = True, True, fg is not None
        while alive_0 or alive_1 or alive_f:
            if alive_f:
                S_.phase = "F%d" % (bi - 1)
                alive_f = drain(fg, 1)
            S_.phase = "B%d" % bi
            if alive_0:
                alive_0 = drain(bg0, 1)
            if alive_1:
                alive_1 = drain(bg1, 1)
        S_.phase = "A%d" % bi
        sin_ops = []
        if bi + 1 < len(blocks):
            rope_tables(blocks[bi + 1][0], blocks[bi + 1][1], sin_ops)
        mask_prep(j, 0)
        attention(j, 0, after_last_qk=lambda: mask_prep(j, 1))
        attention(j, 1)
        rope_sins(sin_ops)
        S_.phase = "C%d" % bi
        c_phase(s, j, cur)
        prev_tail = c_tail(s, j)
        prev_ffn = ffn_gen(s, j)
    S_.phase = "C%d" % (len(blocks) - 1)
    drain(prev_tail)
    S_.phase = "F%d" % (len(blocks) - 1)
    drain(prev_ffn)

    fin = [(c.key, c.cum, {}) for c in Chan.ALL if c.cum > 0]
    S_.wait_tokens(SP, fin)
    if order is None:
        return ws["rec"]
    import os
    if os.environ.get("DUMP_LABELS"):
        import json
        json.dump(S_.labels, open(os.environ["DUMP_LABELS"], "w"))
    S_.emit()
    return nc


_CACHE = {}


def _prep_small(w_in, w_uv, w_pool, pool_scale, conv_w, conv_b):
    wsb = np.zeros((128, 1088), np.float32)
    wsb[:, 0:512] = w_uv.transpose(1, 0, 2).reshape(128, 512)
    wsb[:, 512:1024] = w_pool.transpose(1, 0, 2).reshape(128, 512)
    wsb[:, 1024:1088] = w_in[:, 1664:1672].reshape(8, 128, 8).transpose(1, 0, 2).reshape(128, 64)
    wsf = np.zeros((128, 180), np.float32)
    wsf[:, 0:4] = pool_scale.reshape(4, 128).T
    wsf[:, 4:136] = conv_w.reshape(3, 44, 128).transpose(2, 1, 0).reshape(128, 132)
    wsf[:, 136:180] = conv_b.reshape(44, 128).T
    return wsb, wsf


def kernel(x, positions, w_in, w_uv, w_attn_branch, w_pool, pool_scale, w_pool_branch, w_out,
           ln1_g, ln1_b, conv_w, conv_b, w_ffn_up, w_ffn_down, ln2_g, ln2_b, _dbg=False, _cut=99,
           _nblocks=NSEQ * NBLK, _ncores=8):
    x = np.asarray(x, np.float32)
    positions = np.asarray(positions, np.int32)
    f = lambda a: np.asarray(a, np.float32)[0]
    wall = build_wall(f(w_in), f(w_attn_branch), f(w_pool_branch), f(w_out), f(w_ffn_up), f(w_ffn_down))
    wsb, wsf = _prep_small(f(w_in), f(w_uv), f(w_pool), f(pool_scale), f(conv_w), f(conv_b))
    cf, cb = build_consts()
    lnp = np.ascontiguousarray(np.stack([f(ln1_g), f(ln1_b), f(ln2_g), f(ln2_b)]), dtype=np.float32)
    key = (bool(_dbg), _cut, _nblocks)
    if key not in _CACHE:
        order = build_program(dbg=_dbg, cut=_cut, nblocks=_nblocks, order=None)
        _CACHE[key] = build_program(dbg=_dbg, cut=_cut, nblocks=_nblocks, order=order)
    nc = _CACHE[key]
    in_maps = []
    for c in range(8):
        in_maps.append({
            "x": np.ascontiguousarray(x[2 * c:2 * c + 2]),
            "pos": np.ascontiguousarray(positions[2 * c:2 * c + 2]),
            "wall": wall, "wsb": wsb, "wsf": wsf, "cf": cf, "lnp": lnp,
        })
    res = run_bass_kernel_spmd(nc, in_maps[:_ncores], core_ids=list(range(_ncores)))
    out = np.concatenate([np.asarray(r["out"]) for r in res.results], axis=0).astype(np.float32)
    if _dbg:
        kernel.dbg = [np.asarray(r["dbg"]) for r in res.results]
    return out
```
